# Optimizing a Trainium2 kernel written in Bass

```python
import math
import jax, jax.numpy as jnp
from jax import lax
import numpy as np

D_MODEL = 1024
BATCH = 32
SEQ = 2048
DEPTH = 4

GRID_W = 64
CTX_LEN = 256
HEAD_DIM = 64
N_MIX_HEADS = D_MODEL // HEAD_DIM
MLSTM_HEADS = N_MIX_HEADS // 4
MLSTM_DH = HEAD_DIM
GLA_HEADS = N_MIX_HEADS // 4
GLA_DV = HEAD_DIM
GLA_DK = HEAD_DIM // 2
GLA_RANK = 16
GLA_TAU = 16.0
NAT_HEADS = N_MIX_HEADS // 2
NAT_DH = HEAD_DIM
WIN_ROWS = 8
WIN_COLS = 16
CONV_K = 3
CHUNK = 64
ROPE_BASE = 10000.0
N_EXPERTS = 16
EXPERT_FF = 2 * D_MODEL
CAPACITY_FACTOR = 2
LN_EPS = 1e-5
DEEPNORM_ALPHA = (2 * DEPTH) ** 0.25
DEEPNORM_BETA = (8 * DEPTH) ** -0.25

MLSTM_W = MLSTM_HEADS * MLSTM_DH
GLA_W = GLA_HEADS * GLA_DV
GLA_KW = GLA_HEADS * GLA_DK
NAT_W = NAT_HEADS * NAT_DH
MIX_W = MLSTM_W + GLA_W + NAT_W

SPLIT_NAMES = ("m_q", "m_k", "m_v", "m_o", "m_i_f", "m_f_f", "m_i_b", "m_f_b",
               "g_q", "g_k", "g_v", "g_r", "g_lr_f", "g_lr_b",
               "n_q", "n_k", "n_v")
SPLIT_SIZES = (MLSTM_W, MLSTM_W, MLSTM_W, MLSTM_W, MLSTM_HEADS, MLSTM_HEADS, MLSTM_HEADS, MLSTM_HEADS,
               GLA_KW, GLA_KW, GLA_W, GLA_W, GLA_RANK, GLA_RANK,
               NAT_W, NAT_W, NAT_W)
PROJ_W = sum(SPLIT_SIZES)

kernel_name = "hybrid_mlstm_gla_natten_ec_moe_dit"


def layer_norm(x, g, b):
    xf = x.astype(jnp.float32)
    mu = xf.mean(-1, keepdims=True)
    var = jnp.mean(jnp.square(xf - mu), -1, keepdims=True)
    return ((xf - mu) * lax.rsqrt(var + LN_EPS) * g + b).astype(x.dtype)


def split_heads(x, h):
    B, T, F = x.shape
    return x.reshape(B, T, h, F // h).transpose(0, 2, 1, 3)


def merge_heads(x):
    B, H, T, d = x.shape
    return x.transpose(0, 2, 1, 3).reshape(B, T, H * d)


def head_norm(h, g):
    mu = h.mean(-1, keepdims=True)
    var = jnp.mean(jnp.square(h - mu), -1, keepdims=True)
    return merge_heads((h - mu) * lax.rsqrt(var + LN_EPS)) * g


def centred_conv(x, w):
    K = w.shape[0]
    T = x.shape[1]
    pad = K // 2
    xp = jnp.pad(x, ((0, 0), (pad, pad), (0, 0)))
    out = xp[:, 0:T] * w[0]
    for j in range(1, K):
        out = out + xp[:, j:j + T] * w[j]
    return out


def rope_1d(x, pos):
    d2 = x.shape[-1] // 2
    inv = ROPE_BASE ** (-jnp.arange(d2, dtype=jnp.float32) / d2)
    ang = pos[:, None] * inv[None, :]
    cos, sin = jnp.cos(ang), jnp.sin(ang)
    x1, x2 = x[..., :d2], x[..., d2:]
    return jnp.concatenate([x1 * cos - x2 * sin, x1 * sin + x2 * cos], -1)


def axial_rope(x, rows, cols):
    h = x.shape[-1] // 2
    return jnp.concatenate([rope_1d(x[..., :h], rows), rope_1d(x[..., h:], cols)], -1)


def to_chunks(a):
    B, H, T = a.shape[:3]
    return jnp.moveaxis(a.reshape(B, H, T // CHUNK, CHUNK, *a.shape[3:]), 2, 0)


def from_chunks(a):
    a = jnp.moveaxis(a, 0, 2)
    return a.reshape(a.shape[0], a.shape[1], -1, *a.shape[4:])


def mlstm_scan(q, k, v, li, lf, state):
    tri = jnp.tril(jnp.ones((CHUNK, CHUNK), dtype=bool))

    def step(carry, blk):
        C, n, m = carry
        qc, kc, vc, ic, fc = blk
        b = jnp.cumsum(fc, axis=-1)
        logw = jnp.where(tri, b[..., :, None] - b[..., None, :] + ic[..., None, :], -jnp.inf)
        inter = b + m[..., None]
        m_t = jnp.maximum(inter, logw.max(-1))
        s = jnp.einsum('bhtd,bhsd->bhts', qc, kc) * jnp.exp(logw - m_t[..., None])
        e = jnp.exp(inter - m_t)
        num = jnp.einsum('bhts,bhsv->bhtv', s, vc) + e[..., None] * jnp.einsum('bhvd,bhtd->bhtv', C, qc)
        den = s.sum(-1) + e * jnp.einsum('bhd,bhtd->bht', n, qc)
        h = num / jnp.maximum(jnp.abs(den), jnp.exp(-m_t))[..., None]
        bl = b[..., -1]
        lw_end = bl[..., None] - b + ic
        m_new = jnp.maximum(bl + m, lw_end.max(-1))
        w_end = jnp.exp(lw_end - m_new[..., None])
        decay = jnp.exp(bl + m - m_new)
        C_new = decay[..., None, None] * C + jnp.einsum('bhs,bhsv,bhsd->bhvd', w_end, vc, kc)
        n_new = decay[..., None] * n + jnp.einsum('bhs,bhsd->bhd', w_end, kc)
        return (C_new, n_new, m_new), h

    state, h = lax.scan(step, state, tuple(to_chunks(a) for a in (q, k, v, li, lf)))
    return from_chunks(h), state


def gla_scan(q, k, v, a, state):
    tri = jnp.tril(jnp.ones((CHUNK, CHUNK), dtype=bool))[:, :, None]

    def step(S, blk):
        qc, kc, vc, ac = blk
        b = jnp.cumsum(ac, axis=2)
        expo = jnp.where(tri, b[:, :, :, None, :] - b[:, :, None, :, :], -jnp.inf)
        A = jnp.einsum('bhtd,bhsd,bhtsd->bhts', qc, kc, jnp.exp(expo))
        o = jnp.einsum('bhts,bhsv->bhtv', A, vc) + jnp.einsum('bhtd,bhdv->bhtv', qc * jnp.exp(b), S)
        bl = b[:, :, -1:, :]
        S_new = jnp.exp(bl[:, :, 0])[..., None] * S + jnp.einsum('bhsd,bhsv->bhdv', kc * jnp.exp(bl - b), vc)
        return S_new, o

    state, o = lax.scan(step, state, tuple(to_chunks(t) for t in (q, k, v, a)))
    return from_chunks(o), state


def bidir(scan, ctx_f, lat_f, ctx_b, lat_b, init):
    flip = lambda seq: tuple(jnp.flip(t, axis=2) for t in seq)
    oc_f, st_f = scan(*ctx_f, init)
    ol_f, _ = scan(*lat_f, st_f)
    oc_b, st_b = scan(*flip(ctx_b), init)
    ol_b, _ = scan(*flip(lat_b), st_b)
    return oc_f + jnp.flip(oc_b, 2), ol_f + jnp.flip(ol_b, 2)


def natten_latent(q, k, v, k_ctx, v_ctx, rpb):
    B, H, T, d = q.shape
    R = T // GRID_W
    wh = min(WIN_ROWS, R)
    scale = d ** -0.5
    qg = q.reshape(B, H, R, GRID_W, d)
    kg = k.reshape(B, H, R, GRID_W, d)
    vg = v.reshape(B, H, R, GRID_W, d)
    col = jnp.arange(GRID_W)
    cstart = jnp.clip(col - WIN_COLS // 2, 0, GRID_W - WIN_COLS)
    col_ok = (col[None, :] >= cstart[:, None]) & (col[None, :] < cstart[:, None] + WIN_COLS)
    dc_idx = jnp.clip(col[None, :] - col[:, None] + WIN_COLS - 1, 0, 2 * WIN_COLS - 2)
    n_loc = wh * GRID_W

    def row_block(r):
        rs = jnp.clip(r - wh // 2, 0, R - wh)
        kr = lax.dynamic_slice_in_dim(kg, rs, wh, axis=2)
        vr = lax.dynamic_slice_in_dim(vg, rs, wh, axis=2)
        qr = lax.dynamic_index_in_dim(qg, r, axis=2, keepdims=False)
        s_loc = jnp.einsum('bhqd,bhrkd->bhqrk', qr, kr) * scale
        dr_idx = rs + jnp.arange(wh) - r + WIN_ROWS - 1
        bias = rpb[:, dr_idx][:, :, dc_idx].transpose(0, 2, 1, 3)
        s_loc = jnp.where(col_ok[:, None, :], s_loc + bias, -jnp.inf).reshape(B, H, GRID_W, n_loc)
        s_ctx = jnp.einsum('bhqd,bhcd->bhqc', qr, k_ctx) * scale
        p = jax.nn.softmax(jnp.concatenate([s_loc, s_ctx], -1).astype(jnp.float32), axis=-1)
        return (jnp.einsum('bhqk,bhkd->bhqd', p[..., :n_loc], vr.reshape(B, H, n_loc, d))
                + jnp.einsum('bhqc,bhcd->bhqd', p[..., n_loc:], v_ctx))

    out = lax.map(row_block, jnp.arange(R))
    return jnp.moveaxis(out, 0, 2).reshape(B, H, T, d)


def context_attention(q, k, v):
    s = jnp.einsum('bhqd,bhkd->bhqk', q, k) * q.shape[-1] ** -0.5
    return jnp.einsum('bhqk,bhkd->bhqd', jax.nn.softmax(s.astype(jnp.float32), -1), v)


def project_stream(u, w_in, b_in, conv_w, gla_w2, gla_b2, pos):
    p = (u @ w_in + b_in).astype(jnp.float32)
    cuts = np.cumsum(SPLIT_SIZES)[:-1].tolist()
    parts = dict(zip(SPLIT_NAMES, jnp.split(p, cuts, axis=-1)))
    qk = jax.nn.silu(centred_conv(jnp.concatenate([parts["m_q"], parts["m_k"]], -1),
                                  conv_w.astype(jnp.float32)))
    mq = split_heads(qk[..., :MLSTM_W], MLSTM_HEADS)
    mk = split_heads(qk[..., MLSTM_W:], MLSTM_HEADS) * MLSTM_DH ** -0.5
    mv = split_heads(parts["m_v"], MLSTM_HEADS)
    gq = split_heads(parts["g_q"], GLA_HEADS) * GLA_DK ** -0.5
    gk = split_heads(parts["g_k"], GLA_HEADS)
    gv = split_heads(parts["g_v"], GLA_HEADS)
    if pos is not None:
        rows, cols = pos
        mq, mk = axial_rope(mq, rows, cols), axial_rope(mk, rows, cols)
        gq, gk = axial_rope(gq, rows, cols), axial_rope(gk, rows, cols)
    w2 = gla_w2.astype(jnp.float32)
    b2 = gla_b2.astype(jnp.float32)
    gla_decay = lambda lr, d: split_heads(jax.nn.log_sigmoid(lr @ w2[d] + b2[d]) / GLA_TAU, GLA_HEADS)
    tr = lambda g: jnp.swapaxes(g, 1, 2)
    return {
        "m_q": mq, "m_k": mk, "m_v": mv, "m_o": parts["m_o"],
        "m_i_f": tr(parts["m_i_f"]), "m_f_f": tr(jax.nn.log_sigmoid(parts["m_f_f"])),
        "m_i_b": tr(parts["m_i_b"]), "m_f_b": tr(jax.nn.log_sigmoid(parts["m_f_b"])),
        "g_q": gq, "g_k": gk, "g_v": gv, "g_r": parts["g_r"],
        "g_a_f": gla_decay(parts["g_lr_f"], 0), "g_a_b": gla_decay(parts["g_lr_b"], 1),
        "n_q": split_heads(parts["n_q"], NAT_HEADS), "n_k": split_heads(parts["n_k"], NAT_HEADS),
        "n_v": split_heads(parts["n_v"], NAT_HEADS),
    }


def hybrid_mixer(u_lat, u_ctx, rows, cols, w_in, b_in, conv_w, gla_w2, gla_b2,
                 mlstm_norm_g, gla_norm_g, rpb, w_out, need_ctx):
    L = project_stream(u_lat, w_in, b_in, conv_w, gla_w2, gla_b2, (rows, cols))
    C = project_stream(u_ctx, w_in, b_in, conv_w, gla_w2, gla_b2, None)
    B = u_lat.shape[0]
    f32 = jnp.float32
    m_init = (jnp.zeros((B, MLSTM_HEADS, MLSTM_DH, MLSTM_DH), f32),
              jnp.zeros((B, MLSTM_HEADS, MLSTM_DH), f32), jnp.zeros((B, MLSTM_HEADS), f32))
    mf = lambda S: (S["m_q"], S["m_k"], S["m_v"], S["m_i_f"], S["m_f_f"])
    mb = lambda S: (S["m_q"], S["m_k"], S["m_v"], S["m_i_b"], S["m_f_b"])
    m_ctx, m_lat = bidir(mlstm_scan, mf(C), mf(L), mb(C), mb(L), m_init)
    g_init = jnp.zeros((B, GLA_HEADS, GLA_DK, GLA_DV), f32)
    gf = lambda S: (S["g_q"], S["g_k"], S["g_v"], S["g_a_f"])
    gb = lambda S: (S["g_q"], S["g_k"], S["g_v"], S["g_a_b"])
    g_ctx, g_lat = bidir(gla_scan, gf(C), gf(L), gb(C), gb(L), g_init)
    rpb = rpb.astype(f32)
    n_lat = natten_latent(L["n_q"], L["n_k"], L["n_v"], C["n_k"], C["n_v"], rpb)

    def merge(S, m, g, n, dtype):
        y = jnp.concatenate([head_norm(m, mlstm_norm_g) * jax.nn.sigmoid(S["m_o"]),
                             head_norm(g, gla_norm_g) * jax.nn.silu(S["g_r"]),
                             merge_heads(n)], -1)
        return y.astype(dtype) @ w_out

    y_lat = merge(L, m_lat, g_lat, n_lat, u_lat.dtype)
    y_ctx = None
    if need_ctx:
        n_ctx = context_attention(C["n_q"], C["n_k"], C["n_v"])
        y_ctx = merge(C, m_ctx, g_ctx, n_ctx, u_ctx.dtype)
    return y_lat, y_ctx


def ec_ffn(u, w_router, w_gate, w_up, w_down):
    B, T, D = u.shape
    cap = CAPACITY_FACTOR * T // N_EXPERTS
    aff = jax.nn.softmax((u @ w_router).astype(jnp.float32), -1)
    gate, idx = lax.top_k(jnp.swapaxes(aff, 1, 2), cap)
    bidx = jnp.arange(B)[:, None, None]
    xe = u[bidx, idx]
    h = jax.nn.silu(jnp.einsum('becd,edf->becf', xe, w_gate)) * jnp.einsum('becd,edf->becf', xe, w_up)
    y = (jnp.einsum('becf,efd->becd', h, w_down) * gate[..., None]).astype(u.dtype)
    return jnp.zeros_like(u).at[bidx, idx].add(y)


def setup_inputs(seed: int = 0) -> dict:
    key = jax.random.key(seed)
    ks = jax.random.split(key, 23)
    D = D_MODEL
    nrm = lambda k, shape, s: jax.random.normal(k, shape, jnp.float32) * s
    offs = np.cumsum((0,) + SPLIT_SIZES)
    start = dict(zip(SPLIT_NAMES, offs[:-1].tolist()))
    fbias = np.zeros((PROJ_W,), np.float32)
    for name in ("m_f_f", "m_f_b"):
        fbias[start[name]:start[name] + MLSTM_HEADS] = np.linspace(3.0, 6.0, MLSTM_HEADS)
    return {
        "x": nrm(ks[0], (BATCH, SEQ, D), 1.0),
        "c": nrm(ks[1], (BATCH, D), 1.0),
        "ctx": nrm(ks[2], (BATCH, CTX_LEN, D), 1.0),
        "c_ctx": nrm(ks[3], (D,), 1.0),
        "w_mod": nrm(ks[4], (DEPTH, D, 6 * D), 0.5 * D ** -0.5),
        "b_mod": nrm(ks[5], (DEPTH, 6 * D), 0.02),
        "w_in": nrm(ks[6], (DEPTH, D, PROJ_W), D ** -0.5),
        "b_in": nrm(ks[7], (DEPTH, PROJ_W), 0.02) + jnp.asarray(fbias),
        "conv_w": nrm(ks[8], (DEPTH, CONV_K, 2 * MLSTM_W), CONV_K ** -0.5),
        "gla_w2": nrm(ks[9], (DEPTH, 2, GLA_RANK, GLA_KW), GLA_RANK ** -0.5),
        "gla_b2": nrm(ks[10], (DEPTH, 2, GLA_KW), 0.1),
        "mlstm_norm_g": 1.0 + nrm(ks[11], (DEPTH, MLSTM_W), 0.02),
        "gla_norm_g": 1.0 + nrm(ks[12], (DEPTH, GLA_W), 0.02),
        "rpb": nrm(ks[13], (DEPTH, NAT_HEADS, 2 * WIN_ROWS - 1, 2 * WIN_COLS - 1), 0.05),
        "w_out": nrm(ks[14], (DEPTH, MIX_W, D), MIX_W ** -0.5 * DEEPNORM_BETA),
        "ln1_g": 1.0 + nrm(ks[15], (DEPTH, D), 0.02),
        "ln1_b": nrm(ks[16], (DEPTH, D), 0.02),
        "w_router": nrm(ks[17], (DEPTH, D, N_EXPERTS), D ** -0.5),
        "w_gate": nrm(ks[18], (DEPTH, N_EXPERTS, D, EXPERT_FF), D ** -0.5),
        "w_up": nrm(ks[19], (DEPTH, N_EXPERTS, D, EXPERT_FF), D ** -0.5),
        "w_down": nrm(ks[20], (DEPTH, N_EXPERTS, EXPERT_FF, D), EXPERT_FF ** -0.5 * DEEPNORM_BETA),
        "ln2_g": 1.0 + nrm(ks[21], (DEPTH, D), 0.02),
        "ln2_b": nrm(ks[22], (DEPTH, D), 0.02),
    }


def reference(x, c, ctx, c_ctx, w_mod, b_mod, w_in, b_in, conv_w, gla_w2, gla_b2,
              mlstm_norm_g, gla_norm_g, rpb, w_out, ln1_g, ln1_b, w_router, w_gate, w_up,
              w_down, ln2_g, ln2_b):
    T = x.shape[1]
    t = jnp.arange(T)
    rows = (t // GRID_W).astype(jnp.float32)
    cols = (t % GRID_W).astype(jnp.float32)
    silu_c = jax.nn.silu(c)
    silu_cc = jax.nn.silu(c_ctx)
    for l in range(DEPTH):
        need_ctx = l < DEPTH - 1
        mod = (silu_c @ w_mod[l] + b_mod[l])[:, None, :]
        modc = (silu_cc @ w_mod[l] + b_mod[l])[None, None, :]
        sh1, sc1, g1, sh2, sc2, g2 = jnp.split(mod, 6, axis=-1)
        sh1c, sc1c, g1c, sh2c, sc2c, g2c = jnp.split(modc, 6, axis=-1)
        y_lat, y_ctx = hybrid_mixer(x * (1 + sc1) + sh1, ctx * (1 + sc1c) + sh1c, rows, cols,
                                    w_in[l], b_in[l], conv_w[l], gla_w2[l], gla_b2[l],
                                    mlstm_norm_g[l], gla_norm_g[l], rpb[l], w_out[l], need_ctx)
        x = layer_norm(DEEPNORM_ALPHA * x + g1 * y_lat, ln1_g[l], ln1_b[l])
        f_lat = ec_ffn(x * (1 + sc2) + sh2, w_router[l], w_gate[l], w_up[l], w_down[l])
        x = layer_norm(DEEPNORM_ALPHA * x + g2 * f_lat, ln2_g[l], ln2_b[l])
        if need_ctx:
            ctx = layer_norm(DEEPNORM_ALPHA * ctx + g1c * y_ctx, ln1_g[l], ln1_b[l])
            f_ctx = ec_ffn(ctx * (1 + sc2c) + sh2c, w_router[l], w_gate[l], w_up[l], w_down[l])
            ctx = layer_norm(DEEPNORM_ALPHA * ctx + g2c * f_ctx, ln2_g[l], ln2_b[l])
    return x
```

```python
import contextlib
import numpy as np
import ml_dtypes
import concourse.bass as bass
import concourse.mybir as mybir
from concourse.bass_utils import run_bass_kernel_spmd

F32 = mybir.dt.float32
BF16 = mybir.dt.bfloat16
U32 = mybir.dt.uint32
ALU = mybir.AluOpType
ACT = mybir.ActivationFunctionType
AX = mybir.AxisListType

D = 1024
SEQ = 2048
CTX = 256
T = SEQ + CTX
NT = T // 128
NE = 16
FF = 2048
CAP = 256
CAPC = 32
CAPT = CAP + CAPC
PROJ = 3376
ALPHA = 8.0 ** 0.25
EPS = 1e-5
NEG = -30000.0
O_MQ, O_MK, O_MV, O_MO = 0, 256, 512, 768
O_MIF, O_MFF, O_MIB, O_MFB = 1024, 1028, 1032, 1036
O_GQ, O_GK, O_GV, O_GR = 1040, 1168, 1296, 1552
O_LRF, O_LRB = 1808, 1824
O_NQ, O_NK, O_NV = 1840, 2352, 2864


class Buf:
    __slots__ = ("name", "w", "r")

    def __init__(self, name=""):
        self.name = name
        self.w = None
        self.r = {}


class Sched:
    ENG = ("pe", "act", "dve", "pool", "sp")

    def __init__(self, nc, n_dma_sems=14):
        self.nc = nc
        self.ops = {e: [] for e in self.ENG}
        self.seq = {e: 0 for e in self.ENG}
        self.known = {e: {} for e in self.ENG}
        self.n_dma_sems = n_dma_sems
        self.dma_ring = {e: 0 for e in self.ENG}
        self.dma_val = {}
        self.semkeys = set()

    def _need(self, eng, tok, waits):
        if tok is None:
            return
        sk, val = tok
        if eng == "pe" and sk == ("c", "pe"):
            return
        if self.known[eng].get(sk, 0) >= val:
            return
        self.known[eng][sk] = val
        waits.append((sk, val))

    def _deps(self, eng, reads, writes):
        waits = []
        for b in reads:
            self._need(eng, b.w, waits)
        for b in writes:
            self._need(eng, b.w, waits)
            for sk, val in b.r.items():
                self._need(eng, (sk, val), waits)
        best = {}
        for sk, val in waits:
            if best.get(sk, 0) < val:
                best[sk] = val
        return list(best.items())

    def _mark(self, tok, reads, writes):
        sk, val = tok
        for b in reads:
            if b.r.get(sk, 0) < val:
                b.r[sk] = val
        for b in writes:
            b.w = tok
            b.r = {}

    def op(self, eng, fn, reads=(), writes=()):
        waits = self._deps(eng, reads, writes)
        self.seq[eng] += 1
        sk = ("c", eng)
        self.semkeys.add(sk)
        tok = (sk, self.seq[eng])
        self.ops[eng].append((waits, fn, (sk, 1)))
        self._mark(tok, reads, writes)
        return tok

    def dma(self, eng, out, in_, reads=(), writes=(), **kw):
        slot = self.dma_ring[eng]
        self.dma_ring[eng] = (slot + 1) % self.n_dma_sems
        sk = ("d", eng, slot)
        self.semkeys.add(sk)
        waits = self._deps(eng, reads, writes)
        prev = self.dma_val.get(sk, 0)
        if prev > 0 and self.known[eng].get(sk, 0) < prev:
            self.known[eng][sk] = prev
            waits = [w for w in waits if w[0] != sk] + [(sk, prev)]
        val = prev + 16
        self.dma_val[sk] = val
        tok = (sk, val)

        def fn(e, out=out, in_=in_, kw=kw):
            return e.dma_start(out=out, in_=in_, **kw)
        self.ops[eng].append((waits, fn, (sk, 16)))
        self._mark(tok, reads, writes)
        return tok

    def _all_tokens(self):
        toks = [(("c", e), self.seq[e]) for e in self.ENG if self.seq[e] > 0]
        toks += [(sk, v) for sk, v in self.dma_val.items()]
        return toks

    def barrier(self):
        toks = self._all_tokens()
        for e in self.ENG:
            waits = []
            for tok in toks:
                self._need(e, tok, waits)
            if waits:
                self.ops[e].append((waits, None, None))

    def final_wait(self, eng="sp"):
        waits = []
        for sk, val in self._all_tokens():
            if self.known[eng].get(sk, 0) < val:
                self.known[eng][sk] = val
                waits.append((sk, val))
        self.ops[eng].append((waits, None, None))

    def emit(self):
        nc = self.nc
        with contextlib.ExitStack() as st:
            sems = {}
            for sk in sorted(self.semkeys, key=str):
                sems[sk] = st.enter_context(nc.semaphore("s_" + "_".join(str(x) for x in sk)))
            block = st.enter_context(nc.Block())

            def run(e, lst):
                for waits, fn, inc in lst:
                    for sk, val in waits:
                        e.wait_ge(sems[sk], val)
                    if fn is not None:
                        fn(e).then_inc(sems[inc[0]], inc[1])

            @block.tensor
            def _(e):
                run(e, self.ops["pe"])

            @block.scalar
            def _(e):
                run(e, self.ops["act"])

            @block.vector
            def _(e):
                run(e, self.ops["dve"])

            @block.gpsimd
            def _(e):
                run(e, self.ops["pool"])

            @block.sync
            def _(e):
                run(e, self.ops["sp"])


class Tl:
    __slots__ = ("ap", "b")

    def __init__(self, ap, b):
        self.ap = ap
        self.b = b


def _bf(x):
    return np.ascontiguousarray(x).astype(ml_dtypes.bfloat16)


def host_consts():
    c = {}
    c["ident_f"] = np.eye(128, dtype=np.float32)
    s = np.arange(128)[:, None]
    t = np.arange(128)[None, :]
    c["tri"] = np.stack([(s <= t), (s >= t)]).astype(np.float32)
    c["mneg"] = np.where(c["tri"] > 0, 0.0, NEG).astype(np.float32)
    tt = np.arange(SEQ)
    rows = (tt // 64).astype(np.float32)
    cols = (tt % 64).astype(np.float32)

    def rope_tab(dh, reps):
        h = dh // 2
        d2 = h // 2
        inv = (10000.0 ** (-np.arange(d2, dtype=np.float32) / d2)).astype(np.float32)
        C = np.zeros((dh, SEQ), np.float32)
        Sg = np.zeros((dh, SEQ), np.float32)
        P = np.zeros((dh, dh), np.float32)
        for d in range(dh):
            blk = d // h
            j = (d % h) % d2
            half = (d % h) // d2
            pos = rows if blk == 0 else cols
            ang = (pos * inv[j]).astype(np.float32)
            C[d] = np.cos(ang)
            Sg[d] = np.sin(ang) * (-1.0 if half == 0 else 1.0)
            partner = blk * h + (1 - half) * d2 + j
            P[partner, d] = 1.0
        Cr = np.tile(C, (reps, 1))
        Sr = np.tile(Sg, (reps, 1))
        Pr = np.kron(np.eye(reps, dtype=np.float32), P)
        return np.stack([Cr, Sr]).astype(np.float32), Pr.astype(np.float32)
    c["ropeM"], c["permM"] = rope_tab(64, 2)
    c["ropeG"], c["permG"] = rope_tab(32, 2)
    c["iota_t"] = np.tile(np.arange(T, dtype=np.float32)[None, :], (128, 1))
    c["pidx"] = np.arange(128, dtype=np.float32)[:, None].copy()
    mk = np.ones((64, T), np.float32)
    mk[:, ::128] = 0.0
    c["scanmask"] = mk
    c["sel64"] = np.zeros((64, 64, 128), np.float32)
    for r in range(64):
        c["sel64"][r, r, :] = 1.0
    return c


def natten_tables(rpb):
    L = rpb.shape[0]
    krl = np.arange(2)[:, None, None, None, None]
    ck = np.arange(64)[None, :, None, None, None]
    j = np.arange(8)[None, None, :, None, None]
    qrl = np.arange(8)[None, None, None, :, None]
    cq = np.arange(64)[None, None, None, None, :]
    out = np.empty((L, 8, 3, 128, 8, 512), np.float32)
    for p, (q0, k0) in enumerate(((0, 0), (8, 4), (24, 16))):
        qr = q0 + qrl
        kr = k0 + 2 * j + krl
        rs = np.clip(qr - 4, 0, 24)
        cs = np.clip(cq - 8, 0, 48)
        valid = (kr >= rs) & (kr < rs + 8) & (ck >= cs) & (ck < cs + 16)
        dr = np.clip(kr - qr + 7, 0, 14)
        dc = np.clip(ck - cq + 15, 0, 30)
        valid, dr, dc = np.broadcast_arrays(valid, dr, dc)
        g = rpb[:, :, dr, dc]
        g = np.where(valid[None, None], g, np.float32(NEG))
        out[:, :, p] = g.reshape(L, 8, 128, 8, 512)
    return out


class Builder:
    def __init__(self, NS, NL, stop=None, dbg=(), moe=True, nat=True):
        self.NS, self.NL, self.stop, self.dbgn = NS, NL, stop, set(dbg)
        nc = self.nc = bass.Bass("TRN2", target_bir_lowering=False)
        self.S = Sched(nc)
        self.st = contextlib.ExitStack()
        di = lambda name, shape, dt=F32: nc.dram_tensor(name, list(shape), dt, kind="ExternalInput").ap()
        self.din = {}
        I = self.din
        I["x"] = di("x", [NS, SEQ, D])
        I["ctx"] = di("ctx", [NS, CTX, D])
        I["cc"] = di("cc", [5, D])
        for name, shp in (("w_mod", [NL, D, 6 * D]), ("b_mod", [NL, 6 * D]), ("w_in", [NL, D, PROJ]),
                          ("b_in", [NL, PROJ]), ("conv_w", [NL, 3, 512]), ("gla_w2", [NL, 2, 16, 128]),
                          ("gla_b2", [NL, 2, 128]), ("mlstm_norm_g", [NL, 256]), ("gla_norm_g", [NL, 256]),
                          ("w_out", [NL, D, D]), ("ln1_g", [NL, D]), ("ln1_b", [NL, D]),
                          ("w_router", [NL, D, NE]), ("w_gate", [NL, NE, D, FF] if moe else [1, 1, 8, 8]),
                          ("w_up", [NL, NE, D, FF] if moe else [1, 1, 8, 8]),
                          ("w_down", [NL, NE, FF, D] if moe else [1, 1, 8, 8]), ("ln2_g", [NL, D]), ("ln2_b", [NL, D]),
                          ("nbias", [NL, 8, 3, 128, 8, 512] if nat else [1, 1, 1, 8, 1, 8]),
                          ("ident_f", [128, 128]), ("tri", [2, 128, 128]), ("mneg", [2, 128, 128]),
                          ("ropeM", [2, 128, SEQ]), ("permM", [128, 128]), ("ropeG", [2, 64, SEQ]),
                          ("permG", [64, 64]), ("iota_t", [128, T]), ("pidx", [128, 1]),
                          ("sel64", [64, 64, 128]), ("scanmask", [64, T])):
            I[name] = di(name, shp)
        self.out = nc.dram_tensor("out", [NS, SEQ, D], F32, kind="ExternalOutput").ap()
        ds = lambda name, shape, dt=F32: nc.dram_tensor(name, list(shape), dt, kind="Internal").ap()
        self.xs = ds("xs", [NS, T, D])
        self.xsb = [Buf("xs%d" % s) for s in range(NS)]
        self.modv = ds("modv", [5, 6 * D])
        self.modvb = Buf("modv")
        self.u2b = ds("u2b", [NS, T, D], BF16)
        self.u2bb = [Buf() for _ in range(NS)]
        self.affd = ds("affd", [NS * NE, T])
        self.affdb = Buf("affd")
        self.xeT = ds("xeT", [NE, 8, 128, NS * CAPT], BF16)
        self.xeTb = [Buf() for _ in range(NE)]
        self.ye = ds("ye", [NS, NE, CAPT, D], BF16)
        self.yeb = [[Buf() for _ in range(NE)] for _ in range(NS)]
        self.dbg = {}
        if 'fmoe' in self.dbgn:
            self.dbg_f = nc.dram_tensor('dbg_fmoe', [T, D], F32, kind='ExternalOutput').ap()
        self.ARENA = 51200
        self.arena = self.st.enter_context(nc.sbuf_tensor("arena", [128, self.ARENA], F32))
        self.ps = self.st.enter_context(nc.psum_tensor("ps", [128, 8, 512], F32))
        self.pb = [Buf("ps%d" % i) for i in range(8)]
        self.col = 0
        self.inb = {k: Buf(k) for k in I}

    def alloc(self, n, dt=F32, parts=128):
        nb = n * (2 if dt == BF16 else 4)
        ncol = (nb + 3) // 4
        assert self.col + ncol <= self.ARENA, ("SBUF arena overflow", self.col, ncol)
        ap = self.arena[0:parts, self.col:self.col + ncol]
        self.col += ncol
        if dt != F32:
            ap = ap.bitcast(dt)
        return Tl(ap, Buf())

    def mark(self):
        return self.col

    def release(self, m):
        self.S.barrier()
        self.col = m

    def _bufs(self, lst):
        return [x.b if isinstance(x, Tl) else x for x in lst]

    def dve(self, fn, r=(), w=()):
        return self.S.op("dve", fn, self._bufs(r), self._bufs(w))

    def act(self, fn, r=(), w=()):
        return self.S.op("act", fn, self._bufs(r), self._bufs(w))

    def pool(self, fn, r=(), w=()):
        return self.S.op("pool", fn, self._bufs(r), self._bufs(w))

    def pe(self, fn, r=(), w=()):
        return self.S.op("pe", fn, self._bufs(r), self._bufs(w))

    def dma(self, out, in_, r=(), w=(), q="sp", **kw):
        return self.S.dma(q, out, in_, self._bufs(r), self._bufs(w), **kw)

    def mm(self, out, lhsT, rhs, start, stop, r, w):
        return self.pe(lambda e: e.matmul(out, lhsT=lhsT, rhs=rhs, start=start, stop=stop), r, w)

    def debug_dump(self, name, ap, bufs, shape, dt=F32):
        if name not in self.dbgn:
            return
        d = self.nc.dram_tensor("dbg_" + name, list(shape), dt, kind="ExternalOutput").ap()
        self.dbg[name] = d
        self.dma(d, ap, r=bufs)

    def consts(self):
        I = self.din
        self.ident_f = self.alloc(128)
        self.dma(self.ident_f.ap, I["ident_f"], w=[self.ident_f])
        self.ident_b = self.alloc(128, BF16)
        self.dve(lambda e: e.tensor_copy(self.ident_b.ap, self.ident_f.ap), [self.ident_f], [self.ident_b])
        self.tri = [self.alloc(128), self.alloc(128)]
        self.mneg = [self.alloc(128), self.alloc(128)]
        self.m01b = [self.alloc(128, BF16), self.alloc(128, BF16)]
        for d in range(2):
            self.dma(self.tri[d].ap, I["tri"][d], w=[self.tri[d]])
            self.dma(self.mneg[d].ap, I["mneg"][d], w=[self.mneg[d]])
            self.dve(lambda e, d=d: e.tensor_copy(self.m01b[d].ap, self.tri[d].ap), [self.tri[d]], [self.m01b[d]])
        self.pidx = self.alloc(1)
        self.dma(self.pidx.ap, I["pidx"], w=[self.pidx])
        self.epsc = self.alloc(1)
        self.dve(lambda e: e.memset(self.epsc.ap, EPS), [], [self.epsc])
        self.ccT = self.alloc(8 * 5)
        keep = self.mark()
        cc = self.alloc(D, parts=5)
        self.dma(cc.ap, I["cc"], w=[cc])
        self.act(lambda e: e.activation(out=cc.ap, in_=cc.ap, func=ACT.Silu), [cc], [cc])
        pb = self.pb[0]
        for k in range(8):
            self.pe(lambda e, k=k: e.transpose(self.ps[:, 0, 8 * k:8 * k + 5], cc.ap[:, 128 * k:128 * k + 128],
                                                self.ident_f.ap[0:5, 0:5]), [cc, self.ident_f], [pb])
        self.dve(lambda e: e.tensor_copy(self.ccT.ap.rearrange("p (k c) -> p k c", c=5),
                                          self.ps[:, 0, 0:64].rearrange("p (k c) -> p k c", c=8)[:, :, 0:5]),
                 [pb], [self.ccT])
        self.release(keep)

    def layer_mod(self, l):
        I = self.din
        m = self.mark()
        wts = [self.alloc(8 * 512) for _ in range(2)]
        bms = [self.alloc(512, parts=5) for _ in range(2)]
        ots = [self.alloc(512, parts=5) for _ in range(2)]
        wv = I["w_mod"][l].rearrange("(k p) n -> p k n", p=128)
        for n in range(12):
            wt, bm, ot = wts[n % 2], bms[n % 2], ots[n % 2]
            pb = self.pb[n % 2]
            self.dma(wt.ap.rearrange("p (k n) -> p k n", n=512), wv[:, :, 512 * n:512 * n + 512], w=[wt],
                     q=("sp" if n % 2 == 0 else "pool"))
            self.dma(bm.ap, I["b_mod"][l, 512 * n:512 * n + 512].partition_broadcast(5), w=[bm])
            for k in range(8):
                self.mm(self.ps[0:5, n % 2, :], self.ccT.ap[:, 5 * k:5 * k + 5], wt.ap[:, 512 * k:512 * k + 512],
                        k == 0, k == 7, [self.ccT, wt], [pb])
            self.dve(lambda e, ot=ot, bm=bm, n=n: e.tensor_tensor(ot.ap, self.ps[0:5, n % 2, :], bm.ap, ALU.add),
                     [pb, bm], [ot])
            if n in (2, 3, 8, 9):
                self.dve(lambda e, ot=ot: e.tensor_scalar(ot.ap, ot.ap, 1.0, None, ALU.add), [ot], [ot])
            self.dma(self.modv[:, 512 * n:512 * n + 512], ot.ap, r=[ot], w=[self.modvb])
        self.release(m)

    def load_rep(self, tl, row_ap, extra_r=()):
        self.dma(tl.ap, row_ap.partition_broadcast(128), r=list(extra_r), w=[tl])

    def x_src(self, l, s, i):
        if l == 0:
            if i < 2:
                return self.din["ctx"][s, 128 * i:128 * i + 128, :], self.inb["ctx"]
            return self.din["x"][s, 128 * (i - 2):128 * (i - 2) + 128, :], self.inb["x"]
        return self.xs[s, 128 * i:128 * i + 128, :], self.xsb[s]

    def mixer(self, l, s):
        I = self.din
        S = self.S
        m0 = self.mark()
        uT = self.alloc(8 * T, BF16)
        uTv = uT.ap.rearrange("p (k t) -> p k t", t=T)
        yT = self.alloc(8 * T, BF16)
        yTv = yT.ap.rearrange("p (k t) -> p k t", t=T)
        m1 = self.mark()
        reps = {}
        for nm, row, off in (("scL", s, D), ("shL", s, 0), ("scC", 4, D), ("shC", 4, 0)):
            reps[nm] = self.alloc(D)
            self.load_rep(reps[nm], self.modv[row, off:off + D], [self.modvb])
        xt = [self.alloc(D) for _ in range(2)]
        tm = [self.alloc(D) for _ in range(2)]
        ub = [self.alloc(D, BF16) for _ in range(2)]
        for i in range(NT):
            x_, t_, u_ = xt[i % 2], tm[i % 2], ub[i % 2]
            src, sb = self.x_src(l, s, i)
            self.dma(x_.ap, src, r=[sb], w=[x_], q=("sp" if i % 2 == 0 else "pool"))
            sc, sh = (reps["scC"], reps["shC"]) if i < 2 else (reps["scL"], reps["shL"])
            self.dve(lambda e, x_=x_, t_=t_, sc=sc: e.tensor_tensor(t_.ap, x_.ap, sc.ap, ALU.mult), [x_, sc], [t_])
            self.pool(lambda e, t_=t_, u_=u_, sh=sh: e.tensor_tensor(u_.ap, t_.ap, sh.ap, ALU.add), [t_, sh], [u_])
            bank = 6 + (i % 2)
            psb = self.ps[:, bank, :].bitcast(BF16)
            for k in range(8):
                self.pe(lambda e, k=k, u_=u_, psb=psb: e.transpose(psb[:, 128 * k:128 * k + 128],
                                                                  u_.ap[:, 128 * k:128 * k + 128], self.ident_b.ap),
                        [u_, self.ident_b], [self.pb[bank]])
            self.act(lambda e, i=i, psb=psb: e.copy(uTv[:, :, 128 * i:128 * i + 128],
                                                     psb.rearrange("p (k t) -> p k t", t=128)),
                     [self.pb[bank]], [uT])
        self.release(m1)
        self.debug_dump("uT", uT.ap, [uT], [128, 8 * T], BF16)
        if self.stop == "M0":
            self.release(m0)
            return
        self.mix_mlstm(l, s, uT, uTv, yT, yTv)
        if self.stop == "M1":
            self.release(m0)
            return
        self.mix_gla(l, s, uT, uTv, yT, yTv)
        if self.stop == "M2":
            self.release(m0)
            return
        self.mix_nat(l, s, uT, uTv, yT, yTv)
        self.debug_dump("yT", yT.ap, [yT], [128, 8 * T], BF16)
        if self.stop == "M3":
            self.release(m0)
            return
        self.mix_out(l, s, yT, yTv)
        self.release(m0)

    def proj_fm(self, l, s, uTv, uT, c0, ncols, dst_fn, wtiles, bias_scale=None):
        I = self.din
        wt = wtiles[0]
        bcol = wtiles[1]
        self.dma(wt.ap[:, 0:8 * ncols].rearrange("p (k c) -> p k c", c=ncols),
                 I["w_in"][l, :, c0:c0 + ncols].rearrange("(k p) c -> p k c", p=128), w=[wt], q="pool")
        self.dma(bcol.ap[0:ncols, 0:1], I["b_in"][l, c0:c0 + ncols].rearrange("(c o) -> c o", o=1), w=[bcol])
        if bias_scale is not None:
            self.dve(lambda e: e.tensor_scalar(bcol.ap[0:ncols, 0:1], bcol.ap[0:ncols, 0:1], bias_scale, None, ALU.mult),
                     [bcol], [bcol])
        for tc in range(5):
            t0 = 512 * tc
            n = min(512, T - t0)
            bank = tc % 2
            for k in range(8):
                self.mm(self.ps[0:ncols, bank, 0:n], wt.ap[:, k * ncols:(k + 1) * ncols], uTv[:, k, t0:t0 + n],
                        k == 0, k == 7, [wt, uT], [self.pb[bank]])
            dst_fn(tc, t0, n, self.ps[0:ncols, bank, 0:n], self.pb[bank], bcol)

    def mix_mlstm(self, l, s, uT, uTv, yT, yTv):
        I = self.din
        mA = self.mark()
        qkT = self.alloc(4 * T, BF16)
        qkv = qkT.ap.rearrange("p (c t) -> p c t", t=T)
        Vx = self.alloc(NT * 4 * 65, BF16)
        Vxv = Vx.ap.rearrange("p (i h d) -> p i h d", h=4, d=65)
        go = self.alloc(NT * 256, BF16)
        gov = go.ap.rearrange("p (i c) -> p i c", c=256)
        G = self.alloc(NT * 16)
        Gv = G.ap.rearrange("p (i c) -> p i c", c=16)
        mB = self.mark()
        wq = [self.alloc(8 * 128, BF16), self.alloc(1)]
        pfl = self.alloc(SEQ + 2)
        pfc = self.alloc(CTX + 2)
        t1l = self.alloc(SEQ)
        t1c = self.alloc(CTX)
        cw = self.alloc(4 * 3)
        for c in range(4):
            self.dma(cw.ap[:, 3 * c:3 * c + 3], I["conv_w"][l, :, 128 * c:128 * c + 128].rearrange("j p -> p j"), w=[cw],
                     allow_slow_non_contiguous=True)
        permM = self.alloc(128)
        self.dma(permM.ap, I["permM"], w=[permM])
        rt = [[self.alloc(512), self.alloc(512)] for _ in range(2)]
        ra = [self.alloc(512) for _ in range(2)]
        for e_ in (pfl, pfc):
            self.pool(lambda e, e_=e_: e.memset(e_.ap, 0.0), [], [e_])
        for c in range(4):
            def evac(tc, t0, n, psap, pbuf, bcol, c=c):
                if tc == 0:
                    self.act(lambda e: e.activation(out=pfc.ap[:, 1:257], in_=psap[:, 0:256], func=ACT.Identity,
                                                    bias=bcol.ap[:, 0:1], scale=1.0), [pbuf, bcol], [pfc])
                    self.act(lambda e: e.activation(out=pfl.ap[:, 1:257], in_=psap[:, 256:512], func=ACT.Identity,
                                                    bias=bcol.ap[:, 0:1], scale=1.0), [pbuf, bcol], [pfl])
                else:
                    o0 = t0 - 256 + 1
                    self.act(lambda e: e.activation(out=pfl.ap[:, o0:o0 + n], in_=psap, func=ACT.Identity,
                                                    bias=bcol.ap[:, 0:1], scale=1.0), [pbuf, bcol], [pfl])
            self.proj_fm(l, s, uTv, uT, 128 * c, 128, evac, wq)
            for (pf, t1, n) in ((pfl, t1l, SEQ), (pfc, t1c, CTX)):
                self.dve(lambda e, pf=pf, t1=t1, n=n, c=c: e.tensor_scalar(t1.ap, pf.ap[:, 1:n + 1], cw.ap[:, 3 * c + 1:3 * c + 2],
                                                                       None, ALU.mult), [pf, cw], [t1])
                self.dve(lambda e, pf=pf, t1=t1, n=n, c=c: e.scalar_tensor_tensor(t1.ap, pf.ap[:, 0:n], cw.ap[:, 3 * c:3 * c + 1],
                                                                              t1.ap, ALU.mult, ALU.add), [pf, cw, t1], [t1])
                self.dve(lambda e, pf=pf, t1=t1, n=n, c=c: e.scalar_tensor_tensor(t1.ap, pf.ap[:, 2:n + 2], cw.ap[:, 3 * c + 2:3 * c + 3],
                                                                              t1.ap, ALU.mult, ALU.add), [pf, cw, t1], [t1])
                self.act(lambda e, t1=t1: e.activation(out=t1.ap, in_=t1.ap, func=ACT.Silu), [t1], [t1])
                if c >= 2:
                    self.pool(lambda e, t1=t1: e.tensor_scalar(t1.ap, t1.ap, 0.125, None, ALU.mult), [t1], [t1])
            self.pool(lambda e, c=c: e.tensor_copy(qkv[:, c, 0:CTX], t1c.ap), [t1c], [qkT])
            for j in range(4):
                r0, r1 = rt[j % 2]
                self.dma(r0.ap, I["ropeM"][0, :, 512 * j:512 * j + 512], w=[r0])
                self.dma(r1.ap, I["ropeM"][1, :, 512 * j:512 * j + 512], w=[r1], q="pool")
                bank = 2 + (j % 2)
                self.mm(self.ps[:, bank, :], permM.ap, t1l.ap[:, 512 * j:512 * j + 512], True, True, [permM, t1l], [self.pb[bank]])
                a_ = ra[j % 2]
                self.dve(lambda e, a_=a_, r0=r0, j=j: e.tensor_tensor(a_.ap, t1l.ap[:, 512 * j:512 * j + 512], r0.ap, ALU.mult),
                         [t1l, r0], [a_])
                self.dve(lambda e, r1=r1, bank=bank: e.tensor_tensor(r1.ap, self.ps[:, bank, :], r1.ap, ALU.mult),
                         [self.pb[bank], r1], [r1])
                self.pool(lambda e, a_=a_, r1=r1, j=j, c=c: e.tensor_tensor(qkv[:, c, CTX + 512 * j:CTX + 512 * j + 512], a_.ap, r1.ap, ALU.add),
                          [a_, r1], [qkT])
        self.release(mB)
        self.debug_dump("qkT", qkT.ap, [qkT], [128, 4 * T], BF16)
        mC = self.mark()
        wvo = self.alloc(8 * 512, BF16)
        brep = self.alloc(512)
        tmpv = [self.alloc(512) for _ in range(2)]
        self.pool(lambda e: e.memset(Vxv[:, :, :, 64:65], 1.0), [], [Vx])

        def evac_vo(i, psap, pbuf, br):
            tv = tmpv[i % 2]
            self.dve(lambda e: e.tensor_tensor(tv.ap, psap, br.ap, ALU.add), [pbuf, br], [tv])
            self.pool(lambda e: e.tensor_copy(Vxv[:, i, :, 0:64], tv.ap[:, 0:256].rearrange("p (h d) -> p h d", d=64)), [tv], [Vx])
            self.act(lambda e: e.activation(out=gov[:, i, :], in_=tv.ap[:, 256:512], func=ACT.Sigmoid), [tv], [go])
        self.proj_tm(l, uT, uTv, O_MV, 512, evac_vo, wvo, brep)
        wg = self.alloc(8 * 16, BF16)
        brep2 = self.alloc(16)

        def evac_g(i, psap, pbuf, br):
            self.dve(lambda e: e.tensor_tensor(Gv[:, i, :], psap, br.ap, ALU.add), [pbuf, br], [G])
        self.proj_tm(l, uT, uTv, O_MIF, 16, evac_g, wg, brep2)
        self.release(mC)
        G4 = G.ap.rearrange("p (i a b) -> p i a b", a=4, b=4)
        LF = self.alloc(NT * 8)
        LFv = LF.ap.rearrange("p (i d h) -> p i d h", d=2, h=4)
        CS = self.alloc(NT * 8)
        CSv = CS.ap.rearrange("p (i d h) -> p i d h", d=2, h=4)
        DI = self.alloc(NT * 8)
        DIv = DI.ap.rearrange("p (i d h) -> p i d h", d=2, h=4)
        for d in range(2):
            self.act(lambda e, d=d: e.activation(out=LFv[:, :, d, :], in_=G4[:, :, 1 + 2 * d, :], func=ACT.Exp, scale=-1.0), [G], [LF])
        self.act(lambda e: e.activation(out=LF.ap, in_=LF.ap, func=ACT.Ln, bias=1.0), [LF], [LF])
        self.dve(lambda e: e.tensor_scalar(LF.ap, LF.ap, -1.0, None, ALU.mult), [LF], [LF])
        csp = self.ps[:, 0, 0:NT * 8].rearrange("p (i d h) -> p i d h", d=2, h=4)
        for i in range(NT):
            for d in range(2):
                self.mm(csp[:, i, d, :], self.tri[d].ap, LFv[:, i, d, :], True, True, [self.tri[d], LF], [self.pb[0]])
        self.dve(lambda e: e.tensor_copy(CS.ap, self.ps[:, 0, 0:NT * 8]), [self.pb[0]], [CS])
        for d in range(2):
            self.dve(lambda e, d=d: e.tensor_tensor(DIv[:, :, d, :], G4[:, :, 2 * d, :], CSv[:, :, d, :], ALU.subtract), [G, CS], [DI])
        self.debug_dump("LF", LF.ap, [LF], [128, NT * 8])
        self.debug_dump("CS", CS.ap, [CS], [128, NT * 8])
        self.debug_dump("DI", DI.ap, [DI], [128, NT * 8])
        self.debug_dump("Vx", Vx.ap, [Vx], [128, NT * 4 * 65], BF16)
        ktok = self.alloc(NT * 256, BF16)
        ktv = ktok.ap.rearrange("p (i c) -> p i c", c=256)
        for i in range(NT):
            bank = 6 + (i % 2)
            psb = self.ps[:, bank, 0:128].bitcast(BF16)
            for c in range(2):
                self.pe(lambda e, c=c, i=i, psb=psb: e.transpose(psb[:, 128 * c:128 * c + 128], qkv[:, 2 + c, 128 * i:128 * i + 128],
                                                                self.ident_b.ap), [qkT, self.ident_b], [self.pb[bank]])
            self.act(lambda e, i=i, psb=psb: e.copy(ktv[:, i, :], psb), [self.pb[bank]], [ktok])
        self.debug_dump("ktok", ktok.ap, [ktok], [128, NT * 256], BF16)
        hsum = self.alloc(NT * 256)
        hsv = hsum.ap.rearrange("p (i h d) -> p i h d", h=4, d=64)
        Cn = self.alloc(2 * 65)
        Cnv = Cn.ap.rearrange("p (a c) -> p a c", c=65)
        Cnb = self.alloc(2 * 65, BF16)
        Cnbv = Cnb.ap.rearrange("p (a c) -> p a c", c=65)
        frep = [self.alloc(128) for _ in range(2)]
        arg = [self.alloc(128) for _ in range(2)]
        E = [self.alloc(128) for _ in range(2)]
        W = [self.alloc(128, BF16) for _ in range(2)]
        Dq = [self.alloc(128) for _ in range(2)]
        qt = [self.alloc(128, BF16) for _ in range(2)]
        kt = [self.alloc(64, BF16) for _ in range(2)]
        sm = [self.alloc(4) for _ in range(4)]
        htmp = self.alloc(256)
        cnt = 0
        for d in range(2):
            order = list(range(NT)) if d == 0 else [1, 0] + list(range(NT - 1, 1, -1))
            endc = 127 if d == 0 else 0
            self.dve(lambda e: e.memset(Cn.ap, 0.0), [], [Cn])
            self.dve(lambda e: e.memset(Cnb.ap, 0.0), [], [Cnb])
            for it, i in enumerate(order):
                ndb = 4 + (it % 2)
                ndp = self.ps[:, ndb, 0:260].rearrange("p (h c) -> p h c", c=65)
                ts = slice(128 * i, 128 * i + 128)
                for h in range(4):
                    x = cnt % 2
                    cnt += 1
                    cq, ck, pr, pa = h // 2, 2 + h // 2, 64 * (h % 2), h // 2
                    prs = slice(pr, pr + 64)
                    fr, ar, Ee, Ww, Dd, qq, kk = frep[x], arg[x], E[x], W[x], Dq[x], qt[x], kt[x]
                    bb, sb_, ub = x, 2 + x, 6 + x
                    self.pool(lambda e, fr=fr, i=i, d=d, h=h: e.tensor_copy(fr.ap, LFv[:, i, d, h:h + 1].to_broadcast([128, 128])), [LF], [fr])
                    self.mm(self.ps[:, bb, 0:128], fr.ap, self.tri[d].ap, True, True, [fr, self.tri[d]], [self.pb[bb]])
                    self.dve(lambda e, ar=ar, bb=bb, i=i, d=d, h=h: e.scalar_tensor_tensor(ar.ap, self.ps[:, bb, 0:128], DIv[:, i, d, h:h + 1],
                                                                                      self.mneg[d].ap, ALU.add, ALU.add),
                             [self.pb[bb], DI, self.mneg[d]], [ar])
                    self.act(lambda e, Ee=Ee, ar=ar: e.activation(out=Ee.ap, in_=ar.ap, func=ACT.Exp), [ar], [Ee])
                    self.act(lambda e, Dd=Dd, bb=bb, prs=prs: e.activation(out=Dd.ap[prs, :], in_=self.ps[prs, bb, 0:128], func=ACT.Exp),
                             [self.pb[bb]], [Dd])
                    self.pool(lambda e, qq=qq, Dd=Dd, prs=prs, cq=cq, ts=ts: e.tensor_tensor(qq.ap[prs, :], qkv[prs, cq, ts], Dd.ap[prs, :], ALU.mult),
                              [qkT, Dd], [qq])
                    self.mm(self.ps[:, sb_, 0:128], qkv[prs, ck, ts], qkv[prs, cq, ts], True, True, [qkT], [self.pb[sb_]])
                    self.dve(lambda e, Ww=Ww, sb_=sb_, Ee=Ee: e.tensor_tensor(Ww.ap, self.ps[:, sb_, 0:128], Ee.ap, ALU.mult),
                             [self.pb[sb_], Ee], [Ww])
                    self.mm(ndp[:, h, :], Ww.ap, Vxv[:, i, h, :], True, False, [Ww, Vx], [self.pb[ndb]])
                    self.mm(ndp[:, h, :], qq.ap[prs, :], Cnbv[prs, pa, :], False, True, [qq, Cnb], [self.pb[ndb]])
                    self.pool(lambda e, kk=kk, i=i, h=h, Ee=Ee, endc=endc: e.tensor_scalar(kk.ap, ktv[:, i, 64 * h:64 * h + 64], Ee.ap[:, endc:endc + 1], None, ALU.mult),
                              [ktok, Ee], [kk])
                    self.mm(self.ps[prs, ub, 0:65], kk.ap, Vxv[:, i, h, :], True, True, [kk, Vx], [self.pb[ub]])
                    self.dve(lambda e, prs=prs, pa=pa, Dd=Dd, ub=ub, endc=endc: e.scalar_tensor_tensor(Cnv[prs, pa, :], Cnv[prs, pa, :], Dd.ap[prs, endc:endc + 1],
                                                                                         self.ps[prs, ub, 0:65], ALU.mult, ALU.add),
                             [Cn, Dd, self.pb[ub]], [Cn])
                    self.act(lambda e, prs=prs, pa=pa: e.copy(Cnbv[prs, pa, :], Cnv[prs, pa, :]), [Cn], [Cnb])
                den = ndp[:, :, 64]
                self.dve(lambda e, den=den: e.tensor_scalar(sm[0].ap, den, -1.0, None, ALU.mult), [self.pb[ndb]], [sm[0]])
                self.dve(lambda e, den=den: e.tensor_tensor(sm[1].ap, den, sm[0].ap, ALU.max), [self.pb[ndb], sm[0]], [sm[1]])
                self.dve(lambda e: e.tensor_scalar(sm[2].ap, sm[1].ap, 1.0, None, ALU.max), [sm[1]], [sm[2]])
                self.dve(lambda e: e.reciprocal(sm[3].ap, sm[2].ap), [sm[2]], [sm[3]])
                rb = sm[3].ap.unsqueeze(2).to_broadcast([128, 4, 64])
                if d == 0:
                    self.dve(lambda e, i=i, ndp=ndp, rb=rb: e.tensor_tensor(hsv[:, i, :, :], ndp[:, :, 0:64], rb, ALU.mult),
                             [self.pb[ndb], sm[3]], [hsum])
                else:
                    hv3 = htmp.ap.rearrange("p (h d) -> p h d", d=64)
                    self.dve(lambda e, ndp=ndp, rb=rb, hv3=hv3: e.tensor_tensor(hv3, ndp[:, :, 0:64], rb, ALU.mult),
                             [self.pb[ndb], sm[3]], [htmp])
                    self.pool(lambda e, i=i, hv3=hv3: e.tensor_tensor(hsv[:, i, :, :], hsv[:, i, :, :], hv3, ALU.add), [htmp, hsum], [hsum])
        self.debug_dump("hsum", hsum.ap, [hsum], [128, NT * 256])
        gn = self.alloc(256)
        self.load_rep(gn, I["mlstm_norm_g"][l])
        self.head_norm_out(hsum, gn, go, yT, yTv, 0)
        self.release(mA)

    def mix_gla(self, l, s, uT, uTv, yT, yTv):
        I = self.din
        mA = self.mark()
        gqk = self.alloc(4 * T, BF16, parts=64)
        gqv = gqk.ap.rearrange("p (c t) -> p c t", t=T)
        lrT = [self.alloc(T, parts=16) for _ in range(2)]
        Vg = self.alloc(NT * 256, BF16)
        Vgv = Vg.ap.rearrange("p (i c) -> p i c", c=256)
        gr = self.alloc(NT * 256, BF16)
        grv = gr.ap.rearrange("p (i c) -> p i c", c=256)
        osum = self.alloc(NT * 256)
        osv = osum.ap.rearrange("p (i h d) -> p i h d", h=4, d=64)
        mB = self.mark()
        wq = [self.alloc(8 * 64, BF16), self.alloc(1)]
        gf = self.alloc(T, parts=64)
        permG = self.alloc(64, parts=64)
        self.dma(permG.ap, I["permG"], w=[permG])
        rt = [[self.alloc(512, parts=64), self.alloc(512, parts=64)] for _ in range(2)]
        ra = [self.alloc(512, parts=64) for _ in range(2)]
        for c in range(4):
            c0 = (O_GQ if c < 2 else O_GK) + 64 * (c % 2)
            sc = (32.0 ** -0.5) if c < 2 else 1.0

            def evac(tc, t0, n, psap, pbuf, bcol, sc=sc):
                self.act(lambda e: e.activation(out=gf.ap[:, t0:t0 + n], in_=psap, func=ACT.Identity, bias=bcol.ap[0:64, 0:1], scale=sc),
                         [pbuf, bcol], [gf])
            self.proj_fm(l, s, uTv, uT, c0, 64, evac, wq, bias_scale=(sc if c < 2 else None))
            self.pool(lambda e, c=c: e.tensor_copy(gqv[:, c, 0:CTX], gf.ap[:, 0:CTX]), [gf], [gqk])
            for j in range(4):
                r0, r1 = rt[j % 2]
                self.dma(r0.ap, I["ropeG"][0, :, 512 * j:512 * j + 512], w=[r0])
                self.dma(r1.ap, I["ropeG"][1, :, 512 * j:512 * j + 512], w=[r1], q="pool")
                bank = 2 + (j % 2)
                lat = gf.ap[:, CTX + 512 * j:CTX + 512 * j + 512]
                self.mm(self.ps[0:64, bank, :], permG.ap, lat, True, True, [permG, gf], [self.pb[bank]])
                a_ = ra[j % 2]
                self.dve(lambda e, a_=a_, r0=r0, lat=lat: e.tensor_tensor(a_.ap, lat, r0.ap, ALU.mult), [gf, r0], [a_])
                self.dve(lambda e, r1=r1, bank=bank: e.tensor_tensor(r1.ap, self.ps[0:64, bank, :], r1.ap, ALU.mult),
                         [self.pb[bank], r1], [r1])
                self.pool(lambda e, a_=a_, r1=r1, j=j, c=c: e.tensor_tensor(gqv[:, c, CTX + 512 * j:CTX + 512 * j + 512], a_.ap, r1.ap, ALU.add),
                          [a_, r1], [gqk])
        wl = [self.alloc(8 * 16, BF16), self.alloc(1)]
        for d in range(2):
            def evac2(tc, t0, n, psap, pbuf, bcol, d=d):
                self.act(lambda e: e.activation(out=lrT[d].ap[:, t0:t0 + n], in_=psap, func=ACT.Identity, bias=bcol.ap[0:16, 0:1], scale=1.0),
                         [pbuf, bcol], [lrT[d]])
            self.proj_fm(l, s, uTv, uT, O_LRF + 16 * d, 16, evac2, wl)
        self.release(mB)
        self.debug_dump("gqk", gqk.ap, [gqk], [64, 4 * T], BF16)
        mC = self.mark()
        wvr = self.alloc(8 * 512, BF16)
        brep = self.alloc(512)
        tmpv = [self.alloc(512) for _ in range(2)]

        def evac_vr(i, psap, pbuf, br):
            tv = tmpv[i % 2]
            self.dve(lambda e: e.tensor_tensor(tv.ap, psap, br.ap, ALU.add), [pbuf, br], [tv])
            self.pool(lambda e: e.tensor_copy(Vgv[:, i, :], tv.ap[:, 0:256]), [tv], [Vg])
            self.act(lambda e: e.activation(out=grv[:, i, :], in_=tv.ap[:, 256:512], func=ACT.Silu), [tv], [gr])
        self.proj_tm(l, uT, uTv, O_GV, 512, evac_vr, wvr, brep)
        self.release(mC)
        mD = self.mark()
        w2 = self.alloc(2 * 128, parts=16)
        self.dma(w2.ap.rearrange("p (d c) -> p d c", c=128), I["gla_w2"][l].rearrange("d r c -> r d c"), w=[w2])
        nb2 = self.alloc(4, parts=64)
        for d in range(2):
            for pa in range(2):
                self.dma(nb2.ap[:, 2 * d + pa:2 * d + pa + 1], I["gla_b2"][l, d, 64 * pa:64 * pa + 64].rearrange("(c o) -> c o", o=1), w=[nb2])
        self.dve(lambda e: e.tensor_scalar(nb2.ap, nb2.ap, -1.0, None, ALU.mult), [nb2], [nb2])
        smask = self.alloc(T, parts=64)
        self.dma(smask.ap, I["scanmask"], w=[smask])
        aT = self.alloc(T, parts=64)
        cs = self.alloc(T, parts=64)
        eb = self.alloc(T, parts=64)
        qtl = self.alloc(T, BF16, parts=64)
        ktl = self.alloc(T, BF16, parts=64)
        ktk = self.alloc(NT * 64, BF16)
        ktkv = ktk.ap.rearrange("p (i c) -> p i c", c=64)
        ebe = self.alloc(NT, parts=64)
        Sst = self.alloc(64, parts=64)
        Sb = self.alloc(64, BF16, parts=64)
        Am = [self.alloc(128, BF16) for _ in range(2)]
        cnt = 0
        for d in range(2):
            order = list(range(NT)) if d == 0 else [1, 0] + list(range(NT - 1, 1, -1))
            endc = 127 if d == 0 else 0
            for pa in range(2):
                for tc in range(5):
                    t0 = 512 * tc
                    n = min(512, T - t0)
                    bank = tc % 2
                    self.mm(self.ps[0:64, bank, 0:n], w2.ap[:, 128 * d + 64 * pa:128 * d + 64 * pa + 64], lrT[d].ap[:, t0:t0 + n], True, True,
                            [w2, lrT[d]], [self.pb[bank]])
                    self.act(lambda e, t0=t0, n=n, bank=bank, d=d, pa=pa: e.activation(out=aT.ap[:, t0:t0 + n], in_=self.ps[0:64, bank, 0:n], func=ACT.Exp,
                                                                                      bias=nb2.ap[:, 2 * d + pa:2 * d + pa + 1], scale=-1.0),
                             [self.pb[bank], nb2], [aT])
                self.act(lambda e: e.activation(out=aT.ap, in_=aT.ap, func=ACT.Ln, bias=1.0), [aT], [aT])
                self.dve(lambda e: e.tensor_scalar(aT.ap, aT.ap, -1.0 / 16.0, None, ALU.mult), [aT], [aT])
                self.dve(lambda e: e.tensor_tensor_scan(cs.ap, smask.ap, aT.ap, 0.0, ALU.mult, ALU.add), [smask, aT], [cs])
                if d == 1:
                    a3 = aT.ap.rearrange("p (i t) -> p i t", t=128)
                    c3 = cs.ap.rearrange("p (i t) -> p i t", t=128)
                    e3 = eb.ap.rearrange("p (i t) -> p i t", t=128)
                    self.dve(lambda e, a3=a3, c3=c3: e.tensor_tensor(a3, a3, c3, ALU.subtract), [aT, cs], [aT])
                    self.dve(lambda e, a3=a3, c3=c3, e3=e3: e.tensor_tensor(e3, a3, c3[:, :, 127:128].to_broadcast([64, NT, 128]), ALU.add), [aT, cs], [eb])
                    self.dve(lambda e: e.tensor_copy(cs.ap, eb.ap), [eb], [cs])
                self.act(lambda e: e.activation(out=eb.ap, in_=cs.ap, func=ACT.Exp), [cs], [eb])
                self.pool(lambda e, pa=pa: e.tensor_tensor(qtl.ap, gqv[:, pa, :], eb.ap, ALU.mult), [gqk, eb], [qtl])
                self.dve(lambda e, endc=endc: e.tensor_copy(ebe.ap, eb.ap.rearrange("p (i t) -> p i t", t=128)[:, :, endc]), [eb], [ebe])
                self.act(lambda e: e.activation(out=aT.ap, in_=cs.ap, func=ACT.Exp, scale=-1.0), [cs], [aT])
                self.pool(lambda e, pa=pa: e.tensor_tensor(ktl.ap, gqv[:, 2 + pa, :], aT.ap, ALU.mult), [gqk, aT], [ktl])
                for i in range(NT):
                    bank = 6 + (i % 2)
                    psb = self.ps[:, bank, 0:32].bitcast(BF16)
                    self.pe(lambda e, i=i, psb=psb: e.transpose(psb, ktl.ap[:, 128 * i:128 * i + 128], self.ident_b.ap[0:64, 0:64]),
                            [ktl, self.ident_b], [self.pb[bank]])
                    self.act(lambda e, i=i, psb=psb: e.copy(ktkv[:, i, :], psb), [self.pb[bank]], [ktk])
                self.dve(lambda e: e.memset(Sst.ap, 0.0), [], [Sst])
                self.dve(lambda e: e.memset(Sb.ap, 0.0), [], [Sb])
                for it, i in enumerate(order):
                    ob = 4 + (it % 2)
                    ub = 2 + (it % 2)
                    op_ = self.ps[:, ob, 0:128].rearrange("p (h c) -> p h c", c=64)
                    ts = slice(128 * i, 128 * i + 128)
                    for hh in range(2):
                        h = 2 * pa + hh
                        prs = slice(32 * hh, 32 * hh + 32)
                        x = cnt % 2
                        cnt += 1
                        self.mm(self.ps[:, x, 0:128], ktl.ap[prs, ts], qtl.ap[prs, ts], True, True, [ktl, qtl], [self.pb[x]])
                        self.dve(lambda e, x=x, d=d: e.tensor_tensor(Am[x].ap, self.ps[:, x, 0:128], self.m01b[d].ap, ALU.mult),
                                 [self.pb[x], self.m01b[d]], [Am[x]])
                        self.mm(op_[:, hh, :], Am[x].ap, Vgv[:, i, 64 * h:64 * h + 64], True, False, [Am[x], Vg], [self.pb[ob]])
                        self.mm(op_[:, hh, :], qtl.ap[prs, ts], Sb.ap[prs, :], False, True, [qtl, Sb], [self.pb[ob]])
                        self.mm(self.ps[prs, ub, 0:64], ktkv[:, i, 32 * hh:32 * hh + 32], Vgv[:, i, 64 * h:64 * h + 64], True, True, [ktk, Vg], [self.pb[ub]])
                    self.dve(lambda e, i=i: e.tensor_scalar(Sst.ap, Sst.ap, ebe.ap[:, i:i + 1], None, ALU.mult), [Sst, ebe], [Sst])
                    self.dve(lambda e, i=i, ub=ub: e.scalar_tensor_tensor(Sst.ap, self.ps[0:64, ub, 0:64], ebe.ap[:, i:i + 1], Sst.ap, ALU.mult, ALU.add),
                             [self.pb[ub], ebe, Sst], [Sst])
                    self.act(lambda e: e.copy(Sb.ap, Sst.ap), [Sst], [Sb])
                    if d == 0:
                        self.act(lambda e, i=i, pa=pa, op_=op_: e.copy(osv[:, i, 2 * pa:2 * pa + 2, :], op_), [self.pb[ob]], [osum])
                    else:
                        self.dve(lambda e, i=i, pa=pa, op_=op_: e.tensor_tensor(osv[:, i, 2 * pa:2 * pa + 2, :], op_, osv[:, i, 2 * pa:2 * pa + 2, :], ALU.add),
                                 [self.pb[ob], osum], [osum])
        self.release(mD)
        self.debug_dump("osum", osum.ap, [osum], [128, NT * 256])
        gn = self.alloc(256)
        self.load_rep(gn, I["gla_norm_g"][l])
        self.head_norm_out(osum, gn, gr, yT, yTv, 2)
        self.release(mA)

    def mix_nat(self, l, s, uT, uTv, yT, yTv):
        I = self.din
        for pa in range(4):
            mA = self.mark()
            nq = self.alloc(T, BF16)
            nk = self.alloc(T, BF16)
            Vn = self.alloc(NT * 2 * 65, BF16)
            Vnv = Vn.ap.rearrange("p (i h d) -> p i h d", h=2, d=65)
            yn = self.alloc(NT * 128, BF16)
            ynv = yn.ap.rearrange("p (i c) -> p i c", c=128)
            wq = [self.alloc(8 * 128, BF16), self.alloc(1)]
            wv = self.alloc(8 * 128, BF16)
            brep = self.alloc(128)
            for which, dst, sc in ((0, nq, 0.125), (1, nk, 1.0)):
                c0 = (O_NQ if which == 0 else O_NK) + 128 * pa

                def evac(tc, t0, n, psap, pbuf, bcol, dst=dst, sc=sc):
                    self.act(lambda e: e.activation(out=dst.ap[:, t0:t0 + n], in_=psap, func=ACT.Identity, bias=bcol.ap[:, 0:1], scale=sc),
                             [pbuf, bcol], [dst])
                self.proj_fm(l, s, uTv, uT, c0, 128, evac, wq, bias_scale=(sc if which == 0 else None))
            self.pool(lambda e: e.memset(Vnv[:, :, :, 64:65], 1.0), [], [Vn])

            def evac_v(i, psap, pbuf, br):
                self.dve(lambda e: e.tensor_tensor(Vnv[:, i, :, 0:64], psap.rearrange("p (h d) -> p h d", d=64),
                                                    br.ap.rearrange("p (h d) -> p h d", d=64), ALU.add), [pbuf, br], [Vn])
            self.proj_tm(l, uT, uTv, O_NV + 128 * pa, 128, evac_v, wv, brep)
            tb = [self.alloc(8 * 512) for _ in range(2)]
            arg = [self.alloc(512) for _ in range(2)]
            Ee = [self.alloc(512, BF16) for _ in range(20)]
            rd = [self.alloc(4) for _ in range(2)]
            cnt = 0
            tcnt = 0
            ocnt = 0
            for hh in range(2):
                h = 2 * pa + hh
                prs = slice(64 * hh, 64 * hh + 64)
                blocks = [(0, [0]), (1, [1, 2]), (2, [3]), (3, [-1])]
                for p, qbs in blocks:
                    if p < 3:
                        tbt = tb[tcnt % 2]
                        tcnt += 1
                        self.dma(tbt.ap, I["nbias"][l, h, p].rearrange("p j q -> p (j q)"), w=[tbt], q=("sp" if tcnt % 2 else "pool"))
                        tbv = tbt.ap.rearrange("p (j q) -> p j q", q=512)
                    for qb in qbs:
                        if qb >= 0:
                            q0, nqk, nqs = CTX + 512 * qb, 512, 4
                            kt0 = (0, 4 * qb - 2, 8)[p]
                            keys = [2 + kt0 + j for j in range(8)] + [0, 1]
                            nloc = 8
                            ot0 = 2 + 4 * qb
                        else:
                            q0, nqk, nqs = 0, 256, 2
                            keys = [0, 1]
                            nloc = 0
                            ot0 = 0
                        ob = 4 + (ocnt % 2)
                        ocnt += 1
                        opv = self.ps[:, ob, 0:65 * nqs].rearrange("p (a c) -> p a c", c=65)
                        ebase = 10 * (ocnt % 2)
                        for j, ti in enumerate(keys):
                            sbk = cnt % 4
                            e_ = Ee[ebase + j]
                            a_ = arg[cnt % 2]
                            cnt += 1
                            self.mm(self.ps[:, sbk, 0:nqk], nk.ap[prs, 128 * ti:128 * ti + 128], nq.ap[prs, q0:q0 + nqk], True, True,
                                    [nk, nq], [self.pb[sbk]])
                            if j < nloc:
                                self.dve(lambda e, a_=a_, sbk=sbk, tbv=tbv, j=j: e.tensor_tensor(a_.ap, self.ps[:, sbk, :], tbv[:, j, :], ALU.add),
                                         [self.pb[sbk], tbt], [a_])
                                self.act(lambda e, e_=e_, a_=a_: e.activation(out=e_.ap, in_=a_.ap, func=ACT.Exp), [a_], [e_])
                            else:
                                self.act(lambda e, e_=e_, sbk=sbk, nqk=nqk: e.activation(out=e_.ap[:, 0:nqk], in_=self.ps[:, sbk, 0:nqk], func=ACT.Exp),
                                         [self.pb[sbk]], [e_])
                        for qs in range(nqs):
                            for j, ti in enumerate(keys):
                                e_ = Ee[ebase + j]
                                self.mm(opv[:, qs, :], e_.ap[:, 128 * qs:128 * qs + 128], Vnv[:, ti, hh, :], j == 0, j == len(keys) - 1,
                                        [e_, Vn], [self.pb[ob]])
                        r_ = rd[ocnt % 2]
                        self.dve(lambda e, r_=r_, opv=opv, nqs=nqs: e.reciprocal(r_.ap[:, 0:nqs], opv[:, :, 64]), [self.pb[ob]], [r_])
                        self.dve(lambda e, r_=r_, opv=opv, nqs=nqs, ot0=ot0, hh=hh: e.tensor_tensor(
                            ynv[:, ot0:ot0 + nqs, 64 * hh:64 * hh + 64], opv[:, :, 0:64],
                            r_.ap[:, 0:nqs].unsqueeze(2).to_broadcast([128, nqs, 64]), ALU.mult), [self.pb[ob], r_], [yn])
            for i in range(NT):
                bank = 6 + (i % 2)
                psb = self.ps[:, bank, 0:64].bitcast(BF16)
                self.pe(lambda e, i=i, psb=psb: e.transpose(psb, ynv[:, i, :], self.ident_b.ap), [yn, self.ident_b], [self.pb[bank]])
                self.act(lambda e, i=i, psb=psb, pa=pa: e.copy(yTv[:, 4 + pa, 128 * i:128 * i + 128], psb), [self.pb[bank]], [yT])
            self.release(mA)

    def mix_out(self, l, s, yT, yTv):
        I = self.din
        mA = self.mark()
        wout = self.alloc(8 * D, BF16)
        self.dma(wout.ap.rearrange("p (k n) -> p k n", n=D), I["w_out"][l].rearrange("(k p) n -> p k n", p=128), w=[wout], q="pool")
        wr = self.alloc(8 * NE)
        self.dma(wr.ap.rearrange("p (k e) -> p k e", e=NE), I["w_router"][l].rearrange("(k p) e -> p k e", p=128), w=[wr])
        rp = {}
        for nm, off in (("g1", 2 * D), ("sh2", 3 * D), ("sc2", 4 * D)):
            for tag, row in (("L", s), ("C", 4)):
                rp[nm + tag] = self.alloc(D)
                self.load_rep(rp[nm + tag], self.modv[row, off:off + D], [self.modvb])
        lng = self.alloc(D)
        lnb = self.alloc(D)
        self.load_rep(lng, I["ln1_g"][l])
        self.load_rep(lnb, I["ln1_b"][l])
        xt = [self.alloc(D) for _ in range(2)]
        t1 = [self.alloc(D) for _ in range(2)]
        x1 = [self.alloc(D) for _ in range(2)]
        u2 = [self.alloc(D) for _ in range(2)]
        u2h = [self.alloc(D, BF16) for _ in range(2)]
        u2T = [self.alloc(8 * 128) for _ in range(2)]
        afT = self.alloc(T, parts=16)
        sm = [[self.alloc(12), self.alloc(2), self.alloc(1), self.alloc(1), self.alloc(1), self.alloc(1), self.alloc(1), self.alloc(NE), self.alloc(NE)]
              for _ in range(2)]
        for i in range(NT):
            x = i % 2
            tg = "C" if i < 2 else "L"
            stt, mv, std, rstd, mx, ssum, rs, ex, aff = sm[x]
            b0 = 2 * x
            for half in range(2):
                for k in range(8):
                    self.mm(self.ps[:, b0 + half, :], yTv[:, k, 128 * i:128 * i + 128], wout.ap[:, k * D + 512 * half:k * D + 512 * half + 512],
                            k == 0, k == 7, [yT, wout], [self.pb[b0 + half]])
            src, sb = self.x_src(l, s, i)
            self.dma(xt[x].ap, src, r=[sb], w=[xt[x]])
            opv = self.ps[:, b0:b0 + 2, :].rearrange("p a n -> p (a n)")
            self.dve(lambda e, x=x, opv=opv, tg=tg: e.tensor_tensor(t1[x].ap, opv, rp["g1" + tg].ap, ALU.mult),
                     [self.pb[b0], self.pb[b0 + 1], rp["g1" + tg]], [t1[x]])
            self.dve(lambda e, x=x: e.scalar_tensor_tensor(t1[x].ap, xt[x].ap, ALPHA, t1[x].ap, ALU.mult, ALU.add), [xt[x], t1[x]], [t1[x]])
            self.layer_norm(t1[x], x1[x], lng, lnb, stt, mv, std, rstd)
            self.dma(self.xs[s, 128 * i:128 * i + 128, :], x1[x].ap, r=[x1[x]], w=[self.xsb[s]])
            self.dve(lambda e, x=x, tg=tg: e.tensor_tensor(u2[x].ap, x1[x].ap, rp["sc2" + tg].ap, ALU.mult), [x1[x], rp["sc2" + tg]], [u2[x]])
            self.pool(lambda e, x=x, tg=tg: e.tensor_tensor(u2[x].ap, u2[x].ap, rp["sh2" + tg].ap, ALU.add), [u2[x], rp["sh2" + tg]], [u2[x]])
            self.act(lambda e, x=x: e.copy(u2h[x].ap, u2[x].ap), [u2[x]], [u2h[x]])
            self.dma(self.u2b[s, 128 * i:128 * i + 128, :], u2h[x].ap, r=[u2h[x]], w=[self.u2bb[s]], q="pool")
            tb0 = 4
            for k in range(8):
                self.pe(lambda e, k=k, x=x: e.transpose(self.ps[:, tb0 + k // 4, 128 * (k % 4):128 * (k % 4) + 128], u2[x].ap[:, 128 * k:128 * k + 128],
                                                        self.ident_f.ap), [u2[x], self.ident_f], [self.pb[tb0 + k // 4]])
            self.act(lambda e, x=x: e.copy(u2T[x].ap, self.ps[:, tb0:tb0 + 2, :].rearrange("p a n -> p (a n)")),
                     [self.pb[tb0], self.pb[tb0 + 1]], [u2T[x]])
            for k in range(8):
                self.mm(self.ps[:, 6, 0:NE], u2T[x].ap[:, 128 * k:128 * k + 128], wr.ap[:, NE * k:NE * k + NE], k == 0, k == 7,
                        [u2T[x], wr], [self.pb[6]])
            self.dve(lambda e, mx=mx: e.reduce_max(mx.ap, self.ps[:, 6, 0:NE], axis=AX.X), [self.pb[6]], [mx])
            self.dve(lambda e, mx=mx: e.tensor_scalar(mx.ap, mx.ap, -1.0, None, ALU.mult), [mx], [mx])
            self.act(lambda e, ex=ex, mx=mx, ssum=ssum: e.activation(out=ex.ap, in_=self.ps[:, 6, 0:NE], func=ACT.Exp, bias=mx.ap, scale=1.0,
                                                                   accum_out=ssum.ap), [self.pb[6], mx], [ex, ssum])
            self.dve(lambda e, rs=rs, ssum=ssum: e.reciprocal(rs.ap, ssum.ap), [ssum], [rs])
            self.dve(lambda e, aff=aff, ex=ex, rs=rs: e.tensor_scalar(aff.ap, ex.ap, rs.ap, None, ALU.mult), [ex, rs], [aff])
            self.pe(lambda e, aff=aff: e.transpose(self.ps[0:NE, 7, 0:128], aff.ap, self.ident_f.ap), [aff, self.ident_f], [self.pb[7]])
            self.act(lambda e, i=i: e.copy(afT.ap[:, 128 * i:128 * i + 128], self.ps[0:NE, 7, 0:128]), [self.pb[7]], [afT])
        self.dma(self.affd[NE * s:NE * s + NE, :], afT.ap, r=[afT], w=[self.affdb])
        self.release(mA)

    def moe(self, l, last):
        I = self.din
        NS = self.NS
        R = NS * NE
        N = NS * CAPT
        mL = self.mark()
        idxf = self.alloc(CAPT, parts=R)
        gatef = self.alloc(CAPT, parts=R)
        idxT = self.alloc(3 * R)
        idxTv = idxT.ap.rearrange("p (c r) -> p c r", r=R)
        gateT = self.alloc(3 * R)
        gateTv = gateT.ap.rearrange("p (c r) -> p c r", r=R)
        tcol = self.alloc(NT)
        for i in range(NT):
            self.dve(lambda e, i=i: e.tensor_scalar(tcol.ap[:, i:i + 1], self.pidx.ap, float(128 * i), None, ALU.add), [self.pidx], [tcol])
        m = self.mark()
        wk = self.alloc(T, parts=R)
        self.dma(wk.ap, self.affd[0:R, :], r=[self.affdb], w=[wk])
        idxu = self.alloc(CAPT, U32, parts=R)
        mx8 = self.alloc(8, parts=R)
        for part, (lo, hi, o0, nit) in enumerate(((CTX, T, 0, CAP // 8), (0, CTX, CAP, CAPC // 8))):
            for it in range(nit):
                o = o0 + 8 * it
                self.dve(lambda e, lo=lo, hi=hi, o=o: e.max(out=gatef.ap[:, o:o + 8], in_=wk.ap[:, lo:hi]), [wk], [gatef])
                self.dve(lambda e, lo=lo, hi=hi, o=o: e.max_index(out=idxu.ap[:, o:o + 8], in_max=gatef.ap[:, o:o + 8], in_values=wk.ap[:, lo:hi]),
                         [wk, gatef], [idxu])
                self.dve(lambda e, lo=lo, hi=hi, o=o: e.match_replace(out=wk.ap[:, lo:hi], in_to_replace=gatef.ap[:, o:o + 8], in_values=wk.ap[:, lo:hi],
                                                                      imm_value=-1.0), [wk, gatef], [wk])
        self.dve(lambda e: e.tensor_copy(idxf.ap, idxu.ap), [idxu], [idxf])
        self.dve(lambda e: e.tensor_scalar(idxf.ap[:, 0:CAP], idxf.ap[:, 0:CAP], float(CTX), None, ALU.add), [idxf], [idxf])
        for src, dstv, dst in ((idxf, idxTv, idxT), (gatef, gateTv, gateT)):
            for c, (c0, n) in enumerate(((0, 128), (128, 128), (256, 32))):
                self.pe(lambda e, src=src, c=c, c0=c0, n=n: e.transpose(self.ps[0:n, 7, 64 * c:64 * c + R], src.ap[:, c0:c0 + n], self.ident_f.ap[0:R, 0:R]),
                        [src, self.ident_f], [self.pb[7]])
            self.dve(lambda e, dst=dst: e.memset(dst.ap, 0.0), [], [dst])
            for c, n in enumerate((128, 128, 32)):
                self.dve(lambda e, dstv=dstv, c=c, n=n: e.tensor_copy(dstv[0:n, c, :], self.ps[0:n, 7, 64 * c:64 * c + R]), [self.pb[7]], [dst])
        self.release(m)
        self.debug_dump("idxf", idxf.ap, [idxf], [R, CAPT])
        self.debug_dump("gatef", gatef.ap, [gatef], [R, CAPT])
        m = self.mark()
        U = self.alloc(NT * D, BF16)
        Uv = U.ap.rearrange("p (i d) -> p i d", d=D)
        sel = [self.alloc(128, parts=R) for _ in range(2)]
        PT = [self.alloc(16 * CAP, BF16) for _ in range(2)]
        PTc = [self.alloc(2 * CAPC, BF16) for _ in range(2)]
        xeS = [self.alloc(8 * CAPT, BF16) for _ in range(2)]
        for s in range(NS):
            for i in range(NT):
                self.dma(Uv[:, i, :], self.u2b[s, 128 * i:128 * i + 128, :], r=[self.u2bb[s]], w=[U], q=("sp" if i % 2 else "pool"))
            for e_ in range(NE):
                r = NE * s + e_
                x = e_ % 2
                self.dve(lambda e, x=x, r=r: e.tensor_scalar(sel[x].ap, self.pidx.ap[0:R, 0:1].to_broadcast([R, 128]), float(r), None, ALU.is_equal),
                         [self.pidx], [sel[x]])
                self.mm(self.ps[:, 6, 0:CAPT], sel[x].ap, idxf.ap, True, True, [sel[x], idxf], [self.pb[6]])
                ptv = PT[x].ap.rearrange("p (i c) -> p i c", c=CAP)
                pcv = PTc[x].ap.rearrange("p (i c) -> p i c", c=CAPC)
                for i in range(2, NT):
                    self.dve(lambda e, i=i, ptv=ptv: e.tensor_scalar(ptv[:, i - 2, :], self.ps[:, 6, 0:CAP], tcol.ap[:, i:i + 1], None, ALU.is_equal),
                             [self.pb[6], tcol], [PT[x]])
                for i in range(2):
                    self.dve(lambda e, i=i, pcv=pcv: e.tensor_scalar(pcv[:, i, :], self.ps[:, 6, CAP:CAPT], tcol.ap[:, i:i + 1], None, ALU.is_equal),
                             [self.pb[6], tcol], [PTc[x]])
                xv = xeS[x].ap.rearrange("p (k c) -> p k c", c=CAPT)
                for dk in range(8):
                    bank = dk % 4
                    for i in range(2, NT):
                        self.mm(self.ps[:, bank, 0:CAP], Uv[:, i, 128 * dk:128 * dk + 128], ptv[:, i - 2, :], i == 2, i == NT - 1, [U, PT[x]], [self.pb[bank]])
                    for i in range(2):
                        self.mm(self.ps[:, bank, CAP:CAPT], Uv[:, i, 128 * dk:128 * dk + 128], pcv[:, i, :], i == 0, i == 1, [U, PTc[x]], [self.pb[bank]])
                    self.act(lambda e, xv=xv, dk=dk, bank=bank: e.copy(xv[:, dk, :], self.ps[:, bank, 0:CAPT]), [self.pb[bank]], [xeS[x]])
                self.dma(self.xeT[e_][:, :, CAPT * s:CAPT * s + CAPT].rearrange("k p c -> p k c"), xv, r=[xeS[x]], w=[self.xeTb[e_]])
        self.release(m)
        m = self.mark()
        nchunk = (N + 511) // 512
        cw = N // nchunk
        xin = [self.alloc(8 * N, BF16) for _ in range(2)]
        wgu = [[self.alloc(8 * 512, BF16) for _ in range(2)] for _ in range(2)]
        wd = [self.alloc(16 * D, BF16) for _ in range(2)]
        hT = self.alloc(16 * N, BF16)
        hTv = hT.ap.rearrange("p (j n) -> p j n", n=N)
        sg = [self.alloc(cw) for _ in range(2)]
        yo = [self.alloc(D, BF16) for _ in range(2)]
        cnt = 0
        ycnt = 0
        for e_ in range(NE):
            xi = xin[e_ % 2]
            xiv = xi.ap.rearrange("p (k n) -> p k n", n=N)
            self.dma(xiv, self.xeT[e_].rearrange("k p n -> p k n"), r=[self.xeTb[e_]], w=[xi])
            wdt = wd[e_ % 2]
            wdv = wdt.ap.rearrange("p (j d) -> p j d", d=D)
            for jb in range(4):
                self.dma(wdv[:, 4 * jb:4 * jb + 4, :], I["w_down"][l, e_, 512 * jb:512 * jb + 512, :].rearrange("(j p) d -> p j d", p=128), w=[wdt], q="pool")
            for fb in range(4):
                wts = []
                for mi, nm in enumerate(("w_gate", "w_up")):
                    wt = wgu[mi][fb % 2]
                    self.dma(wt.ap.rearrange("p (k f) -> p k f", f=512), I[nm][l, e_, :, 512 * fb:512 * fb + 512].rearrange("(k p) f -> p k f", p=128),
                             w=[wt], q="pool")
                    wts.append(wt)
                for fc in range(4):
                    j = 4 * fb + fc
                    for ng in range(nchunk):
                        n0 = cw * ng
                        gb, ubk = (cnt % 2), 2 + (cnt % 2)
                        s_ = sg[cnt % 2]
                        cnt += 1
                        for k in range(8):
                            self.mm(self.ps[:, gb, 0:cw], wts[0].ap[:, 512 * k + 128 * fc:512 * k + 128 * fc + 128], xiv[:, k, n0:n0 + cw], k == 0, k == 7,
                                    [wts[0], xi], [self.pb[gb]])
                        for k in range(8):
                            self.mm(self.ps[:, ubk, 0:cw], wts[1].ap[:, 512 * k + 128 * fc:512 * k + 128 * fc + 128], xiv[:, k, n0:n0 + cw], k == 0, k == 7,
                                    [wts[1], xi], [self.pb[ubk]])
                        self.act(lambda e, s_=s_, gb=gb: e.activation(out=s_.ap, in_=self.ps[:, gb, 0:cw], func=ACT.Silu), [self.pb[gb]], [s_])
                        self.dve(lambda e, s_=s_, ubk=ubk, j=j, n0=n0: e.tensor_tensor(hTv[:, j, n0:n0 + cw], self.ps[:, ubk, 0:cw], s_.ap, ALU.mult),
                                 [self.pb[ubk], s_], [hT])
            for s in range(NS):
                r = NE * s + e_
                for c, (c0, n) in enumerate(((0, 128), (128, 128), (256, 32))):
                    y_ = yo[ycnt % 2]
                    for half in range(2):
                        bank = 4 + (ycnt % 2) * 2 + half
                        for j in range(16):
                            self.mm(self.ps[0:n, bank, :], hTv[:, j, CAPT * s + c0:CAPT * s + c0 + n], wdv[:, j, 512 * half:512 * half + 512], j == 0, j == 15,
                                    [hT, wdt], [self.pb[bank]])
                        self.act(lambda e, y_=y_, n=n, bank=bank, half=half, c=c, r=r: e.activation(out=y_.ap[0:n, 512 * half:512 * half + 512], in_=self.ps[0:n, bank, :],
                                                                                               func=ACT.Copy, scale=gateTv[0:n, c, r:r + 1]),
                                 [self.pb[bank], gateT], [y_])
                    ycnt += 1
                    self.dma(self.ye[s, e_, c0:c0 + n, :], y_.ap[0:n, :], r=[y_], w=[self.yeb[s][e_]])
        self.release(m)
        m = self.mark()
        YL = self.alloc(NE * 2 * D, BF16)
        YLv = YL.ap.rearrange("p (e a d) -> p e a d", a=2, d=D)
        YC = self.alloc(NE * D, BF16, parts=32)
        YCv = YC.ap.rearrange("p (e d) -> p e d", d=D)
        iot = self.alloc(T)
        self.dma(iot.ap, I["iota_t"], w=[iot])
        Pm = [self.alloc(32 * 128, BF16) for _ in range(2)]
        Pc = [self.alloc(16 * 128, BF16, parts=32) for _ in range(2)]
        lng = self.alloc(D)
        lnb = self.alloc(D)
        self.load_rep(lng, I["ln2_g"][l])
        self.load_rep(lnb, I["ln2_b"][l])
        g2 = {"L": self.alloc(D), "C": self.alloc(D)}
        xt = [self.alloc(D) for _ in range(2)]
        t1 = [self.alloc(D) for _ in range(2)]
        x2 = [self.alloc(D) for _ in range(2)]
        sm = [[self.alloc(12), self.alloc(2), self.alloc(1), self.alloc(1)] for _ in range(2)]
        for s in range(NS):
            for e_ in range(NE):
                self.dma(YLv[:, e_, :, :], self.ye[s, e_, 0:CAP, :].rearrange("(a p) d -> p a d", p=128), r=[self.yeb[s][e_]], w=[YL],
                         q=("sp" if e_ % 2 else "pool"))
                self.dma(YCv[:, e_, :], self.ye[s, e_, CAP:CAPT, :], r=[self.yeb[s][e_]], w=[YC])
            self.load_rep(g2["L"], self.modv[s, 5 * D:6 * D], [self.modvb])
            self.load_rep(g2["C"], self.modv[4, 5 * D:6 * D], [self.modvb])
            idxL = idxTv[:, 0:2, NE * s:NE * s + NE].rearrange("p a e -> p e a")
            idxC = idxTv[0:32, 2, NE * s:NE * s + NE]
            for i in (range(2, NT) if last else range(NT)):
                x = i % 2
                tg = "C" if i < 2 else "L"
                stt, mv, std, rstd = sm[x]
                b0 = 2 * x
                if i >= 2:
                    pv = Pm[x].ap.rearrange("p (e a t) -> p e a t", a=2, t=128)
                    self.dve(lambda e, pv=pv, i=i, idxL=idxL: e.tensor_tensor(
                        pv, iot.ap[:, 128 * i:128 * i + 128].unsqueeze(1).unsqueeze(1).to_broadcast([128, NE, 2, 128]),
                        idxL.unsqueeze(3).to_broadcast([128, NE, 2, 128]), ALU.is_equal), [iot, idxT], [Pm[x]])
                    for half in range(2):
                        for e_ in range(NE):
                            for a in range(2):
                                self.mm(self.ps[:, b0 + half, :], pv[:, e_, a, :], YLv[:, e_, a, 512 * half:512 * half + 512],
                                        (e_ == 0 and a == 0), (e_ == NE - 1 and a == 1), [Pm[x], YL], [self.pb[b0 + half]])
                else:
                    pv = Pc[x].ap.rearrange("p (e t) -> p e t", t=128)
                    self.dve(lambda e, pv=pv, i=i, idxC=idxC: e.tensor_tensor(
                        pv, iot.ap[0:32, 128 * i:128 * i + 128].unsqueeze(1).to_broadcast([32, NE, 128]),
                        idxC.unsqueeze(2).to_broadcast([32, NE, 128]), ALU.is_equal), [iot, idxT], [Pc[x]])
                    for half in range(2):
                        for e_ in range(NE):
                            self.mm(self.ps[:, b0 + half, :], pv[:, e_, :], YCv[:, e_, 512 * half:512 * half + 512],
                                    e_ == 0, e_ == NE - 1, [Pc[x], YC], [self.pb[b0 + half]])
                self.dma(xt[x].ap, self.xs[s, 128 * i:128 * i + 128, :], r=[self.xsb[s]], w=[xt[x]])
                opv = self.ps[:, b0:b0 + 2, :].rearrange("p a n -> p (a n)")
                if "fmoe" in self.dbgn and s == 0:
                    self.dve(lambda e, x=x, opv=opv: e.tensor_copy(t1[x].ap, opv), [self.pb[b0], self.pb[b0 + 1]], [t1[x]])
                    self.dma(self.dbg_f[128 * i:128 * i + 128, :], t1[x].ap, r=[t1[x]])
                self.dve(lambda e, x=x, opv=opv, tg=tg: e.tensor_tensor(t1[x].ap, opv, g2[tg].ap, ALU.mult),
                         [self.pb[b0], self.pb[b0 + 1], g2[tg]], [t1[x]])
                self.dve(lambda e, x=x: e.scalar_tensor_tensor(t1[x].ap, xt[x].ap, ALPHA, t1[x].ap, ALU.mult, ALU.add), [xt[x], t1[x]], [t1[x]])
                self.layer_norm(t1[x], x2[x], lng, lnb, stt, mv, std, rstd)
                if last:
                    self.dma(self.out[s, 128 * (i - 2):128 * (i - 2) + 128, :], x2[x].ap, r=[x2[x]])
                else:
                    self.dma(self.xs[s, 128 * i:128 * i + 128, :], x2[x].ap, r=[x2[x]], w=[self.xsb[s]])
        self.release(m)
        self.release(mL)

    def layer_norm(self, src, dst, lng, lnb, stt, mv, std, rstd):
        st3 = stt.ap.rearrange("p (c f) -> p c f", f=6)
        for c in range(2):
            self.dve(lambda e, c=c: e.bn_stats(st3[:, c, :], src.ap[:, 512 * c:512 * c + 512]), [src], [stt])
        self.dve(lambda e: e.bn_aggr(mv.ap, st3), [stt], [mv])
        self.act(lambda e: e.activation(out=std.ap, in_=mv.ap[:, 1:2], func=ACT.Sqrt, bias=self.epsc.ap, scale=1.0), [mv, self.epsc], [std])
        self.dve(lambda e: e.reciprocal(rstd.ap, std.ap), [std], [rstd])
        self.dve(lambda e: e.tensor_scalar(src.ap, src.ap, mv.ap[:, 0:1], rstd.ap, ALU.subtract, ALU.mult), [src, mv, rstd], [src])
        self.pool(lambda e: e.tensor_tensor(dst.ap, src.ap, lng.ap, ALU.mult), [src, lng], [dst])
        self.pool(lambda e: e.tensor_tensor(dst.ap, dst.ap, lnb.ap, ALU.add), [dst, lnb], [dst])

    def head_norm_out(self, hsum, gn, gate, yT, yTv, ych):
        hsv = hsum.ap.rearrange("p (i h d) -> p i h d", h=4, d=64)
        gtv = gate.ap.rearrange("p (i c) -> p i c", c=256)
        m = self.mark()
        sq = [self.alloc(256) for _ in range(2)]
        st = [[self.alloc(4) for _ in range(6)] for _ in range(2)]
        tt = [self.alloc(256) for _ in range(2)]
        ym = [self.alloc(256, BF16) for _ in range(2)]
        for i in range(NT):
            x = i % 2
            s1, s2, mean, var, std, rstd = st[x]
            hv = hsv[:, i, :, :]
            sq3 = sq[x].ap.rearrange("p (h d) -> p h d", d=64)
            t3 = tt[x].ap.rearrange("p (h d) -> p h d", d=64)
            self.dve(lambda e, hv=hv, s1=s1: e.reduce_sum(s1.ap, hv, axis=AX.X), [hsum], [s1])
            self.pool(lambda e, hv=hv, sq3=sq3: e.tensor_tensor(sq3, hv, hv, ALU.mult), [hsum], [sq[x]])
            self.dve(lambda e, sq3=sq3, s2=s2: e.reduce_sum(s2.ap, sq3, axis=AX.X), [sq[x]], [s2])
            self.dve(lambda e, s1=s1, mean=mean: e.tensor_scalar(mean.ap, s1.ap, 1.0 / 64, None, ALU.mult), [s1], [mean])
            self.dve(lambda e, mean=mean, var=var: e.tensor_tensor(var.ap, mean.ap, mean.ap, ALU.mult), [mean], [var])
            self.dve(lambda e, s2=s2, var=var: e.scalar_tensor_tensor(var.ap, s2.ap, 1.0 / 64, var.ap, ALU.mult, ALU.subtract), [s2, var], [var])
            self.act(lambda e, var=var, std=std: e.activation(out=std.ap, in_=var.ap, func=ACT.Sqrt, bias=self.epsc.ap, scale=1.0),
                     [var, self.epsc], [std])
            self.dve(lambda e, std=std, rstd=rstd: e.reciprocal(rstd.ap, std.ap), [std], [rstd])
            mb_ = mean.ap.unsqueeze(2).to_broadcast([128, 4, 64])
            rb_ = rstd.ap.unsqueeze(2).to_broadcast([128, 4, 64])
            self.dve(lambda e, hv=hv, t3=t3, mb_=mb_: e.tensor_tensor(t3, hv, mb_, ALU.subtract), [hsum, mean], [tt[x]])
            self.dve(lambda e, t3=t3, rb_=rb_: e.tensor_tensor(t3, t3, rb_, ALU.mult), [tt[x], rstd], [tt[x]])
            self.pool(lambda e, x=x: e.tensor_tensor(tt[x].ap, tt[x].ap, gn.ap, ALU.mult), [tt[x], gn], [tt[x]])
            self.pool(lambda e, x=x, i=i: e.tensor_tensor(ym[x].ap, tt[x].ap, gtv[:, i, :], ALU.mult), [tt[x], gate], [ym[x]])
            bank = 6 + x
            psb = self.ps[:, bank, 0:128].bitcast(BF16)
            for c in range(2):
                self.pe(lambda e, c=c, x=x, psb=psb: e.transpose(psb[:, 128 * c:128 * c + 128], ym[x].ap[:, 128 * c:128 * c + 128], self.ident_b.ap),
                        [ym[x], self.ident_b], [self.pb[bank]])
            self.act(lambda e, i=i, psb=psb: e.copy(yTv[:, ych:ych + 2, 128 * i:128 * i + 128], psb.rearrange("p (c t) -> p c t", t=128)),
                     [self.pb[bank]], [yT])
        self.release(m)

    def proj_tm(self, l, uT, uTv, c0, ncols, evac_fn, wt, brep):
        I = self.din
        self.dma(wt.ap[:, 0:8 * ncols].rearrange("p (k c) -> p k c", c=ncols),
                 I["w_in"][l, :, c0:c0 + ncols].rearrange("(k p) c -> p k c", p=128), w=[wt], q="pool")
        self.load_rep(brep, I["b_in"][l, c0:c0 + ncols])
        for i in range(NT):
            bank = i % 2
            for k in range(8):
                self.mm(self.ps[:, bank, 0:ncols], uTv[:, k, 128 * i:128 * i + 128], wt.ap[:, k * ncols:(k + 1) * ncols],
                        k == 0, k == 7, [uT, wt], [self.pb[bank]])
            evac_fn(i, self.ps[:, bank, 0:ncols], self.pb[bank], brep)


N_CORES = 8
NS_CORE = 4
DEPTH = 4
_W_NAMES = ("w_mod", "b_mod", "w_in", "b_in", "conv_w", "gla_w2", "gla_b2", "mlstm_norm_g", "gla_norm_g",
            "w_out", "ln1_g", "ln1_b", "w_router", "w_gate", "w_up", "w_down", "ln2_g", "ln2_b")


def build_program(NS=NS_CORE, NL=DEPTH):
    B = Builder(NS, NL)
    B.consts()
    for l in range(NL):
        B.layer_mod(l)
        for s in range(NS):
            B.mixer(l, s)
        B.moe(l, l == NL - 1)
    B.S.final_wait("sp")
    B.S.emit()
    return B


def kernel(**inputs):
    f32 = lambda a: np.ascontiguousarray(np.asarray(a), dtype=np.float32)
    x = f32(inputs["x"])
    c = f32(inputs["c"])
    ctx = f32(inputs["ctx"])
    c_ctx = f32(inputs["c_ctx"])
    shared = {k: f32(inputs[k]) for k in _W_NAMES}
    shared["nbias"] = natten_tables(f32(inputs["rpb"]))
    shared.update(host_consts())
    B = build_program()
    in_maps = []
    for ci in range(N_CORES):
        sl = slice(NS_CORE * ci, NS_CORE * ci + NS_CORE)
        m = dict(shared)
        m["x"] = x[sl]
        m["ctx"] = ctx[sl]
        m["cc"] = np.concatenate([c[sl], c_ctx[None, :]], 0)
        in_maps.append(m)
    res = run_bass_kernel_spmd(B.nc, in_maps, core_ids=list(range(N_CORES)))
    return np.concatenate([np.asarray(r["out"]) for r in res.results], axis=0).astype(np.float32)
```

```python
import contextlib
import numpy as np
import ml_dtypes
import concourse.bass as bass
import concourse.mybir as mybir
from concourse.bass_utils import run_bass_kernel_spmd

F32 = mybir.dt.float32
BF16 = mybir.dt.bfloat16
U32 = mybir.dt.uint32
ALU = mybir.AluOpType
ACT = mybir.ActivationFunctionType
AX = mybir.AxisListType

D = 1024
SEQ = 2048
CTX = 256
T = SEQ + CTX
NT = T // 128
NE = 16
FF = 2048
CAP = 256
CAPC = 32
CAPT = CAP + CAPC
PROJ = 3376
ALPHA = 8.0 ** 0.25
EPS = 1e-5
NEG = -30000.0
O_MQ, O_MK, O_MV, O_MO = 0, 256, 512, 768
O_MIF, O_MFF, O_MIB, O_MFB = 1024, 1028, 1032, 1036
O_GQ, O_GK, O_GV, O_GR = 1040, 1168, 1296, 1552
O_LRF, O_LRB = 1808, 1824
O_NQ, O_NK, O_NV = 1840, 2352, 2864


class Buf:
    __slots__ = ("name", "w", "r")

    def __init__(self, name=""):
        self.name = name
        self.w = None
        self.r = {}


class Sched:
    ENG = ("pe", "act", "dve", "pool", "sp")
    NK = 64

    def __init__(self, nc, n_dma_sems=14):
        self.nc = nc
        self.ops = {e: [] for e in self.ENG}
        self.seq = {e: 0 for e in self.ENG}
        self.kidx = {}
        self.known = {e: np.zeros(self.NK, np.int64) for e in self.ENG}
        self.snap = {}
        self.n_dma_sems = n_dma_sems
        self.dma_ring = {e: 0 for e in self.ENG}
        self.dma_val = {}
        self.semkeys = set()

    def _ki(self, sk):
        i = self.kidx.get(sk)
        if i is None:
            i = self.kidx[sk] = len(self.kidx)
            assert i < self.NK
        return i

    def _need(self, eng, tok, waits):
        if tok is None:
            return
        sk, val = tok
        kn = self.known[eng]
        if eng == "pe" and sk == ("c", "pe"):
            return
        i = self._ki(sk)
        if kn[i] >= val:
            return
        kn[i] = val
        sn = self.snap.get(tok)
        if sn is not None:
            np.maximum(kn, sn, out=kn)
        waits.append((sk, val))

    def _deps(self, eng, reads, writes):
        waits = []
        for b in reads:
            self._need(eng, b.w, waits)
        for b in writes:
            self._need(eng, b.w, waits)
            for sk, val in b.r.items():
                self._need(eng, (sk, val), waits)
        best = {}
        for sk, val in waits:
            if best.get(sk, 0) < val:
                best[sk] = val
        return list(best.items())

    def _mark(self, tok, reads, writes):
        sk, val = tok
        for b in reads:
            if b.r.get(sk, 0) < val:
                b.r[sk] = val
        for b in writes:
            b.w = tok
            b.r = {}

    def op(self, eng, fn, reads=(), writes=()):
        waits = self._deps(eng, reads, writes)
        self.seq[eng] += 1
        sk = ("c", eng)
        self.semkeys.add(sk)
        tok = (sk, self.seq[eng])
        self.snap[tok] = self.known[eng].copy()
        self.ops[eng].append((waits, fn, (sk, 1)))
        self._mark(tok, reads, writes)
        return tok

    def dma(self, eng, out, in_, reads=(), writes=(), **kw):
        slot = self.dma_ring[eng]
        self.dma_ring[eng] = (slot + 1) % self.n_dma_sems
        sk = ("d", eng, slot)
        self.semkeys.add(sk)
        waits = self._deps(eng, reads, writes)
        prev = self.dma_val.get(sk, 0)
        i = self._ki(sk)
        if prev > 0 and self.known[eng][i] < prev:
            self.known[eng][i] = prev
            waits = [w for w in waits if w[0] != sk] + [(sk, prev)]
        val = prev + 16
        self.dma_val[sk] = val
        tok = (sk, val)
        self.snap[tok] = self.known[eng].copy()

        def fn(e, out=out, in_=in_, kw=kw):
            return e.dma_start(out=out, in_=in_, **kw)
        self.ops[eng].append((waits, fn, (sk, 16)))
        self._mark(tok, reads, writes)
        return tok

    def _all_tokens(self):
        toks = [(("c", e), self.seq[e]) for e in self.ENG if self.seq[e] > 0]
        toks += [(sk, v) for sk, v in self.dma_val.items()]
        return toks

    def barrier(self):
        toks = self._all_tokens()
        for e in self.ENG:
            waits = []
            for tok in toks:
                self._need(e, tok, waits)
            if waits:
                self.ops[e].append((waits, None, None))

    def final_wait(self, eng="sp"):
        waits = []
        for sk, val in self._all_tokens():
            i = self._ki(sk)
            if self.known[eng][i] < val:
                self.known[eng][i] = val
                waits.append((sk, val))
        self.ops[eng].append((waits, None, None))

    def emit(self):
        nc = self.nc
        with contextlib.ExitStack() as st:
            sems = {}
            for sk in sorted(self.semkeys, key=str):
                sems[sk] = st.enter_context(nc.semaphore("s_" + "_".join(str(x) for x in sk)))
            block = st.enter_context(nc.Block())

            def run(e, lst):
                for waits, fn, inc in lst:
                    for sk, val in waits:
                        e.wait_ge(sems[sk], val)
                    if fn is not None:
                        fn(e).then_inc(sems[inc[0]], inc[1])

            @block.tensor
            def _(e):
                run(e, self.ops["pe"])

            @block.scalar
            def _(e):
                run(e, self.ops["act"])

            @block.vector
            def _(e):
                run(e, self.ops["dve"])

            @block.gpsimd
            def _(e):
                run(e, self.ops["pool"])

            @block.sync
            def _(e):
                run(e, self.ops["sp"])


class Tl:
    __slots__ = ("ap", "b")

    def __init__(self, ap, b):
        self.ap = ap
        self.b = b


def _bf(x):
    return np.ascontiguousarray(x).astype(ml_dtypes.bfloat16)


def host_consts():
    c = {}
    c["ident_f"] = np.eye(128, dtype=np.float32)
    s = np.arange(128)[:, None]
    t = np.arange(128)[None, :]
    c["tri"] = np.stack([(s <= t), (s >= t)]).astype(np.float32)
    c["mneg"] = np.where(c["tri"] > 0, 0.0, NEG).astype(np.float32)
    tt = np.arange(SEQ)
    rows = (tt // 64).astype(np.float32)
    cols = (tt % 64).astype(np.float32)

    def rope_tab(dh, reps):
        h = dh // 2
        d2 = h // 2
        inv = (10000.0 ** (-np.arange(d2, dtype=np.float32) / d2)).astype(np.float32)
        C = np.zeros((dh, SEQ), np.float32)
        Sg = np.zeros((dh, SEQ), np.float32)
        P = np.zeros((dh, dh), np.float32)
        for d in range(dh):
            blk = d // h
            j = (d % h) % d2
            half = (d % h) // d2
            pos = rows if blk == 0 else cols
            ang = (pos * inv[j]).astype(np.float32)
            C[d] = np.cos(ang)
            Sg[d] = np.sin(ang) * (-1.0 if half == 0 else 1.0)
            partner = blk * h + (1 - half) * d2 + j
            P[partner, d] = 1.0
        Cr = np.tile(C, (reps, 1))
        Sr = np.tile(Sg, (reps, 1))
        Pr = np.kron(np.eye(reps, dtype=np.float32), P)
        return np.stack([Cr, Sr]).astype(np.float32), Pr.astype(np.float32)
    c["ropeM"], c["permM"] = rope_tab(64, 2)
    c["ropeG"], c["permG"] = rope_tab(32, 2)
    c["iota_t"] = np.tile(np.arange(T, dtype=np.float32)[None, :], (128, 1))
    c["pidx"] = np.arange(128, dtype=np.float32)[:, None].copy()
    mk = np.ones((64, T), np.float32)
    mk[:, ::128] = 0.0
    c["scanmask"] = mk
    c["sel64"] = np.zeros((64, 64, 128), np.float32)
    for r in range(64):
        c["sel64"][r, r, :] = 1.0
    return c


def natten_tables(rpb):
    L = rpb.shape[0]
    krl = np.arange(2)[:, None, None, None, None]
    ck = np.arange(64)[None, :, None, None, None]
    j = np.arange(8)[None, None, :, None, None]
    qrl = np.arange(8)[None, None, None, :, None]
    cq = np.arange(64)[None, None, None, None, :]
    out = np.empty((L, 8, 3, 128, 8, 512), np.float32)
    for p, (q0, k0) in enumerate(((0, 0), (8, 4), (24, 16))):
        qr = q0 + qrl
        kr = k0 + 2 * j + krl
        rs = np.clip(qr - 4, 0, 24)
        cs = np.clip(cq - 8, 0, 48)
        valid = (kr >= rs) & (kr < rs + 8) & (ck >= cs) & (ck < cs + 16)
        dr = np.clip(kr - qr + 7, 0, 14)
        dc = np.clip(ck - cq + 15, 0, 30)
        valid, dr, dc = np.broadcast_arrays(valid, dr, dc)
        g = rpb[:, :, dr, dc]
        g = np.where(valid[None, None], g, np.float32(NEG))
        out[:, :, p] = g.reshape(L, 8, 128, 8, 512)
    return out


class Builder:
    def __init__(self, NS, NL, stop=None, dbg=(), moe=True, nat=True):
        self.NS, self.NL, self.stop, self.dbgn = NS, NL, stop, set(dbg)
        nc = self.nc = bass.Bass("TRN2", target_bir_lowering=False)
        self.S = Sched(nc)
        self.st = contextlib.ExitStack()
        di = lambda name, shape, dt=F32: nc.dram_tensor(name, list(shape), dt, kind="ExternalInput").ap()
        self.din = {}
        I = self.din
        I["x"] = di("x", [NS, SEQ, D])
        I["ctx"] = di("ctx", [NS, CTX, D])
        I["cc"] = di("cc", [5, D])
        for name, shp in (("w_mod", [NL, D, 6 * D]), ("b_mod", [NL, 6 * D]), ("w_in", [NL, D, PROJ]),
                          ("b_in", [NL, PROJ]), ("conv_w", [NL, 3, 512]), ("gla_w2", [NL, 2, 16, 128]),
                          ("gla_b2", [NL, 2, 128]), ("mlstm_norm_g", [NL, 256]), ("gla_norm_g", [NL, 256]),
                          ("w_out", [NL, D, D]), ("ln1_g", [NL, D]), ("ln1_b", [NL, D]),
                          ("w_router", [NL, D, NE]), ("w_gate", [NL, NE, D, FF] if moe else [1, 1, 8, 8]),
                          ("w_up", [NL, NE, D, FF] if moe else [1, 1, 8, 8]),
                          ("w_down", [NL, NE, FF, D] if moe else [1, 1, 8, 8]), ("ln2_g", [NL, D]), ("ln2_b", [NL, D]),
                          ("nbias", [NL, 8, 3, 128, 8, 512] if nat else [1, 1, 1, 8, 1, 8]),
                          ("ident_f", [128, 128]), ("tri", [2, 128, 128]), ("mneg", [2, 128, 128]),
                          ("ropeM", [2, 128, SEQ]), ("permM", [128, 128]), ("ropeG", [2, 64, SEQ]),
                          ("permG", [64, 64]), ("iota_t", [128, T]), ("pidx", [128, 1]),
                          ("sel64", [64, 64, 128]), ("scanmask", [64, T])):
            I[name] = di(name, shp)
        self.out = nc.dram_tensor("out", [NS, SEQ, D], F32, kind="ExternalOutput").ap()
        ds = lambda name, shape, dt=F32: nc.dram_tensor(name, list(shape), dt, kind="Internal").ap()
        self.xs = ds("xs", [NS, T, D])
        self.xsb = [Buf("xs%d" % s) for s in range(NS)]
        self.modv = ds("modv", [5, 6 * D])
        self.modvb = Buf("modv")
        self.u2b = ds("u2b", [NS, T, D], BF16)
        self.u2bb = [Buf() for _ in range(NS)]
        self.affd = ds("affd", [NS * NE, T])
        self.affdb = Buf("affd")
        self.xeT = ds("xeT", [NE, 8, 128, NS * CAPT], BF16)
        self.xeTb = [Buf() for _ in range(NE)]
        self.ye = ds("ye", [NS, NE, CAPT, D], BF16)
        self.yeb = [[Buf() for _ in range(NE)] for _ in range(NS)]
        self.dbg = {}
        if 'fmoe' in self.dbgn:
            self.dbg_f = nc.dram_tensor('dbg_fmoe', [T, D], F32, kind='ExternalOutput').ap()
        self.ARENA = 51200
        self.arena = self.st.enter_context(nc.sbuf_tensor("arena", [128, self.ARENA], F32))
        self.ps = self.st.enter_context(nc.psum_tensor("ps", [128, 8, 512], F32))
        self.pb = [Buf("ps%d" % i) for i in range(8)]
        self.pq = [[Buf() for _ in range(4)] for _ in range(8)]
        self.col = 0
        self.inb = {k: Buf(k) for k in I}

    def alloc(self, n, dt=F32, parts=128):
        nb = n * (2 if dt == BF16 else 4)
        ncol = (nb + 3) // 4
        assert self.col + ncol <= self.ARENA, ("SBUF arena overflow", self.col, ncol)
        ap = self.arena[0:parts, self.col:self.col + ncol]
        self.col += ncol
        if dt != F32:
            ap = ap.bitcast(dt)
        return Tl(ap, Buf())

    def mark(self):
        return self.col

    def release(self, m):
        self.S.barrier()
        self.col = m

    def _bufs(self, lst):
        return [x.b if isinstance(x, Tl) else x for x in lst]

    def dve(self, fn, r=(), w=()):
        return self.S.op("dve", fn, self._bufs(r), self._bufs(w))

    def act(self, fn, r=(), w=()):
        return self.S.op("act", fn, self._bufs(r), self._bufs(w))

    def pool(self, fn, r=(), w=()):
        return self.S.op("pool", fn, self._bufs(r), self._bufs(w))

    def pe(self, fn, r=(), w=()):
        return self.S.op("pe", fn, self._bufs(r), self._bufs(w))

    def dma(self, out, in_, r=(), w=(), q="sp", **kw):
        return self.S.dma(q, out, in_, self._bufs(r), self._bufs(w), **kw)

    def mm(self, out, lhsT, rhs, start, stop, r, w):
        return self.pe(lambda e: e.matmul(out, lhsT=lhsT, rhs=rhs, start=start, stop=stop), r, w)

    def debug_dump(self, name, ap, bufs, shape, dt=F32):
        if name not in self.dbgn:
            return
        d = self.nc.dram_tensor("dbg_" + name, list(shape), dt, kind="ExternalOutput").ap()
        self.dbg[name] = d
        self.dma(d, ap, r=bufs)

    def consts(self):
        I = self.din
        self.ident_f = self.alloc(128)
        self.dma(self.ident_f.ap, I["ident_f"], w=[self.ident_f])
        self.ident_b = self.alloc(128, BF16)
        self.dve(lambda e: e.tensor_copy(self.ident_b.ap, self.ident_f.ap), [self.ident_f], [self.ident_b])
        self.tri = [self.alloc(128), self.alloc(128)]
        self.mneg = [self.alloc(128), self.alloc(128)]
        self.m01b = [self.alloc(128, BF16), self.alloc(128, BF16)]
        for d in range(2):
            self.dma(self.tri[d].ap, I["tri"][d], w=[self.tri[d]])
            self.dma(self.mneg[d].ap, I["mneg"][d], w=[self.mneg[d]])
            self.dve(lambda e, d=d: e.tensor_copy(self.m01b[d].ap, self.tri[d].ap), [self.tri[d]], [self.m01b[d]])
        self.pidx = self.alloc(1)
        self.dma(self.pidx.ap, I["pidx"], w=[self.pidx])
        self.epsc = self.alloc(1)
        self.dve(lambda e: e.memset(self.epsc.ap, EPS), [], [self.epsc])
        self.ccT = self.alloc(8 * 5)
        keep = self.mark()
        cc = self.alloc(D, parts=5)
        self.dma(cc.ap, I["cc"], w=[cc])
        self.act(lambda e: e.activation(out=cc.ap, in_=cc.ap, func=ACT.Silu), [cc], [cc])
        pb = self.pb[0]
        for k in range(8):
            self.pe(lambda e, k=k: e.transpose(self.ps[:, 0, 8 * k:8 * k + 5], cc.ap[:, 128 * k:128 * k + 128],
                                                self.ident_f.ap[0:5, 0:5]), [cc, self.ident_f], [pb])
        self.dve(lambda e: e.tensor_copy(self.ccT.ap.rearrange("p (k c) -> p k c", c=5),
                                          self.ps[:, 0, 0:64].rearrange("p (k c) -> p k c", c=8)[:, :, 0:5]),
                 [pb], [self.ccT])
        self.release(keep)

    def layer_mod(self, l):
        I = self.din
        m = self.mark()
        wts = [self.alloc(8 * 512) for _ in range(2)]
        bms = [self.alloc(512, parts=5) for _ in range(2)]
        ots = [self.alloc(512, parts=5) for _ in range(2)]
        wv = I["w_mod"][l].rearrange("(k p) n -> p k n", p=128)
        for n in range(12):
            wt, bm, ot = wts[n % 2], bms[n % 2], ots[n % 2]
            pb = self.pb[n % 2]
            self.dma(wt.ap.rearrange("p (k n) -> p k n", n=512), wv[:, :, 512 * n:512 * n + 512], w=[wt],
                     q=("sp" if n % 2 == 0 else "pool"))
            self.dma(bm.ap, I["b_mod"][l, 512 * n:512 * n + 512].partition_broadcast(5), w=[bm])
            for k in range(8):
                self.mm(self.ps[0:5, n % 2, :], self.ccT.ap[:, 5 * k:5 * k + 5], wt.ap[:, 512 * k:512 * k + 512],
                        k == 0, k == 7, [self.ccT, wt], [pb])
            self.dve(lambda e, ot=ot, bm=bm, n=n: e.tensor_tensor(ot.ap, self.ps[0:5, n % 2, :], bm.ap, ALU.add),
                     [pb, bm], [ot])
            if n in (2, 3, 8, 9):
                self.dve(lambda e, ot=ot: e.tensor_scalar(ot.ap, ot.ap, 1.0, None, ALU.add), [ot], [ot])
            self.dma(self.modv[:, 512 * n:512 * n + 512], ot.ap, r=[ot], w=[self.modvb])
        self.release(m)

    def load_rep(self, tl, row_ap, extra_r=()):
        self.dma(tl.ap, row_ap.partition_broadcast(128), r=list(extra_r), w=[tl])

    def x_src(self, l, s, i):
        if l == 0:
            if i < 2:
                return self.din["ctx"][s, 128 * i:128 * i + 128, :], self.inb["ctx"]
            return self.din["x"][s, 128 * (i - 2):128 * (i - 2) + 128, :], self.inb["x"]
        return self.xs[s, 128 * i:128 * i + 128, :], self.xsb[s]

    def mixer(self, l, s):
        I = self.din
        S = self.S
        m0 = self.mark()
        uT = self.alloc(8 * T, BF16)
        uTv = uT.ap.rearrange("p (k t) -> p k t", t=T)
        yT = self.alloc(8 * T, BF16)
        yTv = yT.ap.rearrange("p (k t) -> p k t", t=T)
        m1 = self.mark()
        reps = {}
        for nm, row, off in (("scL", s, D), ("shL", s, 0), ("scC", 4, D), ("shC", 4, 0)):
            reps[nm] = self.alloc(D)
            self.load_rep(reps[nm], self.modv[row, off:off + D], [self.modvb])
        xt = [self.alloc(D) for _ in range(2)]
        tm = [self.alloc(D) for _ in range(2)]
        ub = [self.alloc(D, BF16) for _ in range(2)]
        for i in range(NT):
            x_, t_, u_ = xt[i % 2], tm[i % 2], ub[i % 2]
            src, sb = self.x_src(l, s, i)
            self.dma(x_.ap, src, r=[sb], w=[x_], q=("sp" if i % 2 == 0 else "pool"))
            sc, sh = (reps["scC"], reps["shC"]) if i < 2 else (reps["scL"], reps["shL"])
            self.dve(lambda e, x_=x_, t_=t_, sc=sc: e.tensor_tensor(t_.ap, x_.ap, sc.ap, ALU.mult), [x_, sc], [t_])
            self.pool(lambda e, t_=t_, u_=u_, sh=sh: e.tensor_tensor(u_.ap, t_.ap, sh.ap, ALU.add), [t_, sh], [u_])
            bank = 6 + (i % 2)
            psb = self.ps[:, bank, :].bitcast(BF16)
            for k in range(8):
                self.pe(lambda e, k=k, u_=u_, psb=psb: e.transpose(psb[:, 128 * k:128 * k + 128],
                                                                  u_.ap[:, 128 * k:128 * k + 128], self.ident_b.ap),
                        [u_, self.ident_b], [self.pb[bank]])
            self.act(lambda e, i=i, psb=psb: e.copy(uTv[:, :, 128 * i:128 * i + 128],
                                                     psb.rearrange("p (k t) -> p k t", t=128)),
                     [self.pb[bank]], [uT])
        self.release(m1)
        self.debug_dump("uT", uT.ap, [uT], [128, 8 * T], BF16)
        if self.stop == "M0":
            self.release(m0)
            return
        self.mix_mlstm(l, s, uT, uTv, yT, yTv)
        if self.stop == "M1":
            self.release(m0)
            return
        self.mix_gla(l, s, uT, uTv, yT, yTv)
        if self.stop == "M2":
            self.release(m0)
            return
        self.mix_nat(l, s, uT, uTv, yT, yTv)
        self.debug_dump("yT", yT.ap, [yT], [128, 8 * T], BF16)
        if self.stop == "M3":
            self.release(m0)
            return
        self.mix_out(l, s, yT, yTv)
        self.release(m0)

    def proj_fm(self, l, s, uTv, uT, c0, ncols, dst_fn, wtiles, bias_scale=None):
        I = self.din
        wt = wtiles[0]
        bcol = wtiles[1]
        self.dma(wt.ap[:, 0:8 * ncols].rearrange("p (k c) -> p k c", c=ncols),
                 I["w_in"][l, :, c0:c0 + ncols].rearrange("(k p) c -> p k c", p=128), w=[wt], q="pool")
        self.dma(bcol.ap[0:ncols, 0:1], I["b_in"][l, c0:c0 + ncols].rearrange("(c o) -> c o", o=1), w=[bcol])
        if bias_scale is not None:
            self.dve(lambda e: e.tensor_scalar(bcol.ap[0:ncols, 0:1], bcol.ap[0:ncols, 0:1], bias_scale, None, ALU.mult),
                     [bcol], [bcol])
        for tc in range(5):
            t0 = 512 * tc
            n = min(512, T - t0)
            bank = tc % 2
            for k in range(8):
                self.mm(self.ps[0:ncols, bank, 0:n], wt.ap[:, k * ncols:(k + 1) * ncols], uTv[:, k, t0:t0 + n],
                        k == 0, k == 7, [wt, uT], [self.pb[bank]])
            dst_fn(tc, t0, n, self.ps[0:ncols, bank, 0:n], self.pb[bank], bcol)

    def mix_mlstm(self, l, s, uT, uTv, yT, yTv):
        I = self.din
        mA = self.mark()
        qkT = self.alloc(4 * T, BF16)
        qkv = qkT.ap.rearrange("p (c t) -> p c t", t=T)
        Vx = self.alloc(NT * 4 * 65, BF16)
        Vxv = Vx.ap.rearrange("p (i h d) -> p i h d", h=4, d=65)
        go = self.alloc(NT * 256, BF16)
        gov = go.ap.rearrange("p (i c) -> p i c", c=256)
        G = self.alloc(NT * 16)
        Gv = G.ap.rearrange("p (i c) -> p i c", c=16)
        mB = self.mark()
        wq = [self.alloc(8 * 128, BF16), self.alloc(1)]
        pfl = self.alloc(SEQ + 2)
        pfc = self.alloc(CTX + 2)
        t1l = self.alloc(SEQ)
        t1c = self.alloc(CTX)
        cw = self.alloc(4 * 3)
        for c in range(4):
            self.dma(cw.ap[:, 3 * c:3 * c + 3], I["conv_w"][l, :, 128 * c:128 * c + 128].rearrange("j p -> p j"), w=[cw],
                     allow_slow_non_contiguous=True)
        permM = self.alloc(128)
        self.dma(permM.ap, I["permM"], w=[permM])
        rt = [[self.alloc(512), self.alloc(512)] for _ in range(2)]
        ra = [self.alloc(512) for _ in range(2)]
        for e_ in (pfl, pfc):
            self.pool(lambda e, e_=e_: e.memset(e_.ap, 0.0), [], [e_])
        for c in range(4):
            def evac(tc, t0, n, psap, pbuf, bcol, c=c):
                if tc == 0:
                    self.act(lambda e: e.activation(out=pfc.ap[:, 1:257], in_=psap[:, 0:256], func=ACT.Identity,
                                                    bias=bcol.ap[:, 0:1], scale=1.0), [pbuf, bcol], [pfc])
                    self.act(lambda e: e.activation(out=pfl.ap[:, 1:257], in_=psap[:, 256:512], func=ACT.Identity,
                                                    bias=bcol.ap[:, 0:1], scale=1.0), [pbuf, bcol], [pfl])
                else:
                    o0 = t0 - 256 + 1
                    self.act(lambda e: e.activation(out=pfl.ap[:, o0:o0 + n], in_=psap, func=ACT.Identity,
                                                    bias=bcol.ap[:, 0:1], scale=1.0), [pbuf, bcol], [pfl])
            self.proj_fm(l, s, uTv, uT, 128 * c, 128, evac, wq)
            for (pf, t1, n) in ((pfl, t1l, SEQ), (pfc, t1c, CTX)):
                self.dve(lambda e, pf=pf, t1=t1, n=n, c=c: e.tensor_scalar(t1.ap, pf.ap[:, 1:n + 1], cw.ap[:, 3 * c + 1:3 * c + 2],
                                                                       None, ALU.mult), [pf, cw], [t1])
                self.dve(lambda e, pf=pf, t1=t1, n=n, c=c: e.scalar_tensor_tensor(t1.ap, pf.ap[:, 0:n], cw.ap[:, 3 * c:3 * c + 1],
                                                                              t1.ap, ALU.mult, ALU.add), [pf, cw, t1], [t1])
                self.dve(lambda e, pf=pf, t1=t1, n=n, c=c: e.scalar_tensor_tensor(t1.ap, pf.ap[:, 2:n + 2], cw.ap[:, 3 * c + 2:3 * c + 3],
                                                                              t1.ap, ALU.mult, ALU.add), [pf, cw, t1], [t1])
                self.act(lambda e, t1=t1: e.activation(out=t1.ap, in_=t1.ap, func=ACT.Silu), [t1], [t1])
                if c >= 2:
                    self.pool(lambda e, t1=t1: e.tensor_scalar(t1.ap, t1.ap, 0.125, None, ALU.mult), [t1], [t1])
            self.pool(lambda e, c=c: e.tensor_copy(qkv[:, c, 0:CTX], t1c.ap), [t1c], [qkT])
            for j in range(4):
                r0, r1 = rt[j % 2]
                self.dma(r0.ap, I["ropeM"][0, :, 512 * j:512 * j + 512], w=[r0])
                self.dma(r1.ap, I["ropeM"][1, :, 512 * j:512 * j + 512], w=[r1], q="pool")
                bank = 2 + (j % 2)
                self.mm(self.ps[:, bank, :], permM.ap, t1l.ap[:, 512 * j:512 * j + 512], True, True, [permM, t1l], [self.pb[bank]])
                a_ = ra[j % 2]
                self.dve(lambda e, a_=a_, r0=r0, j=j: e.tensor_tensor(a_.ap, t1l.ap[:, 512 * j:512 * j + 512], r0.ap, ALU.mult),
                         [t1l, r0], [a_])
                self.dve(lambda e, r1=r1, bank=bank: e.tensor_tensor(r1.ap, self.ps[:, bank, :], r1.ap, ALU.mult),
                         [self.pb[bank], r1], [r1])
                self.pool(lambda e, a_=a_, r1=r1, j=j, c=c: e.tensor_tensor(qkv[:, c, CTX + 512 * j:CTX + 512 * j + 512], a_.ap, r1.ap, ALU.add),
                          [a_, r1], [qkT])
        self.release(mB)
        self.debug_dump("qkT", qkT.ap, [qkT], [128, 4 * T], BF16)
        mC = self.mark()
        wvo = self.alloc(8 * 512, BF16)
        brep = self.alloc(512)
        tmpv = [self.alloc(512) for _ in range(2)]
        self.pool(lambda e: e.memset(Vxv[:, :, :, 64:65], 1.0), [], [Vx])

        def evac_vo(i, psap, pbuf, br):
            tv = tmpv[i % 2]
            self.dve(lambda e: e.tensor_tensor(tv.ap, psap, br.ap, ALU.add), [pbuf, br], [tv])
            self.pool(lambda e: e.tensor_copy(Vxv[:, i, :, 0:64], tv.ap[:, 0:256].rearrange("p (h d) -> p h d", d=64)), [tv], [Vx])
            self.act(lambda e: e.activation(out=gov[:, i, :], in_=tv.ap[:, 256:512], func=ACT.Sigmoid), [tv], [go])
        self.proj_tm(l, uT, uTv, O_MV, 512, evac_vo, wvo, brep)
        wg = self.alloc(8 * 16, BF16)
        brep2 = self.alloc(16)

        def evac_g(i, psap, pbuf, br):
            self.dve(lambda e: e.tensor_tensor(Gv[:, i, :], psap, br.ap, ALU.add), [pbuf, br], [G])
        self.proj_tm(l, uT, uTv, O_MIF, 16, evac_g, wg, brep2)
        self.release(mC)
        G4 = G.ap.rearrange("p (i a b) -> p i a b", a=4, b=4)
        LF = self.alloc(NT * 8)
        LFv = LF.ap.rearrange("p (i d h) -> p i d h", d=2, h=4)
        CS = self.alloc(NT * 8)
        CSv = CS.ap.rearrange("p (i d h) -> p i d h", d=2, h=4)
        DI = self.alloc(NT * 8)
        DIv = DI.ap.rearrange("p (i d h) -> p i d h", d=2, h=4)
        for d in range(2):
            self.act(lambda e, d=d: e.activation(out=LFv[:, :, d, :], in_=G4[:, :, 1 + 2 * d, :], func=ACT.Exp, scale=-1.0), [G], [LF])
        self.act(lambda e: e.activation(out=LF.ap, in_=LF.ap, func=ACT.Ln, bias=1.0), [LF], [LF])
        self.dve(lambda e: e.tensor_scalar(LF.ap, LF.ap, -1.0, None, ALU.mult), [LF], [LF])
        csp = self.ps[:, 0, 0:NT * 8].rearrange("p (i d h) -> p i d h", d=2, h=4)
        for i in range(NT):
            for d in range(2):
                self.mm(csp[:, i, d, :], self.tri[d].ap, LFv[:, i, d, :], True, True, [self.tri[d], LF], [self.pb[0]])
        self.dve(lambda e: e.tensor_copy(CS.ap, self.ps[:, 0, 0:NT * 8]), [self.pb[0]], [CS])
        for d in range(2):
            self.dve(lambda e, d=d: e.tensor_tensor(DIv[:, :, d, :], G4[:, :, 2 * d, :], CSv[:, :, d, :], ALU.subtract), [G, CS], [DI])
        self.debug_dump("LF", LF.ap, [LF], [128, NT * 8])
        self.debug_dump("CS", CS.ap, [CS], [128, NT * 8])
        self.debug_dump("DI", DI.ap, [DI], [128, NT * 8])
        self.debug_dump("Vx", Vx.ap, [Vx], [128, NT * 4 * 65], BF16)
        ktok = self.alloc(NT * 256, BF16)
        ktv = ktok.ap.rearrange("p (i c) -> p i c", c=256)
        for i in range(NT):
            bank = 6 + (i % 2)
            psb = self.ps[:, bank, 0:128].bitcast(BF16)
            for c in range(2):
                self.pe(lambda e, c=c, i=i, psb=psb: e.transpose(psb[:, 128 * c:128 * c + 128], qkv[:, 2 + c, 128 * i:128 * i + 128],
                                                                self.ident_b.ap), [qkT, self.ident_b], [self.pb[bank]])
            self.act(lambda e, i=i, psb=psb: e.copy(ktv[:, i, :], psb), [self.pb[bank]], [ktok])
        self.debug_dump("ktok", ktok.ap, [ktok], [128, NT * 256], BF16)
        hsum = self.alloc(NT * 256)
        hsv = hsum.ap.rearrange("p (i h d) -> p i h d", h=4, d=64)
        Cn = self.alloc(2 * 65)
        Cnv = Cn.ap.rearrange("p (a c) -> p a c", c=65)
        Cnb = self.alloc(2 * 65, BF16)
        Cnbv = Cnb.ap.rearrange("p (a c) -> p a c", c=65)
        frep = [self.alloc(128) for _ in range(4)]
        arg = [self.alloc(128) for _ in range(4)]
        E = [self.alloc(128) for _ in range(4)]
        W = [self.alloc(128, BF16) for _ in range(4)]
        Dq = [self.alloc(128) for _ in range(4)]
        qt = [self.alloc(128, BF16) for _ in range(4)]
        kt = [self.alloc(64, BF16) for _ in range(4)]
        sm = [self.alloc(4) for _ in range(4)]
        htmp = self.alloc(256)
        cnt = 0
        self.S.barrier()
        for d in range(2):
            order = list(range(NT)) if d == 0 else [1, 0] + list(range(NT - 1, 1, -1))
            endc = 127 if d == 0 else 0
            CnB = [Buf() for _ in range(4)]
            CnbB = [Buf() for _ in range(4)]
            self.dve(lambda e: e.memset(Cn.ap, 0.0), [], [Cn] + CnB)
            self.dve(lambda e: e.memset(Cnb.ap, 0.0), [], [Cnb] + CnbB)
            for it, i in enumerate(order):
                ndb = 4 + (it % 2)
                ndp = self.ps[:, ndb, 0:260].rearrange("p (h c) -> p h c", c=65)
                ts = slice(128 * i, 128 * i + 128)
                for h in range(4):
                    x = cnt % 4
                    cnt += 1
                    cq, ck, pr, pa = h // 2, 2 + h // 2, 64 * (h % 2), h // 2
                    prs = slice(pr, pr + 64)
                    fr, ar, Ee, Ww, Dd, qq, kk = frep[x], arg[x], E[x], W[x], Dq[x], qt[x], kt[x]
                    qs_ = slice(128 * x, 128 * x + 128)
                    bbq, sbq, ubq = self.pq[0][x], self.pq[1][x], self.pq[2][x]
                    self.pool(lambda e, fr=fr, i=i, d=d, h=h: e.tensor_copy(fr.ap, LFv[:, i, d, h:h + 1].to_broadcast([128, 128])), [LF], [fr])
                    self.mm(self.ps[:, 0, qs_], fr.ap, self.tri[d].ap, True, True, [fr, self.tri[d]], [bbq])
                    self.dve(lambda e, ar=ar, qs_=qs_, i=i, d=d, h=h: e.scalar_tensor_tensor(ar.ap, self.ps[:, 0, qs_], DIv[:, i, d, h:h + 1],
                                                                                        self.mneg[d].ap, ALU.add, ALU.add),
                             [bbq, DI, self.mneg[d]], [ar])
                    self.act(lambda e, Ee=Ee, ar=ar: e.activation(out=Ee.ap, in_=ar.ap, func=ACT.Exp), [ar], [Ee])
                    self.act(lambda e, Dd=Dd, qs_=qs_, prs=prs: e.activation(out=Dd.ap[prs, :], in_=self.ps[prs, 0, qs_], func=ACT.Exp),
                             [bbq], [Dd])
                    self.pool(lambda e, qq=qq, Dd=Dd, prs=prs, cq=cq, ts=ts: e.tensor_tensor(qq.ap[prs, :], qkv[prs, cq, ts], Dd.ap[prs, :], ALU.mult),
                              [qkT, Dd], [qq])
                    self.mm(self.ps[:, 1, qs_], qkv[prs, ck, ts], qkv[prs, cq, ts], True, True, [qkT], [sbq])
                    self.dve(lambda e, Ww=Ww, qs_=qs_, Ee=Ee: e.tensor_tensor(Ww.ap, self.ps[:, 1, qs_], Ee.ap, ALU.mult),
                             [sbq, Ee], [Ww])
                    self.mm(ndp[:, h, :], Ww.ap, Vxv[:, i, h, :], True, False, [Ww, Vx], [self.pb[ndb]])
                    self.mm(ndp[:, h, :], qq.ap[prs, :], Cnbv[prs, pa, :], False, True, [qq, CnbB[h]], [self.pb[ndb]])
                    us_ = slice(128 * x, 128 * x + 65)
                    self.pool(lambda e, kk=kk, i=i, h=h, Ee=Ee, endc=endc: e.tensor_scalar(kk.ap, ktv[:, i, 64 * h:64 * h + 64], Ee.ap[:, endc:endc + 1], None, ALU.mult),
                              [ktok, Ee], [kk])
                    self.mm(self.ps[prs, 2, us_], kk.ap, Vxv[:, i, h, :], True, True, [kk, Vx], [ubq])
                    self.dve(lambda e, prs=prs, pa=pa, Dd=Dd, us_=us_, endc=endc: e.scalar_tensor_tensor(Cnv[prs, pa, :], Cnv[prs, pa, :], Dd.ap[prs, endc:endc + 1],
                                                                                         self.ps[prs, 2, us_], ALU.mult, ALU.add),
                             [CnB[h], Dd, ubq], [CnB[h]])
                    self.act(lambda e, prs=prs, pa=pa: e.copy(Cnbv[prs, pa, :], Cnv[prs, pa, :]), [CnB[h]], [CnbB[h]])
                den = ndp[:, :, 64]
                self.dve(lambda e, den=den: e.tensor_scalar(sm[0].ap, den, -1.0, None, ALU.mult), [self.pb[ndb]], [sm[0]])
                self.dve(lambda e, den=den: e.tensor_tensor(sm[1].ap, den, sm[0].ap, ALU.max), [self.pb[ndb], sm[0]], [sm[1]])
                self.dve(lambda e: e.tensor_scalar(sm[2].ap, sm[1].ap, 1.0, None, ALU.max), [sm[1]], [sm[2]])
                self.dve(lambda e: e.reciprocal(sm[3].ap, sm[2].ap), [sm[2]], [sm[3]])
                rb = sm[3].ap.unsqueeze(2).to_broadcast([128, 4, 64])
                if d == 0:
                    self.dve(lambda e, i=i, ndp=ndp, rb=rb: e.tensor_tensor(hsv[:, i, :, :], ndp[:, :, 0:64], rb, ALU.mult),
                             [self.pb[ndb], sm[3]], [hsum])
                else:
                    hv3 = htmp.ap.rearrange("p (h d) -> p h d", d=64)
                    self.dve(lambda e, ndp=ndp, rb=rb, hv3=hv3: e.tensor_tensor(hv3, ndp[:, :, 0:64], rb, ALU.mult),
                             [self.pb[ndb], sm[3]], [htmp])
                    self.pool(lambda e, i=i, hv3=hv3: e.tensor_tensor(hsv[:, i, :, :], hsv[:, i, :, :], hv3, ALU.add), [htmp, hsum], [hsum])
        self.S.barrier()
        self.debug_dump("hsum", hsum.ap, [hsum], [128, NT * 256])
        gn = self.alloc(256)
        self.load_rep(gn, I["mlstm_norm_g"][l])
        self.head_norm_out(hsum, gn, go, yT, yTv, 0)
        self.release(mA)

    def mix_gla(self, l, s, uT, uTv, yT, yTv):
        I = self.din
        mA = self.mark()
        gqk = self.alloc(4 * T, BF16, parts=64)
        gqv = gqk.ap.rearrange("p (c t) -> p c t", t=T)
        lrT = [self.alloc(T, parts=16) for _ in range(2)]
        Vg = self.alloc(NT * 256, BF16)
        Vgv = Vg.ap.rearrange("p (i c) -> p i c", c=256)
        gr = self.alloc(NT * 256, BF16)
        grv = gr.ap.rearrange("p (i c) -> p i c", c=256)
        osum = self.alloc(NT * 256)
        osv = osum.ap.rearrange("p (i h d) -> p i h d", h=4, d=64)
        mB = self.mark()
        wq = [self.alloc(8 * 64, BF16), self.alloc(1)]
        gf = self.alloc(T, parts=64)
        permG = self.alloc(64, parts=64)
        self.dma(permG.ap, I["permG"], w=[permG])
        rt = [[self.alloc(512, parts=64), self.alloc(512, parts=64)] for _ in range(2)]
        ra = [self.alloc(512, parts=64) for _ in range(2)]
        for c in range(4):
            c0 = (O_GQ if c < 2 else O_GK) + 64 * (c % 2)
            sc = (32.0 ** -0.5) if c < 2 else 1.0

            def evac(tc, t0, n, psap, pbuf, bcol, sc=sc):
                self.act(lambda e: e.activation(out=gf.ap[:, t0:t0 + n], in_=psap, func=ACT.Identity, bias=bcol.ap[0:64, 0:1], scale=sc),
                         [pbuf, bcol], [gf])
            self.proj_fm(l, s, uTv, uT, c0, 64, evac, wq, bias_scale=(sc if c < 2 else None))
            self.pool(lambda e, c=c: e.tensor_copy(gqv[:, c, 0:CTX], gf.ap[:, 0:CTX]), [gf], [gqk])
            for j in range(4):
                r0, r1 = rt[j % 2]
                self.dma(r0.ap, I["ropeG"][0, :, 512 * j:512 * j + 512], w=[r0])
                self.dma(r1.ap, I["ropeG"][1, :, 512 * j:512 * j + 512], w=[r1], q="pool")
                bank = 2 + (j % 2)
                lat = gf.ap[:, CTX + 512 * j:CTX + 512 * j + 512]
                self.mm(self.ps[0:64, bank, :], permG.ap, lat, True, True, [permG, gf], [self.pb[bank]])
                a_ = ra[j % 2]
                self.dve(lambda e, a_=a_, r0=r0, lat=lat: e.tensor_tensor(a_.ap, lat, r0.ap, ALU.mult), [gf, r0], [a_])
                self.dve(lambda e, r1=r1, bank=bank: e.tensor_tensor(r1.ap, self.ps[0:64, bank, :], r1.ap, ALU.mult),
                         [self.pb[bank], r1], [r1])
                self.pool(lambda e, a_=a_, r1=r1, j=j, c=c: e.tensor_tensor(gqv[:, c, CTX + 512 * j:CTX + 512 * j + 512], a_.ap, r1.ap, ALU.add),
                          [a_, r1], [gqk])
        wl = [self.alloc(8 * 16, BF16), self.alloc(1)]
        for d in range(2):
            def evac2(tc, t0, n, psap, pbuf, bcol, d=d):
                self.act(lambda e: e.activation(out=lrT[d].ap[:, t0:t0 + n], in_=psap, func=ACT.Identity, bias=bcol.ap[0:16, 0:1], scale=1.0),
                         [pbuf, bcol], [lrT[d]])
            self.proj_fm(l, s, uTv, uT, O_LRF + 16 * d, 16, evac2, wl)
        self.release(mB)
        self.debug_dump("gqk", gqk.ap, [gqk], [64, 4 * T], BF16)
        mC = self.mark()
        wvr = self.alloc(8 * 512, BF16)
        brep = self.alloc(512)
        tmpv = [self.alloc(512) for _ in range(2)]

        def evac_vr(i, psap, pbuf, br):
            tv = tmpv[i % 2]
            self.dve(lambda e: e.tensor_tensor(tv.ap, psap, br.ap, ALU.add), [pbuf, br], [tv])
            self.pool(lambda e: e.tensor_copy(Vgv[:, i, :], tv.ap[:, 0:256]), [tv], [Vg])
            self.act(lambda e: e.activation(out=grv[:, i, :], in_=tv.ap[:, 256:512], func=ACT.Silu), [tv], [gr])
        self.proj_tm(l, uT, uTv, O_GV, 512, evac_vr, wvr, brep)
        self.release(mC)
        mD = self.mark()
        w2 = self.alloc(2 * 128, parts=16)
        self.dma(w2.ap.rearrange("p (d c) -> p d c", c=128), I["gla_w2"][l].rearrange("d r c -> r d c"), w=[w2])
        nb2 = self.alloc(4, parts=64)
        for d in range(2):
            for pa in range(2):
                self.dma(nb2.ap[:, 2 * d + pa:2 * d + pa + 1], I["gla_b2"][l, d, 64 * pa:64 * pa + 64].rearrange("(c o) -> c o", o=1), w=[nb2])
        self.dve(lambda e: e.tensor_scalar(nb2.ap, nb2.ap, -1.0, None, ALU.mult), [nb2], [nb2])
        smask = self.alloc(T, parts=64)
        self.dma(smask.ap, I["scanmask"], w=[smask])
        aT = self.alloc(T, parts=64)
        cs = self.alloc(T, parts=64)
        eb = self.alloc(T, parts=64)
        qtl = self.alloc(T, BF16, parts=64)
        ktl = self.alloc(T, BF16, parts=64)
        ktk = self.alloc(NT * 64, BF16)
        ktkv = ktk.ap.rearrange("p (i c) -> p i c", c=64)
        ebe = self.alloc(NT, parts=64)
        Sst = self.alloc(64, parts=64)
        Sb = self.alloc(64, BF16, parts=64)
        Am = [self.alloc(128, BF16) for _ in range(2)]
        cnt = 0
        for d in range(2):
            order = list(range(NT)) if d == 0 else [1, 0] + list(range(NT - 1, 1, -1))
            endc = 127 if d == 0 else 0
            for pa in range(2):
                for tc in range(5):
                    t0 = 512 * tc
                    n = min(512, T - t0)
                    bank = tc % 2
                    self.mm(self.ps[0:64, bank, 0:n], w2.ap[:, 128 * d + 64 * pa:128 * d + 64 * pa + 64], lrT[d].ap[:, t0:t0 + n], True, True,
                            [w2, lrT[d]], [self.pb[bank]])
                    self.act(lambda e, t0=t0, n=n, bank=bank, d=d, pa=pa: e.activation(out=aT.ap[:, t0:t0 + n], in_=self.ps[0:64, bank, 0:n], func=ACT.Exp,
                                                                                      bias=nb2.ap[:, 2 * d + pa:2 * d + pa + 1], scale=-1.0),
                             [self.pb[bank], nb2], [aT])
                self.act(lambda e: e.activation(out=aT.ap, in_=aT.ap, func=ACT.Ln, bias=1.0), [aT], [aT])
                self.dve(lambda e: e.tensor_scalar(aT.ap, aT.ap, -1.0 / 16.0, None, ALU.mult), [aT], [aT])
                self.dve(lambda e: e.tensor_tensor_scan(cs.ap, smask.ap, aT.ap, 0.0, ALU.mult, ALU.add), [smask, aT], [cs])
                if d == 1:
                    a3 = aT.ap.rearrange("p (i t) -> p i t", t=128)
                    c3 = cs.ap.rearrange("p (i t) -> p i t", t=128)
                    e3 = eb.ap.rearrange("p (i t) -> p i t", t=128)
                    self.dve(lambda e, a3=a3, c3=c3: e.tensor_tensor(a3, a3, c3, ALU.subtract), [aT, cs], [aT])
                    self.dve(lambda e, a3=a3, c3=c3, e3=e3: e.tensor_tensor(e3, a3, c3[:, :, 127:128].to_broadcast([64, NT, 128]), ALU.add), [aT, cs], [eb])
                    self.dve(lambda e: e.tensor_copy(cs.ap, eb.ap), [eb], [cs])
                self.act(lambda e: e.activation(out=eb.ap, in_=cs.ap, func=ACT.Exp), [cs], [eb])
                self.pool(lambda e, pa=pa: e.tensor_tensor(qtl.ap, gqv[:, pa, :], eb.ap, ALU.mult), [gqk, eb], [qtl])
                self.dve(lambda e, endc=endc: e.tensor_copy(ebe.ap, eb.ap.rearrange("p (i t) -> p i t", t=128)[:, :, endc]), [eb], [ebe])
                self.act(lambda e: e.activation(out=aT.ap, in_=cs.ap, func=ACT.Exp, scale=-1.0), [cs], [aT])
                self.pool(lambda e, pa=pa: e.tensor_tensor(ktl.ap, gqv[:, 2 + pa, :], aT.ap, ALU.mult), [gqk, aT], [ktl])
                for i in range(NT):
                    bank = 6 + (i % 2)
                    psb = self.ps[:, bank, 0:32].bitcast(BF16)
                    self.pe(lambda e, i=i, psb=psb: e.transpose(psb, ktl.ap[:, 128 * i:128 * i + 128], self.ident_b.ap[0:64, 0:64]),
                            [ktl, self.ident_b], [self.pb[bank]])
                    self.act(lambda e, i=i, psb=psb: e.copy(ktkv[:, i, :], psb), [self.pb[bank]], [ktk])
                self.dve(lambda e: e.memset(Sst.ap, 0.0), [], [Sst])
                self.dve(lambda e: e.memset(Sb.ap, 0.0), [], [Sb])
                for it, i in enumerate(order):
                    ob = 4 + (it % 2)
                    ub = 2 + (it % 2)
                    op_ = self.ps[:, ob, 0:128].rearrange("p (h c) -> p h c", c=64)
                    ts = slice(128 * i, 128 * i + 128)
                    for hh in range(2):
                        h = 2 * pa + hh
                        prs = slice(32 * hh, 32 * hh + 32)
                        x = cnt % 2
                        cnt += 1
                        self.mm(self.ps[:, x, 0:128], ktl.ap[prs, ts], qtl.ap[prs, ts], True, True, [ktl, qtl], [self.pb[x]])
                        self.dve(lambda e, x=x, d=d: e.tensor_tensor(Am[x].ap, self.ps[:, x, 0:128], self.m01b[d].ap, ALU.mult),
                                 [self.pb[x], self.m01b[d]], [Am[x]])
                        self.mm(op_[:, hh, :], Am[x].ap, Vgv[:, i, 64 * h:64 * h + 64], True, False, [Am[x], Vg], [self.pb[ob]])
                        self.mm(op_[:, hh, :], qtl.ap[prs, ts], Sb.ap[prs, :], False, True, [qtl, Sb], [self.pb[ob]])
                        self.mm(self.ps[prs, ub, 0:64], ktkv[:, i, 32 * hh:32 * hh + 32], Vgv[:, i, 64 * h:64 * h + 64], True, True, [ktk, Vg], [self.pb[ub]])
                    self.dve(lambda e, i=i: e.tensor_scalar(Sst.ap, Sst.ap, ebe.ap[:, i:i + 1], None, ALU.mult), [Sst, ebe], [Sst])
                    self.dve(lambda e, i=i, ub=ub: e.scalar_tensor_tensor(Sst.ap, self.ps[0:64, ub, 0:64], ebe.ap[:, i:i + 1], Sst.ap, ALU.mult, ALU.add),
                             [self.pb[ub], ebe, Sst], [Sst])
                    self.act(lambda e: e.copy(Sb.ap, Sst.ap), [Sst], [Sb])
                    if d == 0:
                        self.act(lambda e, i=i, pa=pa, op_=op_: e.copy(osv[:, i, 2 * pa:2 * pa + 2, :], op_), [self.pb[ob]], [osum])
                    else:
                        self.dve(lambda e, i=i, pa=pa, op_=op_: e.tensor_tensor(osv[:, i, 2 * pa:2 * pa + 2, :], op_, osv[:, i, 2 * pa:2 * pa + 2, :], ALU.add),
                                 [self.pb[ob], osum], [osum])
        self.release(mD)
        self.debug_dump("osum", osum.ap, [osum], [128, NT * 256])
        gn = self.alloc(256)
        self.load_rep(gn, I["gla_norm_g"][l])
        self.head_norm_out(osum, gn, gr, yT, yTv, 2)
        self.release(mA)

    def mix_nat(self, l, s, uT, uTv, yT, yTv):
        I = self.din
        for pa in range(4):
            mA = self.mark()
            nq = self.alloc(T, BF16)
            nk = self.alloc(T, BF16)
            Vn = self.alloc(NT * 2 * 65, BF16)
            Vnv = Vn.ap.rearrange("p (i h d) -> p i h d", h=2, d=65)
            yn = self.alloc(NT * 128, BF16)
            ynv = yn.ap.rearrange("p (i c) -> p i c", c=128)
            wq = [self.alloc(8 * 128, BF16), self.alloc(1)]
            wv = self.alloc(8 * 128, BF16)
            brep = self.alloc(128)
            for which, dst, sc in ((0, nq, 0.125), (1, nk, 1.0)):
                c0 = (O_NQ if which == 0 else O_NK) + 128 * pa

                def evac(tc, t0, n, psap, pbuf, bcol, dst=dst, sc=sc):
                    self.act(lambda e: e.activation(out=dst.ap[:, t0:t0 + n], in_=psap, func=ACT.Identity, bias=bcol.ap[:, 0:1], scale=sc),
                             [pbuf, bcol], [dst])
                self.proj_fm(l, s, uTv, uT, c0, 128, evac, wq, bias_scale=(sc if which == 0 else None))
            self.pool(lambda e: e.memset(Vnv[:, :, :, 64:65], 1.0), [], [Vn])

            def evac_v(i, psap, pbuf, br):
                self.dve(lambda e: e.tensor_tensor(Vnv[:, i, :, 0:64], psap.rearrange("p (h d) -> p h d", d=64),
                                                    br.ap.rearrange("p (h d) -> p h d", d=64), ALU.add), [pbuf, br], [Vn])
            self.proj_tm(l, uT, uTv, O_NV + 128 * pa, 128, evac_v, wv, brep)
            tb = [self.alloc(8 * 512) for _ in range(2)]
            arg = [self.alloc(512) for _ in range(2)]
            Ee = [self.alloc(512, BF16) for _ in range(20)]
            rd = [self.alloc(4) for _ in range(2)]
            cnt = 0
            tcnt = 0
            ocnt = 0
            for hh in range(2):
                h = 2 * pa + hh
                prs = slice(64 * hh, 64 * hh + 64)
                blocks = [(0, [0]), (1, [1, 2]), (2, [3]), (3, [-1])]
                for p, qbs in blocks:
                    if p < 3:
                        tbt = tb[tcnt % 2]
                        tcnt += 1
                        self.dma(tbt.ap, I["nbias"][l, h, p].rearrange("p j q -> p (j q)"), w=[tbt], q=("sp" if tcnt % 2 else "pool"))
                        tbv = tbt.ap.rearrange("p (j q) -> p j q", q=512)
                    for qb in qbs:
                        if qb >= 0:
                            q0, nqk, nqs = CTX + 512 * qb, 512, 4
                            kt0 = (0, 4 * qb - 2, 8)[p]
                            keys = [2 + kt0 + j for j in range(8)] + [0, 1]
                            nloc = 8
                            ot0 = 2 + 4 * qb
                        else:
                            q0, nqk, nqs = 0, 256, 2
                            keys = [0, 1]
                            nloc = 0
                            ot0 = 0
                        ob = 4 + (ocnt % 2)
                        ocnt += 1
                        opv = self.ps[:, ob, 0:65 * nqs].rearrange("p (a c) -> p a c", c=65)
                        ebase = 10 * (ocnt % 2)
                        for j, ti in enumerate(keys):
                            sbk = cnt % 4
                            e_ = Ee[ebase + j]
                            a_ = arg[cnt % 2]
                            cnt += 1
                            self.mm(self.ps[:, sbk, 0:nqk], nk.ap[prs, 128 * ti:128 * ti + 128], nq.ap[prs, q0:q0 + nqk], True, True,
                                    [nk, nq], [self.pb[sbk]])
                            if j < nloc:
                                self.dve(lambda e, a_=a_, sbk=sbk, tbv=tbv, j=j: e.tensor_tensor(a_.ap, self.ps[:, sbk, :], tbv[:, j, :], ALU.add),
                                         [self.pb[sbk], tbt], [a_])
                                self.act(lambda e, e_=e_, a_=a_: e.activation(out=e_.ap, in_=a_.ap, func=ACT.Exp), [a_], [e_])
                            else:
                                self.act(lambda e, e_=e_, sbk=sbk, nqk=nqk: e.activation(out=e_.ap[:, 0:nqk], in_=self.ps[:, sbk, 0:nqk], func=ACT.Exp),
                                         [self.pb[sbk]], [e_])
                        for qs in range(nqs):
                            for j, ti in enumerate(keys):
                                e_ = Ee[ebase + j]
                                self.mm(opv[:, qs, :], e_.ap[:, 128 * qs:128 * qs + 128], Vnv[:, ti, hh, :], j == 0, j == len(keys) - 1,
                                        [e_, Vn], [self.pb[ob]])
                        r_ = rd[ocnt % 2]
                        self.dve(lambda e, r_=r_, opv=opv, nqs=nqs: e.reciprocal(r_.ap[:, 0:nqs], opv[:, :, 64]), [self.pb[ob]], [r_])
                        self.dve(lambda e, r_=r_, opv=opv, nqs=nqs, ot0=ot0, hh=hh: e.tensor_tensor(
                            ynv[:, ot0:ot0 + nqs, 64 * hh:64 * hh + 64], opv[:, :, 0:64],
                            r_.ap[:, 0:nqs].unsqueeze(2).to_broadcast([128, nqs, 64]), ALU.mult), [self.pb[ob], r_], [yn])
            for i in range(NT):
                bank = 6 + (i % 2)
                psb = self.ps[:, bank, 0:64].bitcast(BF16)
                self.pe(lambda e, i=i, psb=psb: e.transpose(psb, ynv[:, i, :], self.ident_b.ap), [yn, self.ident_b], [self.pb[bank]])
                self.act(lambda e, i=i, psb=psb, pa=pa: e.copy(yTv[:, 4 + pa, 128 * i:128 * i + 128], psb), [self.pb[bank]], [yT])
            self.release(mA)

    def mix_out(self, l, s, yT, yTv):
        I = self.din
        mA = self.mark()
        wout = self.alloc(8 * D, BF16)
        self.dma(wout.ap.rearrange("p (k n) -> p k n", n=D), I["w_out"][l].rearrange("(k p) n -> p k n", p=128), w=[wout], q="pool")
        wr = self.alloc(8 * NE)
        self.dma(wr.ap.rearrange("p (k e) -> p k e", e=NE), I["w_router"][l].rearrange("(k p) e -> p k e", p=128), w=[wr])
        rp = {}
        for nm, off in (("g1", 2 * D), ("sh2", 3 * D), ("sc2", 4 * D)):
            for tag, row in (("L", s), ("C", 4)):
                rp[nm + tag] = self.alloc(D)
                self.load_rep(rp[nm + tag], self.modv[row, off:off + D], [self.modvb])
        lng = self.alloc(D)
        lnb = self.alloc(D)
        self.load_rep(lng, I["ln1_g"][l])
        self.load_rep(lnb, I["ln1_b"][l])
        xt = [self.alloc(D) for _ in range(2)]
        t1 = [self.alloc(D) for _ in range(2)]
        x1 = [self.alloc(D) for _ in range(2)]
        u2 = [self.alloc(D) for _ in range(2)]
        u2h = [self.alloc(D, BF16) for _ in range(2)]
        u2T = [self.alloc(8 * 128) for _ in range(2)]
        afT = self.alloc(T, parts=16)
        sm = [[self.alloc(12), self.alloc(2), self.alloc(1), self.alloc(1), self.alloc(1), self.alloc(1), self.alloc(1), self.alloc(NE), self.alloc(NE)]
              for _ in range(2)]
        for i in range(NT):
            x = i % 2
            tg = "C" if i < 2 else "L"
            stt, mv, std, rstd, mx, ssum, rs, ex, aff = sm[x]
            b0 = 2 * x
            for half in range(2):
                for k in range(8):
                    self.mm(self.ps[:, b0 + half, :], yTv[:, k, 128 * i:128 * i + 128], wout.ap[:, k * D + 512 * half:k * D + 512 * half + 512],
                            k == 0, k == 7, [yT, wout], [self.pb[b0 + half]])
            src, sb = self.x_src(l, s, i)
            self.dma(xt[x].ap, src, r=[sb], w=[xt[x]])
            opv = self.ps[:, b0:b0 + 2, :].rearrange("p a n -> p (a n)")
            self.dve(lambda e, x=x, opv=opv, tg=tg: e.tensor_tensor(t1[x].ap, opv, rp["g1" + tg].ap, ALU.mult),
                     [self.pb[b0], self.pb[b0 + 1], rp["g1" + tg]], [t1[x]])
            self.dve(lambda e, x=x: e.scalar_tensor_tensor(t1[x].ap, xt[x].ap, ALPHA, t1[x].ap, ALU.mult, ALU.add), [xt[x], t1[x]], [t1[x]])
            self.layer_norm(t1[x], x1[x], lng, lnb, stt, mv, std, rstd)
            self.dma(self.xs[s, 128 * i:128 * i + 128, :], x1[x].ap, r=[x1[x]], w=[self.xsb[s]])
            self.dve(lambda e, x=x, tg=tg: e.tensor_tensor(u2[x].ap, x1[x].ap, rp["sc2" + tg].ap, ALU.mult), [x1[x], rp["sc2" + tg]], [u2[x]])
            self.pool(lambda e, x=x, tg=tg: e.tensor_tensor(u2[x].ap, u2[x].ap, rp["sh2" + tg].ap, ALU.add), [u2[x], rp["sh2" + tg]], [u2[x]])
            self.act(lambda e, x=x: e.copy(u2h[x].ap, u2[x].ap), [u2[x]], [u2h[x]])
            self.dma(self.u2b[s, 128 * i:128 * i + 128, :], u2h[x].ap, r=[u2h[x]], w=[self.u2bb[s]], q="pool")
            tb0 = 4
            for k in range(8):
                self.pe(lambda e, k=k, x=x: e.transpose(self.ps[:, tb0 + k // 4, 128 * (k % 4):128 * (k % 4) + 128], u2[x].ap[:, 128 * k:128 * k + 128],
                                                        self.ident_f.ap), [u2[x], self.ident_f], [self.pb[tb0 + k // 4]])
            self.act(lambda e, x=x: e.copy(u2T[x].ap, self.ps[:, tb0:tb0 + 2, :].rearrange("p a n -> p (a n)")),
                     [self.pb[tb0], self.pb[tb0 + 1]], [u2T[x]])
            for k in range(8):
                self.mm(self.ps[:, 6, 0:NE], u2T[x].ap[:, 128 * k:128 * k + 128], wr.ap[:, NE * k:NE * k + NE], k == 0, k == 7,
                        [u2T[x], wr], [self.pb[6]])
            self.dve(lambda e, mx=mx: e.reduce_max(mx.ap, self.ps[:, 6, 0:NE], axis=AX.X), [self.pb[6]], [mx])
            self.dve(lambda e, mx=mx: e.tensor_scalar(mx.ap, mx.ap, -1.0, None, ALU.mult), [mx], [mx])
            self.act(lambda e, ex=ex, mx=mx, ssum=ssum: e.activation(out=ex.ap, in_=self.ps[:, 6, 0:NE], func=ACT.Exp, bias=mx.ap, scale=1.0,
                                                                   accum_out=ssum.ap), [self.pb[6], mx], [ex, ssum])
            self.dve(lambda e, rs=rs, ssum=ssum: e.reciprocal(rs.ap, ssum.ap), [ssum], [rs])
            self.dve(lambda e, aff=aff, ex=ex, rs=rs: e.tensor_scalar(aff.ap, ex.ap, rs.ap, None, ALU.mult), [ex, rs], [aff])
            self.pe(lambda e, aff=aff: e.transpose(self.ps[0:NE, 7, 0:128], aff.ap, self.ident_f.ap), [aff, self.ident_f], [self.pb[7]])
            self.act(lambda e, i=i: e.copy(afT.ap[:, 128 * i:128 * i + 128], self.ps[0:NE, 7, 0:128]), [self.pb[7]], [afT])
        self.dma(self.affd[NE * s:NE * s + NE, :], afT.ap, r=[afT], w=[self.affdb])
        self.release(mA)

    def moe(self, l, last):
        I = self.din
        NS = self.NS
        R = NS * NE
        N = NS * CAPT
        mL = self.mark()
        idxf = self.alloc(CAPT, parts=R)
        gatef = self.alloc(CAPT, parts=R)
        idxT = self.alloc(3 * R)
        idxTv = idxT.ap.rearrange("p (c r) -> p c r", r=R)
        gateT = self.alloc(3 * R)
        gateTv = gateT.ap.rearrange("p (c r) -> p c r", r=R)
        tcol = self.alloc(NT)
        for i in range(NT):
            self.dve(lambda e, i=i: e.tensor_scalar(tcol.ap[:, i:i + 1], self.pidx.ap, float(128 * i), None, ALU.add), [self.pidx], [tcol])
        m = self.mark()
        wk = self.alloc(T, parts=R)
        self.dma(wk.ap, self.affd[0:R, :], r=[self.affdb], w=[wk])
        idxu = self.alloc(CAPT, U32, parts=R)
        mx8 = self.alloc(8, parts=R)
        for part, (lo, hi, o0, nit) in enumerate(((CTX, T, 0, CAP // 8), (0, CTX, CAP, CAPC // 8))):
            for it in range(nit):
                o = o0 + 8 * it
                self.dve(lambda e, lo=lo, hi=hi, o=o: e.max(out=gatef.ap[:, o:o + 8], in_=wk.ap[:, lo:hi]), [wk], [gatef])
                self.dve(lambda e, lo=lo, hi=hi, o=o: e.max_index(out=idxu.ap[:, o:o + 8], in_max=gatef.ap[:, o:o + 8], in_values=wk.ap[:, lo:hi]),
                         [wk, gatef], [idxu])
                self.dve(lambda e, lo=lo, hi=hi, o=o: e.match_replace(out=wk.ap[:, lo:hi], in_to_replace=gatef.ap[:, o:o + 8], in_values=wk.ap[:, lo:hi],
                                                                      imm_value=-1.0), [wk, gatef], [wk])
        self.dve(lambda e: e.tensor_copy(idxf.ap, idxu.ap), [idxu], [idxf])
        self.dve(lambda e: e.tensor_scalar(idxf.ap[:, 0:CAP], idxf.ap[:, 0:CAP], float(CTX), None, ALU.add), [idxf], [idxf])
        for src, dstv, dst in ((idxf, idxTv, idxT), (gatef, gateTv, gateT)):
            for c, (c0, n) in enumerate(((0, 128), (128, 128), (256, 32))):
                self.pe(lambda e, src=src, c=c, c0=c0, n=n: e.transpose(self.ps[0:n, 7, 64 * c:64 * c + R], src.ap[:, c0:c0 + n], self.ident_f.ap[0:R, 0:R]),
                        [src, self.ident_f], [self.pb[7]])
            self.dve(lambda e, dst=dst: e.memset(dst.ap, 0.0), [], [dst])
            for c, n in enumerate((128, 128, 32)):
                self.dve(lambda e, dstv=dstv, c=c, n=n: e.tensor_copy(dstv[0:n, c, :], self.ps[0:n, 7, 64 * c:64 * c + R]), [self.pb[7]], [dst])
        self.release(m)
        self.debug_dump("idxf", idxf.ap, [idxf], [R, CAPT])
        self.debug_dump("gatef", gatef.ap, [gatef], [R, CAPT])
        m = self.mark()
        U = self.alloc(NT * D, BF16)
        Uv = U.ap.rearrange("p (i d) -> p i d", d=D)
        sel = [self.alloc(128, parts=R) for _ in range(2)]
        PT = [self.alloc(16 * CAP, BF16) for _ in range(2)]
        PTc = [self.alloc(2 * CAPC, BF16) for _ in range(2)]
        xeS = [self.alloc(8 * CAPT, BF16) for _ in range(2)]
        for s in range(NS):
            for i in range(NT):
                self.dma(Uv[:, i, :], self.u2b[s, 128 * i:128 * i + 128, :], r=[self.u2bb[s]], w=[U], q=("sp" if i % 2 else "pool"))
            for e_ in range(NE):
                r = NE * s + e_
                x = e_ % 2
                self.dve(lambda e, x=x, r=r: e.tensor_scalar(sel[x].ap, self.pidx.ap[0:R, 0:1].to_broadcast([R, 128]), float(r), None, ALU.is_equal),
                         [self.pidx], [sel[x]])
                self.mm(self.ps[:, 6, 0:CAPT], sel[x].ap, idxf.ap, True, True, [sel[x], idxf], [self.pb[6]])
                ptv = PT[x].ap.rearrange("p (i c) -> p i c", c=CAP)
                pcv = PTc[x].ap.rearrange("p (i c) -> p i c", c=CAPC)
                for i in range(2, NT):
                    self.dve(lambda e, i=i, ptv=ptv: e.tensor_scalar(ptv[:, i - 2, :], self.ps[:, 6, 0:CAP], tcol.ap[:, i:i + 1], None, ALU.is_equal),
                             [self.pb[6], tcol], [PT[x]])
                for i in range(2):
                    self.dve(lambda e, i=i, pcv=pcv: e.tensor_scalar(pcv[:, i, :], self.ps[:, 6, CAP:CAPT], tcol.ap[:, i:i + 1], None, ALU.is_equal),
                             [self.pb[6], tcol], [PTc[x]])
                xv = xeS[x].ap.rearrange("p (k c) -> p k c", c=CAPT)
                for dk in range(8):
                    bank = dk % 4
                    for i in range(2, NT):
                        self.mm(self.ps[:, bank, 0:CAP], Uv[:, i, 128 * dk:128 * dk + 128], ptv[:, i - 2, :], i == 2, i == NT - 1, [U, PT[x]], [self.pb[bank]])
                    for i in range(2):
                        self.mm(self.ps[:, bank, CAP:CAPT], Uv[:, i, 128 * dk:128 * dk + 128], pcv[:, i, :], i == 0, i == 1, [U, PTc[x]], [self.pb[bank]])
                    self.act(lambda e, xv=xv, dk=dk, bank=bank: e.copy(xv[:, dk, :], self.ps[:, bank, 0:CAPT]), [self.pb[bank]], [xeS[x]])
                self.dma(self.xeT[e_][:, :, CAPT * s:CAPT * s + CAPT].rearrange("k p c -> p k c"), xv, r=[xeS[x]], w=[self.xeTb[e_]])
        self.release(m)
        m = self.mark()
        nchunk = (N + 511) // 512
        cw = N // nchunk
        xin = [self.alloc(8 * N, BF16) for _ in range(2)]
        wgu = [[self.alloc(8 * 512, BF16) for _ in range(2)] for _ in range(2)]
        wd = [self.alloc(16 * D, BF16) for _ in range(2)]
        hT = self.alloc(16 * N, BF16)
        hTv = hT.ap.rearrange("p (j n) -> p j n", n=N)
        sg = [self.alloc(cw) for _ in range(2)]
        yo = [self.alloc(D, BF16) for _ in range(2)]
        cnt = 0
        ycnt = 0
        for e_ in range(NE):
            xi = xin[e_ % 2]
            xiv = xi.ap.rearrange("p (k n) -> p k n", n=N)
            self.dma(xiv, self.xeT[e_].rearrange("k p n -> p k n"), r=[self.xeTb[e_]], w=[xi])
            wdt = wd[e_ % 2]
            wdv = wdt.ap.rearrange("p (j d) -> p j d", d=D)
            for jb in range(4):
                self.dma(wdv[:, 4 * jb:4 * jb + 4, :], I["w_down"][l, e_, 512 * jb:512 * jb + 512, :].rearrange("(j p) d -> p j d", p=128), w=[wdt], q="pool")
            for fb in range(4):
                wts = []
                for mi, nm in enumerate(("w_gate", "w_up")):
                    wt = wgu[mi][fb % 2]
                    self.dma(wt.ap.rearrange("p (k f) -> p k f", f=512), I[nm][l, e_, :, 512 * fb:512 * fb + 512].rearrange("(k p) f -> p k f", p=128),
                             w=[wt], q="pool")
                    wts.append(wt)
                for fc in range(4):
                    j = 4 * fb + fc
                    for ng in range(nchunk):
                        n0 = cw * ng
                        gb, ubk = (cnt % 2), 2 + (cnt % 2)
                        s_ = sg[cnt % 2]
                        cnt += 1
                        for k in range(8):
                            self.mm(self.ps[:, gb, 0:cw], wts[0].ap[:, 512 * k + 128 * fc:512 * k + 128 * fc + 128], xiv[:, k, n0:n0 + cw], k == 0, k == 7,
                                    [wts[0], xi], [self.pb[gb]])
                        for k in range(8):
                            self.mm(self.ps[:, ubk, 0:cw], wts[1].ap[:, 512 * k + 128 * fc:512 * k + 128 * fc + 128], xiv[:, k, n0:n0 + cw], k == 0, k == 7,
                                    [wts[1], xi], [self.pb[ubk]])
                        self.act(lambda e, s_=s_, gb=gb: e.activation(out=s_.ap, in_=self.ps[:, gb, 0:cw], func=ACT.Silu), [self.pb[gb]], [s_])
                        self.dve(lambda e, s_=s_, ubk=ubk, j=j, n0=n0: e.tensor_tensor(hTv[:, j, n0:n0 + cw], self.ps[:, ubk, 0:cw], s_.ap, ALU.mult),
                                 [self.pb[ubk], s_], [hT])
            for s in range(NS):
                r = NE * s + e_
                for c, (c0, n) in enumerate(((0, 128), (128, 128), (256, 32))):
                    y_ = yo[ycnt % 2]
                    for half in range(2):
                        bank = 4 + (ycnt % 2) * 2 + half
                        for j in range(16):
                            self.mm(self.ps[0:n, bank, :], hTv[:, j, CAPT * s + c0:CAPT * s + c0 + n], wdv[:, j, 512 * half:512 * half + 512], j == 0, j == 15,
                                    [hT, wdt], [self.pb[bank]])
                        self.act(lambda e, y_=y_, n=n, bank=bank, half=half, c=c, r=r: e.activation(out=y_.ap[0:n, 512 * half:512 * half + 512], in_=self.ps[0:n, bank, :],
                                                                                               func=ACT.Copy, scale=gateTv[0:n, c, r:r + 1]),
                                 [self.pb[bank], gateT], [y_])
                    ycnt += 1
                    self.dma(self.ye[s, e_, c0:c0 + n, :], y_.ap[0:n, :], r=[y_], w=[self.yeb[s][e_]])
        self.release(m)
        m = self.mark()
        YL = self.alloc(NE * 2 * D, BF16)
        YLv = YL.ap.rearrange("p (e a d) -> p e a d", a=2, d=D)
        YC = self.alloc(NE * D, BF16, parts=32)
        YCv = YC.ap.rearrange("p (e d) -> p e d", d=D)
        iot = self.alloc(T)
        self.dma(iot.ap, I["iota_t"], w=[iot])
        Pm = [self.alloc(32 * 128, BF16) for _ in range(2)]
        Pc = [self.alloc(16 * 128, BF16, parts=32) for _ in range(2)]
        lng = self.alloc(D)
        lnb = self.alloc(D)
        self.load_rep(lng, I["ln2_g"][l])
        self.load_rep(lnb, I["ln2_b"][l])
        g2 = {"L": self.alloc(D), "C": self.alloc(D)}
        xt = [self.alloc(D) for _ in range(2)]
        t1 = [self.alloc(D) for _ in range(2)]
        x2 = [self.alloc(D) for _ in range(2)]
        sm = [[self.alloc(12), self.alloc(2), self.alloc(1), self.alloc(1)] for _ in range(2)]
        for s in range(NS):
            for e_ in range(NE):
                self.dma(YLv[:, e_, :, :], self.ye[s, e_, 0:CAP, :].rearrange("(a p) d -> p a d", p=128), r=[self.yeb[s][e_]], w=[YL],
                         q=("sp" if e_ % 2 else "pool"))
                self.dma(YCv[:, e_, :], self.ye[s, e_, CAP:CAPT, :], r=[self.yeb[s][e_]], w=[YC])
            self.load_rep(g2["L"], self.modv[s, 5 * D:6 * D], [self.modvb])
            self.load_rep(g2["C"], self.modv[4, 5 * D:6 * D], [self.modvb])
            idxL = idxTv[:, 0:2, NE * s:NE * s + NE].rearrange("p a e -> p e a")
            idxC = idxTv[0:32, 2, NE * s:NE * s + NE]
            for i in (range(2, NT) if last else range(NT)):
                x = i % 2
                tg = "C" if i < 2 else "L"
                stt, mv, std, rstd = sm[x]
                b0 = 2 * x
                if i >= 2:
                    pv = Pm[x].ap.rearrange("p (e a t) -> p e a t", a=2, t=128)
                    self.dve(lambda e, pv=pv, i=i, idxL=idxL: e.tensor_tensor(
                        pv, iot.ap[:, 128 * i:128 * i + 128].unsqueeze(1).unsqueeze(1).to_broadcast([128, NE, 2, 128]),
                        idxL.unsqueeze(3).to_broadcast([128, NE, 2, 128]), ALU.is_equal), [iot, idxT], [Pm[x]])
                    for half in range(2):
                        for e_ in range(NE):
                            for a in range(2):
                                self.mm(self.ps[:, b0 + half, :], pv[:, e_, a, :], YLv[:, e_, a, 512 * half:512 * half + 512],
                                        (e_ == 0 and a == 0), (e_ == NE - 1 and a == 1), [Pm[x], YL], [self.pb[b0 + half]])
                else:
                    pv = Pc[x].ap.rearrange("p (e t) -> p e t", t=128)
                    self.dve(lambda e, pv=pv, i=i, idxC=idxC: e.tensor_tensor(
                        pv, iot.ap[0:32, 128 * i:128 * i + 128].unsqueeze(1).to_broadcast([32, NE, 128]),
                        idxC.unsqueeze(2).to_broadcast([32, NE, 128]), ALU.is_equal), [iot, idxT], [Pc[x]])
                    for half in range(2):
                        for e_ in range(NE):
                            self.mm(self.ps[:, b0 + half, :], pv[:, e_, :], YCv[:, e_, 512 * half:512 * half + 512],
                                    e_ == 0, e_ == NE - 1, [Pc[x], YC], [self.pb[b0 + half]])
                self.dma(xt[x].ap, self.xs[s, 128 * i:128 * i + 128, :], r=[self.xsb[s]], w=[xt[x]])
                opv = self.ps[:, b0:b0 + 2, :].rearrange("p a n -> p (a n)")
                if "fmoe" in self.dbgn and s == 0:
                    self.dve(lambda e, x=x, opv=opv: e.tensor_copy(t1[x].ap, opv), [self.pb[b0], self.pb[b0 + 1]], [t1[x]])
                    self.dma(self.dbg_f[128 * i:128 * i + 128, :], t1[x].ap, r=[t1[x]])
                self.dve(lambda e, x=x, opv=opv, tg=tg: e.tensor_tensor(t1[x].ap, opv, g2[tg].ap, ALU.mult),
                         [self.pb[b0], self.pb[b0 + 1], g2[tg]], [t1[x]])
                self.dve(lambda e, x=x: e.scalar_tensor_tensor(t1[x].ap, xt[x].ap, ALPHA, t1[x].ap, ALU.mult, ALU.add), [xt[x], t1[x]], [t1[x]])
                self.layer_norm(t1[x], x2[x], lng, lnb, stt, mv, std, rstd)
                if last:
                    self.dma(self.out[s, 128 * (i - 2):128 * (i - 2) + 128, :], x2[x].ap, r=[x2[x]])
                else:
                    self.dma(self.xs[s, 128 * i:128 * i + 128, :], x2[x].ap, r=[x2[x]], w=[self.xsb[s]])
        self.release(m)
        self.release(mL)

    def layer_norm(self, src, dst, lng, lnb, stt, mv, std, rstd):
        st3 = stt.ap.rearrange("p (c f) -> p c f", f=6)
        for c in range(2):
            self.dve(lambda e, c=c: e.bn_stats(st3[:, c, :], src.ap[:, 512 * c:512 * c + 512]), [src], [stt])
        self.dve(lambda e: e.bn_aggr(mv.ap, st3), [stt], [mv])
        self.act(lambda e: e.activation(out=std.ap, in_=mv.ap[:, 1:2], func=ACT.Sqrt, bias=self.epsc.ap, scale=1.0), [mv, self.epsc], [std])
        self.dve(lambda e: e.reciprocal(rstd.ap, std.ap), [std], [rstd])
        self.dve(lambda e: e.tensor_scalar(src.ap, src.ap, mv.ap[:, 0:1], rstd.ap, ALU.subtract, ALU.mult), [src, mv, rstd], [src])
        self.pool(lambda e: e.tensor_tensor(dst.ap, src.ap, lng.ap, ALU.mult), [src, lng], [dst])
        self.pool(lambda e: e.tensor_tensor(dst.ap, dst.ap, lnb.ap, ALU.add), [dst, lnb], [dst])

    def head_norm_out(self, hsum, gn, gate, yT, yTv, ych):
        hsv = hsum.ap.rearrange("p (i h d) -> p i h d", h=4, d=64)
        gtv = gate.ap.rearrange("p (i c) -> p i c", c=256)
        m = self.mark()
        sq = [self.alloc(256) for _ in range(2)]
        st = [[self.alloc(4) for _ in range(6)] for _ in range(2)]
        tt = [self.alloc(256) for _ in range(2)]
        ym = [self.alloc(256, BF16) for _ in range(2)]
        for i in range(NT):
            x = i % 2
            s1, s2, mean, var, std, rstd = st[x]
            hv = hsv[:, i, :, :]
            sq3 = sq[x].ap.rearrange("p (h d) -> p h d", d=64)
            t3 = tt[x].ap.rearrange("p (h d) -> p h d", d=64)
            self.dve(lambda e, hv=hv, s1=s1: e.reduce_sum(s1.ap, hv, axis=AX.X), [hsum], [s1])
            self.pool(lambda e, hv=hv, sq3=sq3: e.tensor_tensor(sq3, hv, hv, ALU.mult), [hsum], [sq[x]])
            self.dve(lambda e, sq3=sq3, s2=s2: e.reduce_sum(s2.ap, sq3, axis=AX.X), [sq[x]], [s2])
            self.dve(lambda e, s1=s1, mean=mean: e.tensor_scalar(mean.ap, s1.ap, 1.0 / 64, None, ALU.mult), [s1], [mean])
            self.dve(lambda e, mean=mean, var=var: e.tensor_tensor(var.ap, mean.ap, mean.ap, ALU.mult), [mean], [var])
            self.dve(lambda e, s2=s2, var=var: e.scalar_tensor_tensor(var.ap, s2.ap, 1.0 / 64, var.ap, ALU.mult, ALU.subtract), [s2, var], [var])
            self.act(lambda e, var=var, std=std: e.activation(out=std.ap, in_=var.ap, func=ACT.Sqrt, bias=self.epsc.ap, scale=1.0),
                     [var, self.epsc], [std])
            self.dve(lambda e, std=std, rstd=rstd: e.reciprocal(rstd.ap, std.ap), [std], [rstd])
            mb_ = mean.ap.unsqueeze(2).to_broadcast([128, 4, 64])
            rb_ = rstd.ap.unsqueeze(2).to_broadcast([128, 4, 64])
            self.dve(lambda e, hv=hv, t3=t3, mb_=mb_: e.tensor_tensor(t3, hv, mb_, ALU.subtract), [hsum, mean], [tt[x]])
            self.dve(lambda e, t3=t3, rb_=rb_: e.tensor_tensor(t3, t3, rb_, ALU.mult), [tt[x], rstd], [tt[x]])
            self.pool(lambda e, x=x: e.tensor_tensor(tt[x].ap, tt[x].ap, gn.ap, ALU.mult), [tt[x], gn], [tt[x]])
            self.pool(lambda e, x=x, i=i: e.tensor_tensor(ym[x].ap, tt[x].ap, gtv[:, i, :], ALU.mult), [tt[x], gate], [ym[x]])
            bank = 6 + x
            psb = self.ps[:, bank, 0:128].bitcast(BF16)
            for c in range(2):
                self.pe(lambda e, c=c, x=x, psb=psb: e.transpose(psb[:, 128 * c:128 * c + 128], ym[x].ap[:, 128 * c:128 * c + 128], self.ident_b.ap),
                        [ym[x], self.ident_b], [self.pb[bank]])
            self.act(lambda e, i=i, psb=psb: e.copy(yTv[:, ych:ych + 2, 128 * i:128 * i + 128], psb.rearrange("p (c t) -> p c t", t=128)),
                     [self.pb[bank]], [yT])
        self.release(m)

    def proj_tm(self, l, uT, uTv, c0, ncols, evac_fn, wt, brep):
        I = self.din
        self.dma(wt.ap[:, 0:8 * ncols].rearrange("p (k c) -> p k c", c=ncols),
                 I["w_in"][l, :, c0:c0 + ncols].rearrange("(k p) c -> p k c", p=128), w=[wt], q="pool")
        self.load_rep(brep, I["b_in"][l, c0:c0 + ncols])
        for i in range(NT):
            bank = i % 2
            for k in range(8):
                self.mm(self.ps[:, bank, 0:ncols], uTv[:, k, 128 * i:128 * i + 128], wt.ap[:, k * ncols:(k + 1) * ncols],
                        k == 0, k == 7, [uT, wt], [self.pb[bank]])
            evac_fn(i, self.ps[:, bank, 0:ncols], self.pb[bank], brep)


N_CORES = 8
NS_CORE = 4
DEPTH = 4
_W_NAMES = ("w_mod", "b_mod", "w_in", "b_in", "conv_w", "gla_w2", "gla_b2", "mlstm_norm_g", "gla_norm_g",
            "w_out", "ln1_g", "ln1_b", "w_router", "w_gate", "w_up", "w_down", "ln2_g", "ln2_b")


def build_program(NS=NS_CORE, NL=DEPTH):
    B = Builder(NS, NL)
    B.consts()
    for l in range(NL):
        B.layer_mod(l)
        for s in range(NS):
            B.mixer(l, s)
        B.moe(l, l == NL - 1)
    B.S.final_wait("sp")
    B.S.emit()
    return B


def kernel(**inputs):
    f32 = lambda a: np.ascontiguousarray(np.asarray(a), dtype=np.float32)
    x = f32(inputs["x"])
    c = f32(inputs["c"])
    ctx = f32(inputs["ctx"])
    c_ctx = f32(inputs["c_ctx"])
    shared = {k: f32(inputs[k]) for k in _W_NAMES}
    shared["nbias"] = natten_tables(f32(inputs["rpb"]))
    shared.update(host_consts())
    B = build_program()
    in_maps = []
    for ci in range(N_CORES):
        sl = slice(NS_CORE * ci, NS_CORE * ci + NS_CORE)
        m = dict(shared)
        m["x"] = x[sl]
        m["ctx"] = ctx[sl]
        m["cc"] = np.concatenate([c[sl], c_ctx[None, :]], 0)
        in_maps.append(m)
    res = run_bass_kernel_spmd(B.nc, in_maps, core_ids=list(range(N_CORES)))
    return np.concatenate([np.asarray(r["out"]) for r in res.results], axis=0).astype(np.float32)
```

```python
import contextlib
import numpy as np
import ml_dtypes
import concourse.bass as bass
import concourse.mybir as mybir
from concourse.bass_utils import run_bass_kernel_spmd

F32 = mybir.dt.float32
BF16 = mybir.dt.bfloat16
U32 = mybir.dt.uint32
ALU = mybir.AluOpType
ACT = mybir.ActivationFunctionType
AX = mybir.AxisListType

D = 1024
SEQ = 2048
CTX = 256
T = SEQ + CTX
NT = T // 128
NE = 16
FF = 2048
CAP = 256
CAPC = 32
CAPT = CAP + CAPC
PROJ = 3376
ALPHA = 8.0 ** 0.25
EPS = 1e-5
NEG = -30000.0
O_MQ, O_MK, O_MV, O_MO = 0, 256, 512, 768
O_MIF, O_MFF, O_MIB, O_MFB = 1024, 1028, 1032, 1036
O_GQ, O_GK, O_GV, O_GR = 1040, 1168, 1296, 1552
O_LRF, O_LRB = 1808, 1824
O_NQ, O_NK, O_NV = 1840, 2352, 2864


class Buf:
    __slots__ = ("name", "w", "r")

    def __init__(self, name=""):
        self.name = name
        self.w = None
        self.r = {}


class Sched:
    ENG = ("pe", "act", "dve", "pool", "sp")
    NK = 64

    def __init__(self, nc, n_dma_sems=14):
        self.nc = nc
        self.ops = {e: [] for e in self.ENG}
        self.seq = {e: 0 for e in self.ENG}
        self.kidx = {}
        self.known = {e: np.zeros(self.NK, np.int64) for e in self.ENG}
        self.snap = {}
        self.n_dma_sems = n_dma_sems
        self.dma_ring = {e: 0 for e in self.ENG}
        self.dma_val = {}
        self.semkeys = set()

    def _ki(self, sk):
        i = self.kidx.get(sk)
        if i is None:
            i = self.kidx[sk] = len(self.kidx)
            assert i < self.NK
        return i

    def _need(self, eng, tok, waits):
        if tok is None:
            return
        sk, val = tok
        kn = self.known[eng]
        if eng == "pe" and sk == ("c", "pe"):
            return
        i = self._ki(sk)
        if kn[i] >= val:
            return
        kn[i] = val
        sn = self.snap.get(tok)
        if sn is not None:
            np.maximum(kn, sn, out=kn)
        waits.append((sk, val))

    def _deps(self, eng, reads, writes):
        waits = []
        for b in reads:
            self._need(eng, b.w, waits)
        for b in writes:
            self._need(eng, b.w, waits)
            for sk, val in b.r.items():
                self._need(eng, (sk, val), waits)
        best = {}
        for sk, val in waits:
            if best.get(sk, 0) < val:
                best[sk] = val
        return list(best.items())

    def _mark(self, tok, reads, writes):
        sk, val = tok
        for b in reads:
            if b.r.get(sk, 0) < val:
                b.r[sk] = val
        for b in writes:
            b.w = tok
            b.r = {}

    def op(self, eng, fn, reads=(), writes=()):
        waits = self._deps(eng, reads, writes)
        self.seq[eng] += 1
        sk = ("c", eng)
        self.semkeys.add(sk)
        tok = (sk, self.seq[eng])
        self.snap[tok] = self.known[eng].copy()
        self.ops[eng].append((waits, fn, (sk, 1)))
        self._mark(tok, reads, writes)
        return tok

    def dma(self, eng, out, in_, reads=(), writes=(), **kw):
        slot = self.dma_ring[eng]
        self.dma_ring[eng] = (slot + 1) % self.n_dma_sems
        sk = ("d", eng, slot)
        self.semkeys.add(sk)
        waits = self._deps(eng, reads, writes)
        prev = self.dma_val.get(sk, 0)
        i = self._ki(sk)
        if prev > 0 and self.known[eng][i] < prev:
            self.known[eng][i] = prev
            waits = [w for w in waits if w[0] != sk] + [(sk, prev)]
        val = prev + 16
        self.dma_val[sk] = val
        tok = (sk, val)
        self.snap[tok] = self.known[eng].copy()

        def fn(e, out=out, in_=in_, kw=kw):
            return e.dma_start(out=out, in_=in_, **kw)
        self.ops[eng].append((waits, fn, (sk, 16)))
        self._mark(tok, reads, writes)
        return tok

    def _all_tokens(self):
        toks = [(("c", e), self.seq[e]) for e in self.ENG if self.seq[e] > 0]
        toks += [(sk, v) for sk, v in self.dma_val.items()]
        return toks

    def barrier(self):
        toks = self._all_tokens()
        for e in self.ENG:
            waits = []
            for tok in toks:
                self._need(e, tok, waits)
            if waits:
                self.ops[e].append((waits, None, None))

    def final_wait(self, eng="sp"):
        waits = []
        for sk, val in self._all_tokens():
            i = self._ki(sk)
            if self.known[eng][i] < val:
                self.known[eng][i] = val
                waits.append((sk, val))
        self.ops[eng].append((waits, None, None))

    def emit(self):
        nc = self.nc
        with contextlib.ExitStack() as st:
            sems = {}
            for sk in sorted(self.semkeys, key=str):
                sems[sk] = st.enter_context(nc.semaphore("s_" + "_".join(str(x) for x in sk)))
            block = st.enter_context(nc.Block())

            def run(e, lst):
                for waits, fn, inc in lst:
                    for sk, val in waits:
                        e.wait_ge(sems[sk], val)
                    if fn is not None:
                        fn(e).then_inc(sems[inc[0]], inc[1])

            @block.tensor
            def _(e):
                run(e, self.ops["pe"])

            @block.scalar
            def _(e):
                run(e, self.ops["act"])

            @block.vector
            def _(e):
                run(e, self.ops["dve"])

            @block.gpsimd
            def _(e):
                run(e, self.ops["pool"])

            @block.sync
            def _(e):
                run(e, self.ops["sp"])


class Tl:
    __slots__ = ("ap", "b")

    def __init__(self, ap, b):
        self.ap = ap
        self.b = b


def _bf(x):
    return np.ascontiguousarray(x).astype(ml_dtypes.bfloat16)


def host_consts():
    c = {}
    c["ident_f"] = np.eye(128, dtype=np.float32)
    s = np.arange(128)[:, None]
    t = np.arange(128)[None, :]
    c["tri"] = np.stack([(s <= t), (s >= t)]).astype(np.float32)
    c["mneg"] = np.where(c["tri"] > 0, 0.0, NEG).astype(np.float32)
    tt = np.arange(SEQ)
    rows = (tt // 64).astype(np.float32)
    cols = (tt % 64).astype(np.float32)

    def rope_tab(dh, reps):
        h = dh // 2
        d2 = h // 2
        inv = (10000.0 ** (-np.arange(d2, dtype=np.float32) / d2)).astype(np.float32)
        C = np.zeros((dh, SEQ), np.float32)
        Sg = np.zeros((dh, SEQ), np.float32)
        P = np.zeros((dh, dh), np.float32)
        for d in range(dh):
            blk = d // h
            j = (d % h) % d2
            half = (d % h) // d2
            pos = rows if blk == 0 else cols
            ang = (pos * inv[j]).astype(np.float32)
            C[d] = np.cos(ang)
            Sg[d] = np.sin(ang) * (-1.0 if half == 0 else 1.0)
            partner = blk * h + (1 - half) * d2 + j
            P[partner, d] = 1.0
        Cr = np.tile(C, (reps, 1))
        Sr = np.tile(Sg, (reps, 1))
        Pr = np.kron(np.eye(reps, dtype=np.float32), P)
        return np.stack([Cr, Sr]).astype(np.float32), Pr.astype(np.float32)
    c["ropeM"], c["permM"] = rope_tab(64, 2)
    c["ropeG"], c["permG"] = rope_tab(32, 2)
    c["iota_t"] = np.tile(np.arange(T, dtype=np.float32)[None, :], (128, 1))
    c["pidx"] = np.arange(128, dtype=np.float32)[:, None].copy()
    mk = np.ones((64, T), np.float32)
    mk[:, ::128] = 0.0
    c["scanmask"] = mk
    c["sel64"] = np.zeros((64, 64, 128), np.float32)
    for r in range(64):
        c["sel64"][r, r, :] = 1.0
    return c


def natten_tables(rpb):
    L = rpb.shape[0]
    krl = np.arange(2)[:, None, None, None, None]
    ck = np.arange(64)[None, :, None, None, None]
    j = np.arange(8)[None, None, :, None, None]
    qrl = np.arange(8)[None, None, None, :, None]
    cq = np.arange(64)[None, None, None, None, :]
    out = np.empty((L, 8, 3, 128, 8, 512), np.float32)
    for p, (q0, k0) in enumerate(((0, 0), (8, 4), (24, 16))):
        qr = q0 + qrl
        kr = k0 + 2 * j + krl
        rs = np.clip(qr - 4, 0, 24)
        cs = np.clip(cq - 8, 0, 48)
        valid = (kr >= rs) & (kr < rs + 8) & (ck >= cs) & (ck < cs + 16)
        dr = np.clip(kr - qr + 7, 0, 14)
        dc = np.clip(ck - cq + 15, 0, 30)
        valid, dr, dc = np.broadcast_arrays(valid, dr, dc)
        g = rpb[:, :, dr, dc]
        g = np.where(valid[None, None], g, np.float32(NEG))
        out[:, :, p] = g.reshape(L, 8, 128, 8, 512)
    return out


class Builder:
    def __init__(self, NS, NL, stop=None, dbg=(), moe=True, nat=True):
        self.NS, self.NL, self.stop, self.dbgn = NS, NL, stop, set(dbg)
        nc = self.nc = bass.Bass("TRN2", target_bir_lowering=False)
        self.S = Sched(nc)
        self.st = contextlib.ExitStack()
        di = lambda name, shape, dt=F32: nc.dram_tensor(name, list(shape), dt, kind="ExternalInput").ap()
        self.din = {}
        I = self.din
        I["x"] = di("x", [NS, SEQ, D])
        I["ctx"] = di("ctx", [NS, CTX, D])
        I["cc"] = di("cc", [5, D])
        for name, shp in (("w_mod", [NL, D, 6 * D]), ("b_mod", [NL, 6 * D]), ("w_in", [NL, D, PROJ]),
                          ("b_in", [NL, PROJ]), ("conv_w", [NL, 3, 512]), ("gla_w2", [NL, 2, 16, 128]),
                          ("gla_b2", [NL, 2, 128]), ("mlstm_norm_g", [NL, 256]), ("gla_norm_g", [NL, 256]),
                          ("w_out", [NL, D, D]), ("ln1_g", [NL, D]), ("ln1_b", [NL, D]),
                          ("w_router", [NL, D, NE]), ("w_gate", [NL, NE, D, FF] if moe else [1, 1, 8, 8]),
                          ("w_up", [NL, NE, D, FF] if moe else [1, 1, 8, 8]),
                          ("w_down", [NL, NE, FF, D] if moe else [1, 1, 8, 8]), ("ln2_g", [NL, D]), ("ln2_b", [NL, D]),
                          ("nbias", [NL, 8, 3, 128, 8, 512] if nat else [1, 1, 1, 8, 1, 8]),
                          ("ident_f", [128, 128]), ("tri", [2, 128, 128]), ("mneg", [2, 128, 128]),
                          ("ropeM", [2, 128, SEQ]), ("permM", [128, 128]), ("ropeG", [2, 64, SEQ]),
                          ("permG", [64, 64]), ("iota_t", [128, T]), ("pidx", [128, 1]),
                          ("sel64", [64, 64, 128]), ("scanmask", [64, T])):
            I[name] = di(name, shp)
        self.out = nc.dram_tensor("out", [NS, SEQ, D], F32, kind="ExternalOutput").ap()
        ds = lambda name, shape, dt=F32: nc.dram_tensor(name, list(shape), dt, kind="Internal").ap()
        self.xs = ds("xs", [NS, T, D])
        self.xsb = [Buf("xs%d" % s) for s in range(NS)]
        self.modv = ds("modv", [5, 6 * D])
        self.modvb = Buf("modv")
        self.u2b = ds("u2b", [NS, T, D], BF16)
        self.u2bb = [Buf() for _ in range(NS)]
        self.affd = ds("affd", [NS * NE, T])
        self.affdb = Buf("affd")
        self.xeT = ds("xeT", [NE, 8, 128, NS * CAPT], BF16)
        self.xeTb = [Buf() for _ in range(NE)]
        self.ye = ds("ye", [NS, NE, CAPT, D], BF16)
        self.yeb = [[Buf() for _ in range(NE)] for _ in range(NS)]
        self.dbg = {}
        if 'fmoe' in self.dbgn:
            self.dbg_f = nc.dram_tensor('dbg_fmoe', [T, D], F32, kind='ExternalOutput').ap()
        self.ARENA = 51200
        self.arena = self.st.enter_context(nc.sbuf_tensor("arena", [128, self.ARENA], F32))
        self.ps = self.st.enter_context(nc.psum_tensor("ps", [128, 8, 512], F32))
        self.pb = [Buf("ps%d" % i) for i in range(8)]
        self.pq = [[Buf() for _ in range(4)] for _ in range(8)]
        self.col = 0
        self.inb = {k: Buf(k) for k in I}

    def alloc(self, n, dt=F32, parts=128):
        nb = n * (2 if dt == BF16 else 4)
        ncol = (nb + 3) // 4
        assert self.col + ncol <= self.ARENA, ("SBUF arena overflow", self.col, ncol)
        ap = self.arena[0:parts, self.col:self.col + ncol]
        self.col += ncol
        if dt != F32:
            ap = ap.bitcast(dt)
        return Tl(ap, Buf())

    def mark(self):
        return self.col

    def release(self, m):
        self.S.barrier()
        self.col = m

    def _bufs(self, lst):
        return [x.b if isinstance(x, Tl) else x for x in lst]

    def dve(self, fn, r=(), w=()):
        return self.S.op("dve", fn, self._bufs(r), self._bufs(w))

    def act(self, fn, r=(), w=()):
        return self.S.op("act", fn, self._bufs(r), self._bufs(w))

    def pool(self, fn, r=(), w=()):
        return self.S.op("pool", fn, self._bufs(r), self._bufs(w))

    def pe(self, fn, r=(), w=()):
        return self.S.op("pe", fn, self._bufs(r), self._bufs(w))

    def dma(self, out, in_, r=(), w=(), q="sp", **kw):
        return self.S.dma(q, out, in_, self._bufs(r), self._bufs(w), **kw)

    def mm(self, out, lhsT, rhs, start, stop, r, w):
        return self.pe(lambda e: e.matmul(out, lhsT=lhsT, rhs=rhs, start=start, stop=stop), r, w)

    def debug_dump(self, name, ap, bufs, shape, dt=F32):
        if name not in self.dbgn:
            return
        d = self.nc.dram_tensor("dbg_" + name, list(shape), dt, kind="ExternalOutput").ap()
        self.dbg[name] = d
        self.dma(d, ap, r=bufs)

    def consts(self):
        I = self.din
        self.ident_f = self.alloc(128)
        self.dma(self.ident_f.ap, I["ident_f"], w=[self.ident_f])
        self.ident_b = self.alloc(128, BF16)
        self.dve(lambda e: e.tensor_copy(self.ident_b.ap, self.ident_f.ap), [self.ident_f], [self.ident_b])
        self.tri = [self.alloc(128), self.alloc(128)]
        self.mneg = [self.alloc(128), self.alloc(128)]
        self.m01b = [self.alloc(128, BF16), self.alloc(128, BF16)]
        for d in range(2):
            self.dma(self.tri[d].ap, I["tri"][d], w=[self.tri[d]])
            self.dma(self.mneg[d].ap, I["mneg"][d], w=[self.mneg[d]])
            self.dve(lambda e, d=d: e.tensor_copy(self.m01b[d].ap, self.tri[d].ap), [self.tri[d]], [self.m01b[d]])
        self.pidx = self.alloc(1)
        self.dma(self.pidx.ap, I["pidx"], w=[self.pidx])
        self.epsc = self.alloc(1)
        self.dve(lambda e: e.memset(self.epsc.ap, EPS), [], [self.epsc])
        self.ccT = self.alloc(8 * 5)
        keep = self.mark()
        cc = self.alloc(D, parts=5)
        self.dma(cc.ap, I["cc"], w=[cc])
        self.act(lambda e: e.activation(out=cc.ap, in_=cc.ap, func=ACT.Silu), [cc], [cc])
        pb = self.pb[0]
        for k in range(8):
            self.pe(lambda e, k=k: e.transpose(self.ps[:, 0, 8 * k:8 * k + 5], cc.ap[:, 128 * k:128 * k + 128],
                                                self.ident_f.ap[0:5, 0:5]), [cc, self.ident_f], [pb])
        self.dve(lambda e: e.tensor_copy(self.ccT.ap.rearrange("p (k c) -> p k c", c=5),
                                          self.ps[:, 0, 0:64].rearrange("p (k c) -> p k c", c=8)[:, :, 0:5]),
                 [pb], [self.ccT])
        self.release(keep)

    def layer_mod(self, l):
        I = self.din
        m = self.mark()
        wts = [self.alloc(8 * 512) for _ in range(2)]
        bms = [self.alloc(512, parts=5) for _ in range(2)]
        ots = [self.alloc(512, parts=5) for _ in range(2)]
        wv = I["w_mod"][l].rearrange("(k p) n -> p k n", p=128)
        for n in range(12):
            wt, bm, ot = wts[n % 2], bms[n % 2], ots[n % 2]
            pb = self.pb[n % 2]
            self.dma(wt.ap.rearrange("p (k n) -> p k n", n=512), wv[:, :, 512 * n:512 * n + 512], w=[wt],
                     q=("sp" if n % 2 == 0 else "pool"))
            self.dma(bm.ap, I["b_mod"][l, 512 * n:512 * n + 512].partition_broadcast(5), w=[bm])
            for k in range(8):
                self.mm(self.ps[0:5, n % 2, :], self.ccT.ap[:, 5 * k:5 * k + 5], wt.ap[:, 512 * k:512 * k + 512],
                        k == 0, k == 7, [self.ccT, wt], [pb])
            self.dve(lambda e, ot=ot, bm=bm, n=n: e.tensor_tensor(ot.ap, self.ps[0:5, n % 2, :], bm.ap, ALU.add),
                     [pb, bm], [ot])
            if n in (2, 3, 8, 9):
                self.dve(lambda e, ot=ot: e.tensor_scalar(ot.ap, ot.ap, 1.0, None, ALU.add), [ot], [ot])
            self.dma(self.modv[:, 512 * n:512 * n + 512], ot.ap, r=[ot], w=[self.modvb])
        self.release(m)

    def load_rep(self, tl, row_ap, extra_r=()):
        self.dma(tl.ap, row_ap.partition_broadcast(128), r=list(extra_r), w=[tl])

    def x_src(self, l, s, i):
        if l == 0:
            if i < 2:
                return self.din["ctx"][s, 128 * i:128 * i + 128, :], self.inb["ctx"]
            return self.din["x"][s, 128 * (i - 2):128 * (i - 2) + 128, :], self.inb["x"]
        return self.xs[s, 128 * i:128 * i + 128, :], self.xsb[s]

    def mixer(self, l, s):
        I = self.din
        S = self.S
        m0 = self.mark()
        uT = self.alloc(8 * T, BF16)
        uTv = uT.ap.rearrange("p (k t) -> p k t", t=T)
        yT = self.alloc(8 * T, BF16)
        yTv = yT.ap.rearrange("p (k t) -> p k t", t=T)
        m1 = self.mark()
        reps = {}
        for nm, row, off in (("scL", s, D), ("shL", s, 0), ("scC", 4, D), ("shC", 4, 0)):
            reps[nm] = self.alloc(D)
            self.load_rep(reps[nm], self.modv[row, off:off + D], [self.modvb])
        xt = [self.alloc(D) for _ in range(2)]
        tm = [self.alloc(D) for _ in range(2)]
        ub = [self.alloc(D, BF16) for _ in range(2)]
        for i in range(NT):
            x_, t_, u_ = xt[i % 2], tm[i % 2], ub[i % 2]
            src, sb = self.x_src(l, s, i)
            self.dma(x_.ap, src, r=[sb], w=[x_], q=("sp" if i % 2 == 0 else "pool"))
            sc, sh = (reps["scC"], reps["shC"]) if i < 2 else (reps["scL"], reps["shL"])
            self.dve(lambda e, x_=x_, t_=t_, sc=sc: e.tensor_tensor(t_.ap, x_.ap, sc.ap, ALU.mult), [x_, sc], [t_])
            self.pool(lambda e, t_=t_, u_=u_, sh=sh: e.tensor_tensor(u_.ap, t_.ap, sh.ap, ALU.add), [t_, sh], [u_])
            bank = 6 + (i % 2)
            psb = self.ps[:, bank, :].bitcast(BF16)
            for k in range(8):
                self.pe(lambda e, k=k, u_=u_, psb=psb: e.transpose(psb[:, 128 * k:128 * k + 128],
                                                                  u_.ap[:, 128 * k:128 * k + 128], self.ident_b.ap),
                        [u_, self.ident_b], [self.pb[bank]])
            self.act(lambda e, i=i, psb=psb: e.copy(uTv[:, :, 128 * i:128 * i + 128],
                                                     psb.rearrange("p (k t) -> p k t", t=128)),
                     [self.pb[bank]], [uT])
        self.release(m1)
        self.debug_dump("uT", uT.ap, [uT], [128, 8 * T], BF16)
        if self.stop == "M0":
            self.release(m0)
            return
        self.mix_mlstm(l, s, uT, uTv, yT, yTv)
        if self.stop == "M1":
            self.release(m0)
            return
        self.mix_gla(l, s, uT, uTv, yT, yTv)
        if self.stop == "M2":
            self.release(m0)
            return
        self.mix_nat(l, s, uT, uTv, yT, yTv)
        self.debug_dump("yT", yT.ap, [yT], [128, 8 * T], BF16)
        if self.stop == "M3":
            self.release(m0)
            return
        self.mix_out(l, s, yT, yTv)
        self.release(m0)

    def proj_fm(self, l, s, uTv, uT, c0, ncols, dst_fn, wtiles, bias_scale=None):
        I = self.din
        wt = wtiles[0]
        bcol = wtiles[1]
        self.dma(wt.ap[:, 0:8 * ncols].rearrange("p (k c) -> p k c", c=ncols),
                 I["w_in"][l, :, c0:c0 + ncols].rearrange("(k p) c -> p k c", p=128), w=[wt], q="pool")
        self.dma(bcol.ap[0:ncols, 0:1], I["b_in"][l, c0:c0 + ncols].rearrange("(c o) -> c o", o=1), w=[bcol])
        if bias_scale is not None:
            self.dve(lambda e: e.tensor_scalar(bcol.ap[0:ncols, 0:1], bcol.ap[0:ncols, 0:1], bias_scale, None, ALU.mult),
                     [bcol], [bcol])
        for tc in range(5):
            t0 = 512 * tc
            n = min(512, T - t0)
            bank = tc % 2
            for k in range(8):
                self.mm(self.ps[0:ncols, bank, 0:n], wt.ap[:, k * ncols:(k + 1) * ncols], uTv[:, k, t0:t0 + n],
                        k == 0, k == 7, [wt, uT], [self.pb[bank]])
            dst_fn(tc, t0, n, self.ps[0:ncols, bank, 0:n], self.pb[bank], bcol)

    def mix_mlstm(self, l, s, uT, uTv, yT, yTv):
        I = self.din
        mA = self.mark()
        qkT = self.alloc(4 * T, BF16)
        qkv = qkT.ap.rearrange("p (c t) -> p c t", t=T)
        Vx = self.alloc(NT * 4 * 65, BF16)
        Vxv = Vx.ap.rearrange("p (i h d) -> p i h d", h=4, d=65)
        go = self.alloc(NT * 256, BF16)
        gov = go.ap.rearrange("p (i c) -> p i c", c=256)
        G = self.alloc(NT * 16)
        Gv = G.ap.rearrange("p (i c) -> p i c", c=16)
        mB = self.mark()
        wq = [self.alloc(8 * 128, BF16), self.alloc(1)]
        pfl = self.alloc(SEQ + 2)
        pfc = self.alloc(CTX + 2)
        t1l = self.alloc(SEQ)
        t1c = self.alloc(CTX)
        cw = self.alloc(4 * 3)
        for c in range(4):
            self.dma(cw.ap[:, 3 * c:3 * c + 3], I["conv_w"][l, :, 128 * c:128 * c + 128].rearrange("j p -> p j"), w=[cw],
                     allow_slow_non_contiguous=True)
        permM = self.alloc(128)
        self.dma(permM.ap, I["permM"], w=[permM])
        rt = [[self.alloc(512), self.alloc(512)] for _ in range(2)]
        ra = [self.alloc(512) for _ in range(2)]
        for e_ in (pfl, pfc):
            self.pool(lambda e, e_=e_: e.memset(e_.ap, 0.0), [], [e_])
        for c in range(4):
            def evac(tc, t0, n, psap, pbuf, bcol, c=c):
                if tc == 0:
                    self.act(lambda e: e.activation(out=pfc.ap[:, 1:257], in_=psap[:, 0:256], func=ACT.Identity,
                                                    bias=bcol.ap[:, 0:1], scale=1.0), [pbuf, bcol], [pfc])
                    self.act(lambda e: e.activation(out=pfl.ap[:, 1:257], in_=psap[:, 256:512], func=ACT.Identity,
                                                    bias=bcol.ap[:, 0:1], scale=1.0), [pbuf, bcol], [pfl])
                else:
                    o0 = t0 - 256 + 1
                    self.act(lambda e: e.activation(out=pfl.ap[:, o0:o0 + n], in_=psap, func=ACT.Identity,
                                                    bias=bcol.ap[:, 0:1], scale=1.0), [pbuf, bcol], [pfl])
            self.proj_fm(l, s, uTv, uT, 128 * c, 128, evac, wq)
            for (pf, t1, n) in ((pfl, t1l, SEQ), (pfc, t1c, CTX)):
                self.dve(lambda e, pf=pf, t1=t1, n=n, c=c: e.tensor_scalar(t1.ap, pf.ap[:, 1:n + 1], cw.ap[:, 3 * c + 1:3 * c + 2],
                                                                       None, ALU.mult), [pf, cw], [t1])
                self.dve(lambda e, pf=pf, t1=t1, n=n, c=c: e.scalar_tensor_tensor(t1.ap, pf.ap[:, 0:n], cw.ap[:, 3 * c:3 * c + 1],
                                                                              t1.ap, ALU.mult, ALU.add), [pf, cw, t1], [t1])
                self.dve(lambda e, pf=pf, t1=t1, n=n, c=c: e.scalar_tensor_tensor(t1.ap, pf.ap[:, 2:n + 2], cw.ap[:, 3 * c + 2:3 * c + 3],
                                                                              t1.ap, ALU.mult, ALU.add), [pf, cw, t1], [t1])
                self.act(lambda e, t1=t1: e.activation(out=t1.ap, in_=t1.ap, func=ACT.Silu), [t1], [t1])
                if c >= 2:
                    self.pool(lambda e, t1=t1: e.tensor_scalar(t1.ap, t1.ap, 0.125, None, ALU.mult), [t1], [t1])
            self.pool(lambda e, c=c: e.tensor_copy(qkv[:, c, 0:CTX], t1c.ap), [t1c], [qkT])
            for j in range(4):
                r0, r1 = rt[j % 2]
                self.dma(r0.ap, I["ropeM"][0, :, 512 * j:512 * j + 512], w=[r0])
                self.dma(r1.ap, I["ropeM"][1, :, 512 * j:512 * j + 512], w=[r1], q="pool")
                bank = 2 + (j % 2)
                self.mm(self.ps[:, bank, :], permM.ap, t1l.ap[:, 512 * j:512 * j + 512], True, True, [permM, t1l], [self.pb[bank]])
                a_ = ra[j % 2]
                self.dve(lambda e, a_=a_, r0=r0, j=j: e.tensor_tensor(a_.ap, t1l.ap[:, 512 * j:512 * j + 512], r0.ap, ALU.mult),
                         [t1l, r0], [a_])
                self.dve(lambda e, r1=r1, bank=bank: e.tensor_tensor(r1.ap, self.ps[:, bank, :], r1.ap, ALU.mult),
                         [self.pb[bank], r1], [r1])
                self.pool(lambda e, a_=a_, r1=r1, j=j, c=c: e.tensor_tensor(qkv[:, c, CTX + 512 * j:CTX + 512 * j + 512], a_.ap, r1.ap, ALU.add),
                          [a_, r1], [qkT])
        self.release(mB)
        self.debug_dump("qkT", qkT.ap, [qkT], [128, 4 * T], BF16)
        mC = self.mark()
        wvo = self.alloc(8 * 512, BF16)
        brep = self.alloc(512)
        tmpv = [self.alloc(512) for _ in range(2)]
        self.pool(lambda e: e.memset(Vxv[:, :, :, 64:65], 1.0), [], [Vx])

        def evac_vo(i, psap, pbuf, br):
            tv = tmpv[i % 2]
            self.dve(lambda e: e.tensor_tensor(tv.ap, psap, br.ap, ALU.add), [pbuf, br], [tv])
            self.pool(lambda e: e.tensor_copy(Vxv[:, i, :, 0:64], tv.ap[:, 0:256].rearrange("p (h d) -> p h d", d=64)), [tv], [Vx])
            self.act(lambda e: e.activation(out=gov[:, i, :], in_=tv.ap[:, 256:512], func=ACT.Sigmoid), [tv], [go])
        self.proj_tm(l, uT, uTv, O_MV, 512, evac_vo, wvo, brep)
        wg = self.alloc(8 * 16, BF16)
        brep2 = self.alloc(16)

        def evac_g(i, psap, pbuf, br):
            self.dve(lambda e: e.tensor_tensor(Gv[:, i, :], psap, br.ap, ALU.add), [pbuf, br], [G])
        self.proj_tm(l, uT, uTv, O_MIF, 16, evac_g, wg, brep2)
        self.release(mC)
        G4 = G.ap.rearrange("p (i a b) -> p i a b", a=4, b=4)
        LF = self.alloc(NT * 8)
        LFv = LF.ap.rearrange("p (i d h) -> p i d h", d=2, h=4)
        CS = self.alloc(NT * 8)
        CSv = CS.ap.rearrange("p (i d h) -> p i d h", d=2, h=4)
        DI = self.alloc(NT * 8)
        DIv = DI.ap.rearrange("p (i d h) -> p i d h", d=2, h=4)
        for d in range(2):
            self.act(lambda e, d=d: e.activation(out=LFv[:, :, d, :], in_=G4[:, :, 1 + 2 * d, :], func=ACT.Exp, scale=-1.0), [G], [LF])
        self.act(lambda e: e.activation(out=LF.ap, in_=LF.ap, func=ACT.Ln, bias=1.0), [LF], [LF])
        self.dve(lambda e: e.tensor_scalar(LF.ap, LF.ap, -1.0, None, ALU.mult), [LF], [LF])
        csp = self.ps[:, 0, 0:NT * 8].rearrange("p (i d h) -> p i d h", d=2, h=4)
        for i in range(NT):
            for d in range(2):
                self.mm(csp[:, i, d, :], self.tri[d].ap, LFv[:, i, d, :], True, True, [self.tri[d], LF], [self.pb[0]])
        self.dve(lambda e: e.tensor_copy(CS.ap, self.ps[:, 0, 0:NT * 8]), [self.pb[0]], [CS])
        for d in range(2):
            self.dve(lambda e, d=d: e.tensor_tensor(DIv[:, :, d, :], G4[:, :, 2 * d, :], CSv[:, :, d, :], ALU.subtract), [G, CS], [DI])
        self.debug_dump("LF", LF.ap, [LF], [128, NT * 8])
        self.debug_dump("CS", CS.ap, [CS], [128, NT * 8])
        self.debug_dump("DI", DI.ap, [DI], [128, NT * 8])
        self.debug_dump("Vx", Vx.ap, [Vx], [128, NT * 4 * 65], BF16)
        ktok = self.alloc(NT * 256, BF16)
        ktv = ktok.ap.rearrange("p (i c) -> p i c", c=256)
        for i in range(NT):
            bank = 6 + (i % 2)
            psb = self.ps[:, bank, 0:128].bitcast(BF16)
            for c in range(2):
                self.pe(lambda e, c=c, i=i, psb=psb: e.transpose(psb[:, 128 * c:128 * c + 128], qkv[:, 2 + c, 128 * i:128 * i + 128],
                                                                self.ident_b.ap), [qkT, self.ident_b], [self.pb[bank]])
            self.act(lambda e, i=i, psb=psb: e.copy(ktv[:, i, :], psb), [self.pb[bank]], [ktok])
        self.debug_dump("ktok", ktok.ap, [ktok], [128, NT * 256], BF16)
        hsum = self.alloc(NT * 256)
        hsv = hsum.ap.rearrange("p (i h d) -> p i h d", h=4, d=64)
        Cn = self.alloc(2 * 65)
        Cnv = Cn.ap.rearrange("p (a c) -> p a c", c=65)
        Cnb = self.alloc(2 * 65, BF16)
        Cnbv = Cnb.ap.rearrange("p (a c) -> p a c", c=65)
        frep = [self.alloc(128) for _ in range(4)]
        arg = [self.alloc(128) for _ in range(4)]
        E = [self.alloc(128) for _ in range(4)]
        W = [self.alloc(128, BF16) for _ in range(4)]
        Dq = [self.alloc(128) for _ in range(4)]
        qt = [self.alloc(128, BF16) for _ in range(4)]
        kt = [self.alloc(64, BF16) for _ in range(4)]
        sm = [self.alloc(4) for _ in range(4)]
        htmp = self.alloc(256)
        cnt = 0
        self.S.barrier()
        for d in range(2):
            order = list(range(NT)) if d == 0 else [1, 0] + list(range(NT - 1, 1, -1))
            endc = 127 if d == 0 else 0
            CnB = [Buf() for _ in range(4)]
            CnbB = [Buf() for _ in range(4)]
            self.dve(lambda e: e.memset(Cn.ap, 0.0), [], [Cn] + CnB)
            self.dve(lambda e: e.memset(Cnb.ap, 0.0), [], [Cnb] + CnbB)
            for it, i in enumerate(order):
                ndb = 4 + (it % 2)
                ndp = self.ps[:, ndb, 0:260].rearrange("p (h c) -> p h c", c=65)
                ts = slice(128 * i, 128 * i + 128)
                def hv(h):
                    x = h
                    cq, ck, pr, pa = h // 2, 2 + h // 2, 64 * (h % 2), h // 2
                    return dict(h=h, x=x, cq=cq, ck=ck, pa=pa, prs=slice(pr, pr + 64), fr=frep[x], ar=arg[x], Ee=E[x], Ww=W[x], Dd=Dq[x],
                                qq=qt[x], kk=kt[x], bb=h % 2, sb=2 + h % 2, ub=6 + h % 2)
                for hp in range(2):
                    H = [hv(2 * hp), hv(2 * hp + 1)]
                    for v in H:
                        self.pool(lambda e, v=v, i=i, d=d: e.tensor_copy(v["fr"].ap, LFv[:, i, d, v["h"]:v["h"] + 1].to_broadcast([128, 128])), [LF], [v["fr"]])
                    for v in H:
                        self.mm(self.ps[:, v["bb"], 0:128], v["fr"].ap, self.tri[d].ap, True, True, [v["fr"], self.tri[d]], [self.pb[v["bb"]]])
                        self.mm(self.ps[:, v["sb"], 0:128], qkv[v["prs"], v["ck"], ts], qkv[v["prs"], v["cq"], ts], True, True, [qkT], [self.pb[v["sb"]]])
                    for v in H:
                        self.dve(lambda e, v=v, i=i, d=d: e.scalar_tensor_tensor(v["ar"].ap, self.ps[:, v["bb"], 0:128], DIv[:, i, d, v["h"]:v["h"] + 1],
                                                                           self.mneg[d].ap, ALU.add, ALU.add),
                                 [self.pb[v["bb"]], DI, self.mneg[d]], [v["ar"]])
                    for v in H:
                        self.act(lambda e, v=v: e.activation(out=v["Ee"].ap, in_=v["ar"].ap, func=ACT.Exp), [v["ar"]], [v["Ee"]])
                        self.act(lambda e, v=v: e.activation(out=v["Dd"].ap[v["prs"], :], in_=self.ps[v["prs"], v["bb"], 0:128], func=ACT.Exp),
                                 [self.pb[v["bb"]]], [v["Dd"]])
                    for v in H:
                        self.pool(lambda e, v=v, ts=ts: e.tensor_tensor(v["qq"].ap[v["prs"], :], qkv[v["prs"], v["cq"], ts], v["Dd"].ap[v["prs"], :], ALU.mult),
                                  [qkT, v["Dd"]], [v["qq"]])
                        self.pool(lambda e, v=v, i=i, endc=endc: e.tensor_scalar(v["kk"].ap, ktv[:, i, 64 * v["h"]:64 * v["h"] + 64],
                                                                                v["Ee"].ap[:, endc:endc + 1], None, ALU.mult), [ktok, v["Ee"]], [v["kk"]])
                    for v in H:
                        self.dve(lambda e, v=v: e.tensor_tensor(v["Ww"].ap, self.ps[:, v["sb"], 0:128], v["Ee"].ap, ALU.mult),
                                 [self.pb[v["sb"]], v["Ee"]], [v["Ww"]])
                    for v in H:
                        h = v["h"]
                        self.mm(ndp[:, h, :], v["Ww"].ap, Vxv[:, i, h, :], True, False, [v["Ww"], Vx], [self.pb[ndb]])
                        self.mm(ndp[:, h, :], v["qq"].ap[v["prs"], :], Cnbv[v["prs"], v["pa"], :], False, True, [v["qq"], CnbB[h]], [self.pb[ndb]])
                        self.mm(self.ps[v["prs"], v["ub"], 0:65], v["kk"].ap, Vxv[:, i, h, :], True, True, [v["kk"], Vx], [self.pb[v["ub"]]])
                    for v in H:
                        self.dve(lambda e, v=v, endc=endc: e.scalar_tensor_tensor(Cnv[v["prs"], v["pa"], :], Cnv[v["prs"], v["pa"], :], v["Dd"].ap[v["prs"], endc:endc + 1],
                                                                                 self.ps[v["prs"], v["ub"], 0:65], ALU.mult, ALU.add),
                                 [CnB[v["h"]], v["Dd"], self.pb[v["ub"]]], [CnB[v["h"]]])
                    for v in H:
                        self.act(lambda e, v=v: e.copy(Cnbv[v["prs"], v["pa"], :], Cnv[v["prs"], v["pa"], :]), [CnB[v["h"]]], [CnbB[v["h"]]])
                den = ndp[:, :, 64]
                self.dve(lambda e, den=den: e.tensor_scalar(sm[0].ap, den, -1.0, None, ALU.mult), [self.pb[ndb]], [sm[0]])
                self.dve(lambda e, den=den: e.tensor_tensor(sm[1].ap, den, sm[0].ap, ALU.max), [self.pb[ndb], sm[0]], [sm[1]])
                self.dve(lambda e: e.tensor_scalar(sm[2].ap, sm[1].ap, 1.0, None, ALU.max), [sm[1]], [sm[2]])
                self.dve(lambda e: e.reciprocal(sm[3].ap, sm[2].ap), [sm[2]], [sm[3]])
                rb = sm[3].ap.unsqueeze(2).to_broadcast([128, 4, 64])
                if d == 0:
                    self.dve(lambda e, i=i, ndp=ndp, rb=rb: e.tensor_tensor(hsv[:, i, :, :], ndp[:, :, 0:64], rb, ALU.mult),
                             [self.pb[ndb], sm[3]], [hsum])
                else:
                    hv3 = htmp.ap.rearrange("p (h d) -> p h d", d=64)
                    self.dve(lambda e, ndp=ndp, rb=rb, hv3=hv3: e.tensor_tensor(hv3, ndp[:, :, 0:64], rb, ALU.mult),
                             [self.pb[ndb], sm[3]], [htmp])
                    self.pool(lambda e, i=i, hv3=hv3: e.tensor_tensor(hsv[:, i, :, :], hsv[:, i, :, :], hv3, ALU.add), [htmp, hsum], [hsum])
        self.S.barrier()
        self.debug_dump("hsum", hsum.ap, [hsum], [128, NT * 256])
        gn = self.alloc(256)
        self.load_rep(gn, I["mlstm_norm_g"][l])
        self.head_norm_out(hsum, gn, go, yT, yTv, 0)
        self.release(mA)

    def mix_gla(self, l, s, uT, uTv, yT, yTv):
        I = self.din
        mA = self.mark()
        gqk = self.alloc(4 * T, BF16, parts=64)
        gqv = gqk.ap.rearrange("p (c t) -> p c t", t=T)
        lrT = [self.alloc(T, parts=16) for _ in range(2)]
        Vg = self.alloc(NT * 256, BF16)
        Vgv = Vg.ap.rearrange("p (i c) -> p i c", c=256)
        gr = self.alloc(NT * 256, BF16)
        grv = gr.ap.rearrange("p (i c) -> p i c", c=256)
        osum = self.alloc(NT * 256)
        osv = osum.ap.rearrange("p (i h d) -> p i h d", h=4, d=64)
        mB = self.mark()
        wq = [self.alloc(8 * 64, BF16), self.alloc(1)]
        gf = self.alloc(T, parts=64)
        permG = self.alloc(64, parts=64)
        self.dma(permG.ap, I["permG"], w=[permG])
        rt = [[self.alloc(512, parts=64), self.alloc(512, parts=64)] for _ in range(2)]
        ra = [self.alloc(512, parts=64) for _ in range(2)]
        for c in range(4):
            c0 = (O_GQ if c < 2 else O_GK) + 64 * (c % 2)
            sc = (32.0 ** -0.5) if c < 2 else 1.0

            def evac(tc, t0, n, psap, pbuf, bcol, sc=sc):
                self.act(lambda e: e.activation(out=gf.ap[:, t0:t0 + n], in_=psap, func=ACT.Identity, bias=bcol.ap[0:64, 0:1], scale=sc),
                         [pbuf, bcol], [gf])
            self.proj_fm(l, s, uTv, uT, c0, 64, evac, wq, bias_scale=(sc if c < 2 else None))
            self.pool(lambda e, c=c: e.tensor_copy(gqv[:, c, 0:CTX], gf.ap[:, 0:CTX]), [gf], [gqk])
            for j in range(4):
                r0, r1 = rt[j % 2]
                self.dma(r0.ap, I["ropeG"][0, :, 512 * j:512 * j + 512], w=[r0])
                self.dma(r1.ap, I["ropeG"][1, :, 512 * j:512 * j + 512], w=[r1], q="pool")
                bank = 2 + (j % 2)
                lat = gf.ap[:, CTX + 512 * j:CTX + 512 * j + 512]
                self.mm(self.ps[0:64, bank, :], permG.ap, lat, True, True, [permG, gf], [self.pb[bank]])
                a_ = ra[j % 2]
                self.dve(lambda e, a_=a_, r0=r0, lat=lat: e.tensor_tensor(a_.ap, lat, r0.ap, ALU.mult), [gf, r0], [a_])
                self.dve(lambda e, r1=r1, bank=bank: e.tensor_tensor(r1.ap, self.ps[0:64, bank, :], r1.ap, ALU.mult),
                         [self.pb[bank], r1], [r1])
                self.pool(lambda e, a_=a_, r1=r1, j=j, c=c: e.tensor_tensor(gqv[:, c, CTX + 512 * j:CTX + 512 * j + 512], a_.ap, r1.ap, ALU.add),
                          [a_, r1], [gqk])
        wl = [self.alloc(8 * 16, BF16), self.alloc(1)]
        for d in range(2):
            def evac2(tc, t0, n, psap, pbuf, bcol, d=d):
                self.act(lambda e: e.activation(out=lrT[d].ap[:, t0:t0 + n], in_=psap, func=ACT.Identity, bias=bcol.ap[0:16, 0:1], scale=1.0),
                         [pbuf, bcol], [lrT[d]])
            self.proj_fm(l, s, uTv, uT, O_LRF + 16 * d, 16, evac2, wl)
        self.release(mB)
        self.debug_dump("gqk", gqk.ap, [gqk], [64, 4 * T], BF16)
        mC = self.mark()
        wvr = self.alloc(8 * 512, BF16)
        brep = self.alloc(512)
        tmpv = [self.alloc(512) for _ in range(2)]

        def evac_vr(i, psap, pbuf, br):
            tv = tmpv[i % 2]
            self.dve(lambda e: e.tensor_tensor(tv.ap, psap, br.ap, ALU.add), [pbuf, br], [tv])
            self.pool(lambda e: e.tensor_copy(Vgv[:, i, :], tv.ap[:, 0:256]), [tv], [Vg])
            self.act(lambda e: e.activation(out=grv[:, i, :], in_=tv.ap[:, 256:512], func=ACT.Silu), [tv], [gr])
        self.proj_tm(l, uT, uTv, O_GV, 512, evac_vr, wvr, brep)
        self.release(mC)
        mD = self.mark()
        w2 = self.alloc(2 * 128, parts=16)
        self.dma(w2.ap.rearrange("p (d c) -> p d c", c=128), I["gla_w2"][l].rearrange("d r c -> r d c"), w=[w2])
        nb2 = self.alloc(4, parts=64)
        for d in range(2):
            for pa in range(2):
                self.dma(nb2.ap[:, 2 * d + pa:2 * d + pa + 1], I["gla_b2"][l, d, 64 * pa:64 * pa + 64].rearrange("(c o) -> c o", o=1), w=[nb2])
        self.dve(lambda e: e.tensor_scalar(nb2.ap, nb2.ap, -1.0, None, ALU.mult), [nb2], [nb2])
        smask = self.alloc(T, parts=64)
        self.dma(smask.ap, I["scanmask"], w=[smask])
        aT = self.alloc(T, parts=64)
        cs = self.alloc(T, parts=64)
        eb = self.alloc(T, parts=64)
        qtl = self.alloc(T, BF16, parts=64)
        ktl = self.alloc(T, BF16, parts=64)
        ktk = self.alloc(NT * 64, BF16)
        ktkv = ktk.ap.rearrange("p (i c) -> p i c", c=64)
        ebe = self.alloc(NT, parts=64)
        Sst = self.alloc(64, parts=64)
        Sb = self.alloc(64, BF16, parts=64)
        Am = [self.alloc(128, BF16) for _ in range(2)]
        cnt = 0
        for d in range(2):
            order = list(range(NT)) if d == 0 else [1, 0] + list(range(NT - 1, 1, -1))
            endc = 127 if d == 0 else 0
            for pa in range(2):
                for tc in range(5):
                    t0 = 512 * tc
                    n = min(512, T - t0)
                    bank = tc % 2
                    self.mm(self.ps[0:64, bank, 0:n], w2.ap[:, 128 * d + 64 * pa:128 * d + 64 * pa + 64], lrT[d].ap[:, t0:t0 + n], True, True,
                            [w2, lrT[d]], [self.pb[bank]])
                    self.act(lambda e, t0=t0, n=n, bank=bank, d=d, pa=pa: e.activation(out=aT.ap[:, t0:t0 + n], in_=self.ps[0:64, bank, 0:n], func=ACT.Exp,
                                                                                      bias=nb2.ap[:, 2 * d + pa:2 * d + pa + 1], scale=-1.0),
                             [self.pb[bank], nb2], [aT])
                self.act(lambda e: e.activation(out=aT.ap, in_=aT.ap, func=ACT.Ln, bias=1.0), [aT], [aT])
                self.dve(lambda e: e.tensor_scalar(aT.ap, aT.ap, -1.0 / 16.0, None, ALU.mult), [aT], [aT])
                self.dve(lambda e: e.tensor_tensor_scan(cs.ap, smask.ap, aT.ap, 0.0, ALU.mult, ALU.add), [smask, aT], [cs])
                if d == 1:
                    a3 = aT.ap.rearrange("p (i t) -> p i t", t=128)
                    c3 = cs.ap.rearrange("p (i t) -> p i t", t=128)
                    e3 = eb.ap.rearrange("p (i t) -> p i t", t=128)
                    self.dve(lambda e, a3=a3, c3=c3: e.tensor_tensor(a3, a3, c3, ALU.subtract), [aT, cs], [aT])
                    self.dve(lambda e, a3=a3, c3=c3, e3=e3: e.tensor_tensor(e3, a3, c3[:, :, 127:128].to_broadcast([64, NT, 128]), ALU.add), [aT, cs], [eb])
                    self.dve(lambda e: e.tensor_copy(cs.ap, eb.ap), [eb], [cs])
                self.act(lambda e: e.activation(out=eb.ap, in_=cs.ap, func=ACT.Exp), [cs], [eb])
                self.pool(lambda e, pa=pa: e.tensor_tensor(qtl.ap, gqv[:, pa, :], eb.ap, ALU.mult), [gqk, eb], [qtl])
                self.dve(lambda e, endc=endc: e.tensor_copy(ebe.ap, eb.ap.rearrange("p (i t) -> p i t", t=128)[:, :, endc]), [eb], [ebe])
                self.act(lambda e: e.activation(out=aT.ap, in_=cs.ap, func=ACT.Exp, scale=-1.0), [cs], [aT])
                self.pool(lambda e, pa=pa: e.tensor_tensor(ktl.ap, gqv[:, 2 + pa, :], aT.ap, ALU.mult), [gqk, aT], [ktl])
                for i in range(NT):
                    bank = 6 + (i % 2)
                    psb = self.ps[:, bank, 0:32].bitcast(BF16)
                    self.pe(lambda e, i=i, psb=psb: e.transpose(psb, ktl.ap[:, 128 * i:128 * i + 128], self.ident_b.ap[0:64, 0:64]),
                            [ktl, self.ident_b], [self.pb[bank]])
                    self.act(lambda e, i=i, psb=psb: e.copy(ktkv[:, i, :], psb), [self.pb[bank]], [ktk])
                self.dve(lambda e: e.memset(Sst.ap, 0.0), [], [Sst])
                self.dve(lambda e: e.memset(Sb.ap, 0.0), [], [Sb])
                for it, i in enumerate(order):
                    ob = 4 + (it % 2)
                    ub = 2 + (it % 2)
                    op_ = self.ps[:, ob, 0:128].rearrange("p (h c) -> p h c", c=64)
                    ts = slice(128 * i, 128 * i + 128)
                    for hh in range(2):
                        h = 2 * pa + hh
                        prs = slice(32 * hh, 32 * hh + 32)
                        x = cnt % 2
                        cnt += 1
                        self.mm(self.ps[:, x, 0:128], ktl.ap[prs, ts], qtl.ap[prs, ts], True, True, [ktl, qtl], [self.pb[x]])
                        self.dve(lambda e, x=x, d=d: e.tensor_tensor(Am[x].ap, self.ps[:, x, 0:128], self.m01b[d].ap, ALU.mult),
                                 [self.pb[x], self.m01b[d]], [Am[x]])
                        self.mm(op_[:, hh, :], Am[x].ap, Vgv[:, i, 64 * h:64 * h + 64], True, False, [Am[x], Vg], [self.pb[ob]])
                        self.mm(op_[:, hh, :], qtl.ap[prs, ts], Sb.ap[prs, :], False, True, [qtl, Sb], [self.pb[ob]])
                        self.mm(self.ps[prs, ub, 0:64], ktkv[:, i, 32 * hh:32 * hh + 32], Vgv[:, i, 64 * h:64 * h + 64], True, True, [ktk, Vg], [self.pb[ub]])
                    self.dve(lambda e, i=i: e.tensor_scalar(Sst.ap, Sst.ap, ebe.ap[:, i:i + 1], None, ALU.mult), [Sst, ebe], [Sst])
                    self.dve(lambda e, i=i, ub=ub: e.scalar_tensor_tensor(Sst.ap, self.ps[0:64, ub, 0:64], ebe.ap[:, i:i + 1], Sst.ap, ALU.mult, ALU.add),
                             [self.pb[ub], ebe, Sst], [Sst])
                    self.act(lambda e: e.copy(Sb.ap, Sst.ap), [Sst], [Sb])
                    if d == 0:
                        self.act(lambda e, i=i, pa=pa, op_=op_: e.copy(osv[:, i, 2 * pa:2 * pa + 2, :], op_), [self.pb[ob]], [osum])
                    else:
                        self.dve(lambda e, i=i, pa=pa, op_=op_: e.tensor_tensor(osv[:, i, 2 * pa:2 * pa + 2, :], op_, osv[:, i, 2 * pa:2 * pa + 2, :], ALU.add),
                                 [self.pb[ob], osum], [osum])
        self.release(mD)
        self.debug_dump("osum", osum.ap, [osum], [128, NT * 256])
        gn = self.alloc(256)
        self.load_rep(gn, I["gla_norm_g"][l])
        self.head_norm_out(osum, gn, gr, yT, yTv, 2)
        self.release(mA)

    def mix_nat(self, l, s, uT, uTv, yT, yTv):
        I = self.din
        for pa in range(4):
            mA = self.mark()
            nq = self.alloc(T, BF16)
            nk = self.alloc(T, BF16)
            Vn = self.alloc(NT * 2 * 65, BF16)
            Vnv = Vn.ap.rearrange("p (i h d) -> p i h d", h=2, d=65)
            yn = self.alloc(NT * 128, BF16)
            ynv = yn.ap.rearrange("p (i c) -> p i c", c=128)
            wq = [self.alloc(8 * 128, BF16), self.alloc(1)]
            wv = self.alloc(8 * 128, BF16)
            brep = self.alloc(128)
            for which, dst, sc in ((0, nq, 0.125), (1, nk, 1.0)):
                c0 = (O_NQ if which == 0 else O_NK) + 128 * pa

                def evac(tc, t0, n, psap, pbuf, bcol, dst=dst, sc=sc):
                    self.act(lambda e: e.activation(out=dst.ap[:, t0:t0 + n], in_=psap, func=ACT.Identity, bias=bcol.ap[:, 0:1], scale=sc),
                             [pbuf, bcol], [dst])
                self.proj_fm(l, s, uTv, uT, c0, 128, evac, wq, bias_scale=(sc if which == 0 else None))
            self.pool(lambda e: e.memset(Vnv[:, :, :, 64:65], 1.0), [], [Vn])

            def evac_v(i, psap, pbuf, br):
                self.dve(lambda e: e.tensor_tensor(Vnv[:, i, :, 0:64], psap.rearrange("p (h d) -> p h d", d=64),
                                                    br.ap.rearrange("p (h d) -> p h d", d=64), ALU.add), [pbuf, br], [Vn])
            self.proj_tm(l, uT, uTv, O_NV + 128 * pa, 128, evac_v, wv, brep)
            tb = [self.alloc(8 * 512) for _ in range(2)]
            arg = [self.alloc(512) for _ in range(2)]
            Ee = [self.alloc(512, BF16) for _ in range(20)]
            rd = [self.alloc(4) for _ in range(2)]
            cnt = 0
            tcnt = 0
            ocnt = 0
            for hh in range(2):
                h = 2 * pa + hh
                prs = slice(64 * hh, 64 * hh + 64)
                blocks = [(0, [0]), (1, [1, 2]), (2, [3]), (3, [-1])]
                for p, qbs in blocks:
                    if p < 3:
                        tbt = tb[tcnt % 2]
                        tcnt += 1
                        self.dma(tbt.ap, I["nbias"][l, h, p].rearrange("p j q -> p (j q)"), w=[tbt], q=("sp" if tcnt % 2 else "pool"))
                        tbv = tbt.ap.rearrange("p (j q) -> p j q", q=512)
                    for qb in qbs:
                        if qb >= 0:
                            q0, nqk, nqs = CTX + 512 * qb, 512, 4
                            kt0 = (0, 4 * qb - 2, 8)[p]
                            keys = [2 + kt0 + j for j in range(8)] + [0, 1]
                            nloc = 8
                            ot0 = 2 + 4 * qb
                        else:
                            q0, nqk, nqs = 0, 256, 2
                            keys = [0, 1]
                            nloc = 0
                            ot0 = 0
                        ob = 4 + (ocnt % 2)
                        ocnt += 1
                        opv = self.ps[:, ob, 0:65 * nqs].rearrange("p (a c) -> p a c", c=65)
                        ebase = 10 * (ocnt % 2)
                        for j, ti in enumerate(keys):
                            sbk = cnt % 4
                            e_ = Ee[ebase + j]
                            a_ = arg[cnt % 2]
                            cnt += 1
                            self.mm(self.ps[:, sbk, 0:nqk], nk.ap[prs, 128 * ti:128 * ti + 128], nq.ap[prs, q0:q0 + nqk], True, True,
                                    [nk, nq], [self.pb[sbk]])
                            if j < nloc:
                                self.dve(lambda e, a_=a_, sbk=sbk, tbv=tbv, j=j: e.tensor_tensor(a_.ap, self.ps[:, sbk, :], tbv[:, j, :], ALU.add),
                                         [self.pb[sbk], tbt], [a_])
                                self.act(lambda e, e_=e_, a_=a_: e.activation(out=e_.ap, in_=a_.ap, func=ACT.Exp), [a_], [e_])
                            else:
                                self.act(lambda e, e_=e_, sbk=sbk, nqk=nqk: e.activation(out=e_.ap[:, 0:nqk], in_=self.ps[:, sbk, 0:nqk], func=ACT.Exp),
                                         [self.pb[sbk]], [e_])
                        for qs in range(nqs):
                            for j, ti in enumerate(keys):
                                e_ = Ee[ebase + j]
                                self.mm(opv[:, qs, :], e_.ap[:, 128 * qs:128 * qs + 128], Vnv[:, ti, hh, :], j == 0, j == len(keys) - 1,
                                        [e_, Vn], [self.pb[ob]])
                        r_ = rd[ocnt % 2]
                        self.dve(lambda e, r_=r_, opv=opv, nqs=nqs: e.reciprocal(r_.ap[:, 0:nqs], opv[:, :, 64]), [self.pb[ob]], [r_])
                        self.dve(lambda e, r_=r_, opv=opv, nqs=nqs, ot0=ot0, hh=hh: e.tensor_tensor(
                            ynv[:, ot0:ot0 + nqs, 64 * hh:64 * hh + 64], opv[:, :, 0:64],
                            r_.ap[:, 0:nqs].unsqueeze(2).to_broadcast([128, nqs, 64]), ALU.mult), [self.pb[ob], r_], [yn])
            for i in range(NT):
                bank = 6 + (i % 2)
                psb = self.ps[:, bank, 0:64].bitcast(BF16)
                self.pe(lambda e, i=i, psb=psb: e.transpose(psb, ynv[:, i, :], self.ident_b.ap), [yn, self.ident_b], [self.pb[bank]])
                self.act(lambda e, i=i, psb=psb, pa=pa: e.copy(yTv[:, 4 + pa, 128 * i:128 * i + 128], psb), [self.pb[bank]], [yT])
            self.release(mA)

    def mix_out(self, l, s, yT, yTv):
        I = self.din
        mA = self.mark()
        wout = self.alloc(8 * D, BF16)
        self.dma(wout.ap.rearrange("p (k n) -> p k n", n=D), I["w_out"][l].rearrange("(k p) n -> p k n", p=128), w=[wout], q="pool")
        wr = self.alloc(8 * NE)
        self.dma(wr.ap.rearrange("p (k e) -> p k e", e=NE), I["w_router"][l].rearrange("(k p) e -> p k e", p=128), w=[wr])
        rp = {}
        for nm, off in (("g1", 2 * D), ("sh2", 3 * D), ("sc2", 4 * D)):
            for tag, row in (("L", s), ("C", 4)):
                rp[nm + tag] = self.alloc(D)
                self.load_rep(rp[nm + tag], self.modv[row, off:off + D], [self.modvb])
        lng = self.alloc(D)
        lnb = self.alloc(D)
        self.load_rep(lng, I["ln1_g"][l])
        self.load_rep(lnb, I["ln1_b"][l])
        xt = [self.alloc(D) for _ in range(2)]
        t1 = [self.alloc(D) for _ in range(2)]
        x1 = [self.alloc(D) for _ in range(2)]
        u2 = [self.alloc(D) for _ in range(2)]
        u2h = [self.alloc(D, BF16) for _ in range(2)]
        u2T = [self.alloc(8 * 128) for _ in range(2)]
        afT = self.alloc(T, parts=16)
        sm = [[self.alloc(12), self.alloc(2), self.alloc(1), self.alloc(1), self.alloc(1), self.alloc(1), self.alloc(1), self.alloc(NE), self.alloc(NE)]
              for _ in range(2)]
        for i in range(NT):
            x = i % 2
            tg = "C" if i < 2 else "L"
            stt, mv, std, rstd, mx, ssum, rs, ex, aff = sm[x]
            b0 = 2 * x
            for half in range(2):
                for k in range(8):
                    self.mm(self.ps[:, b0 + half, :], yTv[:, k, 128 * i:128 * i + 128], wout.ap[:, k * D + 512 * half:k * D + 512 * half + 512],
                            k == 0, k == 7, [yT, wout], [self.pb[b0 + half]])
            src, sb = self.x_src(l, s, i)
            self.dma(xt[x].ap, src, r=[sb], w=[xt[x]])
            opv = self.ps[:, b0:b0 + 2, :].rearrange("p a n -> p (a n)")
            self.dve(lambda e, x=x, opv=opv, tg=tg: e.tensor_tensor(t1[x].ap, opv, rp["g1" + tg].ap, ALU.mult),
                     [self.pb[b0], self.pb[b0 + 1], rp["g1" + tg]], [t1[x]])
            self.dve(lambda e, x=x: e.scalar_tensor_tensor(t1[x].ap, xt[x].ap, ALPHA, t1[x].ap, ALU.mult, ALU.add), [xt[x], t1[x]], [t1[x]])
            self.layer_norm(t1[x], x1[x], lng, lnb, stt, mv, std, rstd)
            self.dma(self.xs[s, 128 * i:128 * i + 128, :], x1[x].ap, r=[x1[x]], w=[self.xsb[s]])
            self.dve(lambda e, x=x, tg=tg: e.tensor_tensor(u2[x].ap, x1[x].ap, rp["sc2" + tg].ap, ALU.mult), [x1[x], rp["sc2" + tg]], [u2[x]])
            self.pool(lambda e, x=x, tg=tg: e.tensor_tensor(u2[x].ap, u2[x].ap, rp["sh2" + tg].ap, ALU.add), [u2[x], rp["sh2" + tg]], [u2[x]])
            self.act(lambda e, x=x: e.copy(u2h[x].ap, u2[x].ap), [u2[x]], [u2h[x]])
            self.dma(self.u2b[s, 128 * i:128 * i + 128, :], u2h[x].ap, r=[u2h[x]], w=[self.u2bb[s]], q="pool")
            tb0 = 4
            for k in range(8):
                self.pe(lambda e, k=k, x=x: e.transpose(self.ps[:, tb0 + k // 4, 128 * (k % 4):128 * (k % 4) + 128], u2[x].ap[:, 128 * k:128 * k + 128],
                                                        self.ident_f.ap), [u2[x], self.ident_f], [self.pb[tb0 + k // 4]])
            self.act(lambda e, x=x: e.copy(u2T[x].ap, self.ps[:, tb0:tb0 + 2, :].rearrange("p a n -> p (a n)")),
                     [self.pb[tb0], self.pb[tb0 + 1]], [u2T[x]])
            for k in range(8):
                self.mm(self.ps[:, 6, 0:NE], u2T[x].ap[:, 128 * k:128 * k + 128], wr.ap[:, NE * k:NE * k + NE], k == 0, k == 7,
                        [u2T[x], wr], [self.pb[6]])
            self.dve(lambda e, mx=mx: e.reduce_max(mx.ap, self.ps[:, 6, 0:NE], axis=AX.X), [self.pb[6]], [mx])
            self.dve(lambda e, mx=mx: e.tensor_scalar(mx.ap, mx.ap, -1.0, None, ALU.mult), [mx], [mx])
            self.act(lambda e, ex=ex, mx=mx, ssum=ssum: e.activation(out=ex.ap, in_=self.ps[:, 6, 0:NE], func=ACT.Exp, bias=mx.ap, scale=1.0,
                                                                   accum_out=ssum.ap), [self.pb[6], mx], [ex, ssum])
            self.dve(lambda e, rs=rs, ssum=ssum: e.reciprocal(rs.ap, ssum.ap), [ssum], [rs])
            self.dve(lambda e, aff=aff, ex=ex, rs=rs: e.tensor_scalar(aff.ap, ex.ap, rs.ap, None, ALU.mult), [ex, rs], [aff])
            self.pe(lambda e, aff=aff: e.transpose(self.ps[0:NE, 7, 0:128], aff.ap, self.ident_f.ap), [aff, self.ident_f], [self.pb[7]])
            self.act(lambda e, i=i: e.copy(afT.ap[:, 128 * i:128 * i + 128], self.ps[0:NE, 7, 0:128]), [self.pb[7]], [afT])
        self.dma(self.affd[NE * s:NE * s + NE, :], afT.ap, r=[afT], w=[self.affdb])
        self.release(mA)

    def moe(self, l, last):
        I = self.din
        NS = self.NS
        R = NS * NE
        N = NS * CAPT
        mL = self.mark()
        idxf = self.alloc(CAPT, parts=R)
        gatef = self.alloc(CAPT, parts=R)
        idxT = self.alloc(3 * R)
        idxTv = idxT.ap.rearrange("p (c r) -> p c r", r=R)
        gateT = self.alloc(3 * R)
        gateTv = gateT.ap.rearrange("p (c r) -> p c r", r=R)
        tcol = self.alloc(NT)
        for i in range(NT):
            self.dve(lambda e, i=i: e.tensor_scalar(tcol.ap[:, i:i + 1], self.pidx.ap, float(128 * i), None, ALU.add), [self.pidx], [tcol])
        m = self.mark()
        wk = self.alloc(T, parts=R)
        self.dma(wk.ap, self.affd[0:R, :], r=[self.affdb], w=[wk])
        idxu = self.alloc(CAPT, U32, parts=R)
        mx8 = self.alloc(8, parts=R)
        for part, (lo, hi, o0, nit) in enumerate(((CTX, T, 0, CAP // 8), (0, CTX, CAP, CAPC // 8))):
            for it in range(nit):
                o = o0 + 8 * it
                self.dve(lambda e, lo=lo, hi=hi, o=o: e.max(out=gatef.ap[:, o:o + 8], in_=wk.ap[:, lo:hi]), [wk], [gatef])
                self.dve(lambda e, lo=lo, hi=hi, o=o: e.max_index(out=idxu.ap[:, o:o + 8], in_max=gatef.ap[:, o:o + 8], in_values=wk.ap[:, lo:hi]),
                         [wk, gatef], [idxu])
                self.dve(lambda e, lo=lo, hi=hi, o=o: e.match_replace(out=wk.ap[:, lo:hi], in_to_replace=gatef.ap[:, o:o + 8], in_values=wk.ap[:, lo:hi],
                                                                      imm_value=-1.0), [wk, gatef], [wk])
        self.dve(lambda e: e.tensor_copy(idxf.ap, idxu.ap), [idxu], [idxf])
        self.dve(lambda e: e.tensor_scalar(idxf.ap[:, 0:CAP], idxf.ap[:, 0:CAP], float(CTX), None, ALU.add), [idxf], [idxf])
        for src, dstv, dst in ((idxf, idxTv, idxT), (gatef, gateTv, gateT)):
            for c, (c0, n) in enumerate(((0, 128), (128, 128), (256, 32))):
                self.pe(lambda e, src=src, c=c, c0=c0, n=n: e.transpose(self.ps[0:n, 7, 64 * c:64 * c + R], src.ap[:, c0:c0 + n], self.ident_f.ap[0:R, 0:R]),
                        [src, self.ident_f], [self.pb[7]])
            self.dve(lambda e, dst=dst: e.memset(dst.ap, 0.0), [], [dst])
            for c, n in enumerate((128, 128, 32)):
                self.dve(lambda e, dstv=dstv, c=c, n=n: e.tensor_copy(dstv[0:n, c, :], self.ps[0:n, 7, 64 * c:64 * c + R]), [self.pb[7]], [dst])
        self.release(m)
        self.debug_dump("idxf", idxf.ap, [idxf], [R, CAPT])
        self.debug_dump("gatef", gatef.ap, [gatef], [R, CAPT])
        m = self.mark()
        U = self.alloc(NT * D, BF16)
        Uv = U.ap.rearrange("p (i d) -> p i d", d=D)
        sel = [self.alloc(128, parts=R) for _ in range(2)]
        PT = [self.alloc(16 * CAP, BF16) for _ in range(2)]
        PTc = [self.alloc(2 * CAPC, BF16) for _ in range(2)]
        xeS = [self.alloc(8 * CAPT, BF16) for _ in range(2)]
        for s in range(NS):
            for i in range(NT):
                self.dma(Uv[:, i, :], self.u2b[s, 128 * i:128 * i + 128, :], r=[self.u2bb[s]], w=[U], q=("sp" if i % 2 else "pool"))
            for e_ in range(NE):
                r = NE * s + e_
                x = e_ % 2
                self.dve(lambda e, x=x, r=r: e.tensor_scalar(sel[x].ap, self.pidx.ap[0:R, 0:1].to_broadcast([R, 128]), float(r), None, ALU.is_equal),
                         [self.pidx], [sel[x]])
                self.mm(self.ps[:, 6, 0:CAPT], sel[x].ap, idxf.ap, True, True, [sel[x], idxf], [self.pb[6]])
                ptv = PT[x].ap.rearrange("p (i c) -> p i c", c=CAP)
                pcv = PTc[x].ap.rearrange("p (i c) -> p i c", c=CAPC)
                for i in range(2, NT):
                    self.dve(lambda e, i=i, ptv=ptv: e.tensor_scalar(ptv[:, i - 2, :], self.ps[:, 6, 0:CAP], tcol.ap[:, i:i + 1], None, ALU.is_equal),
                             [self.pb[6], tcol], [PT[x]])
                for i in range(2):
                    self.dve(lambda e, i=i, pcv=pcv: e.tensor_scalar(pcv[:, i, :], self.ps[:, 6, CAP:CAPT], tcol.ap[:, i:i + 1], None, ALU.is_equal),
                             [self.pb[6], tcol], [PTc[x]])
                xv = xeS[x].ap.rearrange("p (k c) -> p k c", c=CAPT)
                for dk in range(8):
                    bank = dk % 4
                    for i in range(2, NT):
                        self.mm(self.ps[:, bank, 0:CAP], Uv[:, i, 128 * dk:128 * dk + 128], ptv[:, i - 2, :], i == 2, i == NT - 1, [U, PT[x]], [self.pb[bank]])
                    for i in range(2):
                        self.mm(self.ps[:, bank, CAP:CAPT], Uv[:, i, 128 * dk:128 * dk + 128], pcv[:, i, :], i == 0, i == 1, [U, PTc[x]], [self.pb[bank]])
                    self.act(lambda e, xv=xv, dk=dk, bank=bank: e.copy(xv[:, dk, :], self.ps[:, bank, 0:CAPT]), [self.pb[bank]], [xeS[x]])
                self.dma(self.xeT[e_][:, :, CAPT * s:CAPT * s + CAPT].rearrange("k p c -> p k c"), xv, r=[xeS[x]], w=[self.xeTb[e_]])
        self.release(m)
        m = self.mark()
        nchunk = (N + 511) // 512
        cw = N // nchunk
        xin = [self.alloc(8 * N, BF16) for _ in range(2)]
        wgu = [[self.alloc(8 * 512, BF16) for _ in range(2)] for _ in range(2)]
        wd = [self.alloc(16 * D, BF16) for _ in range(2)]
        hT = self.alloc(16 * N, BF16)
        hTv = hT.ap.rearrange("p (j n) -> p j n", n=N)
        sg = [self.alloc(cw) for _ in range(2)]
        yo = [self.alloc(D, BF16) for _ in range(2)]
        cnt = 0
        ycnt = 0
        for e_ in range(NE):
            xi = xin[e_ % 2]
            xiv = xi.ap.rearrange("p (k n) -> p k n", n=N)
            self.dma(xiv, self.xeT[e_].rearrange("k p n -> p k n"), r=[self.xeTb[e_]], w=[xi])
            wdt = wd[e_ % 2]
            wdv = wdt.ap.rearrange("p (j d) -> p j d", d=D)
            for jb in range(4):
                self.dma(wdv[:, 4 * jb:4 * jb + 4, :], I["w_down"][l, e_, 512 * jb:512 * jb + 512, :].rearrange("(j p) d -> p j d", p=128), w=[wdt], q="pool")
            for fb in range(4):
                wts = []
                for mi, nm in enumerate(("w_gate", "w_up")):
                    wt = wgu[mi][fb % 2]
                    self.dma(wt.ap.rearrange("p (k f) -> p k f", f=512), I[nm][l, e_, :, 512 * fb:512 * fb + 512].rearrange("(k p) f -> p k f", p=128),
                             w=[wt], q="pool")
                    wts.append(wt)
                for fc in range(4):
                    j = 4 * fb + fc
                    for ng in range(nchunk):
                        n0 = cw * ng
                        gb, ubk = (cnt % 2), 2 + (cnt % 2)
                        s_ = sg[cnt % 2]
                        cnt += 1
                        for k in range(8):
                            self.mm(self.ps[:, gb, 0:cw], wts[0].ap[:, 512 * k + 128 * fc:512 * k + 128 * fc + 128], xiv[:, k, n0:n0 + cw], k == 0, k == 7,
                                    [wts[0], xi], [self.pb[gb]])
                        for k in range(8):
                            self.mm(self.ps[:, ubk, 0:cw], wts[1].ap[:, 512 * k + 128 * fc:512 * k + 128 * fc + 128], xiv[:, k, n0:n0 + cw], k == 0, k == 7,
                                    [wts[1], xi], [self.pb[ubk]])
                        self.act(lambda e, s_=s_, gb=gb: e.activation(out=s_.ap, in_=self.ps[:, gb, 0:cw], func=ACT.Silu), [self.pb[gb]], [s_])
                        self.dve(lambda e, s_=s_, ubk=ubk, j=j, n0=n0: e.tensor_tensor(hTv[:, j, n0:n0 + cw], self.ps[:, ubk, 0:cw], s_.ap, ALU.mult),
                                 [self.pb[ubk], s_], [hT])
            for s in range(NS):
                r = NE * s + e_
                for c, (c0, n) in enumerate(((0, 128), (128, 128), (256, 32))):
                    y_ = yo[ycnt % 2]
                    for half in range(2):
                        bank = 4 + (ycnt % 2) * 2 + half
                        for j in range(16):
                            self.mm(self.ps[0:n, bank, :], hTv[:, j, CAPT * s + c0:CAPT * s + c0 + n], wdv[:, j, 512 * half:512 * half + 512], j == 0, j == 15,
                                    [hT, wdt], [self.pb[bank]])
                        self.act(lambda e, y_=y_, n=n, bank=bank, half=half, c=c, r=r: e.activation(out=y_.ap[0:n, 512 * half:512 * half + 512], in_=self.ps[0:n, bank, :],
                                                                                               func=ACT.Copy, scale=gateTv[0:n, c, r:r + 1]),
                                 [self.pb[bank], gateT], [y_])
                    ycnt += 1
                    self.dma(self.ye[s, e_, c0:c0 + n, :], y_.ap[0:n, :], r=[y_], w=[self.yeb[s][e_]])
        self.release(m)
        m = self.mark()
        YL = self.alloc(NE * 2 * D, BF16)
        YLv = YL.ap.rearrange("p (e a d) -> p e a d", a=2, d=D)
        YC = self.alloc(NE * D, BF16, parts=32)
        YCv = YC.ap.rearrange("p (e d) -> p e d", d=D)
        iot = self.alloc(T)
        self.dma(iot.ap, I["iota_t"], w=[iot])
        Pm = [self.alloc(32 * 128, BF16) for _ in range(2)]
        Pc = [self.alloc(16 * 128, BF16, parts=32) for _ in range(2)]
        lng = self.alloc(D)
        lnb = self.alloc(D)
        self.load_rep(lng, I["ln2_g"][l])
        self.load_rep(lnb, I["ln2_b"][l])
        g2 = {"L": self.alloc(D), "C": self.alloc(D)}
        xt = [self.alloc(D) for _ in range(2)]
        t1 = [self.alloc(D) for _ in range(2)]
        x2 = [self.alloc(D) for _ in range(2)]
        sm = [[self.alloc(12), self.alloc(2), self.alloc(1), self.alloc(1)] for _ in range(2)]
        for s in range(NS):
            for e_ in range(NE):
                self.dma(YLv[:, e_, :, :], self.ye[s, e_, 0:CAP, :].rearrange("(a p) d -> p a d", p=128), r=[self.yeb[s][e_]], w=[YL],
                         q=("sp" if e_ % 2 else "pool"))
                self.dma(YCv[:, e_, :], self.ye[s, e_, CAP:CAPT, :], r=[self.yeb[s][e_]], w=[YC])
            self.load_rep(g2["L"], self.modv[s, 5 * D:6 * D], [self.modvb])
            self.load_rep(g2["C"], self.modv[4, 5 * D:6 * D], [self.modvb])
            idxL = idxTv[:, 0:2, NE * s:NE * s + NE].rearrange("p a e -> p e a")
            idxC = idxTv[0:32, 2, NE * s:NE * s + NE]
            for i in (range(2, NT) if last else range(NT)):
                x = i % 2
                tg = "C" if i < 2 else "L"
                stt, mv, std, rstd = sm[x]
                b0 = 2 * x
                if i >= 2:
                    pv = Pm[x].ap.rearrange("p (e a t) -> p e a t", a=2, t=128)
                    self.dve(lambda e, pv=pv, i=i, idxL=idxL: e.tensor_tensor(
                        pv, iot.ap[:, 128 * i:128 * i + 128].unsqueeze(1).unsqueeze(1).to_broadcast([128, NE, 2, 128]),
                        idxL.unsqueeze(3).to_broadcast([128, NE, 2, 128]), ALU.is_equal), [iot, idxT], [Pm[x]])
                    for half in range(2):
                        for e_ in range(NE):
                            for a in range(2):
                                self.mm(self.ps[:, b0 + half, :], pv[:, e_, a, :], YLv[:, e_, a, 512 * half:512 * half + 512],
                                        (e_ == 0 and a == 0), (e_ == NE - 1 and a == 1), [Pm[x], YL], [self.pb[b0 + half]])
                else:
                    pv = Pc[x].ap.rearrange("p (e t) -> p e t", t=128)
                    self.dve(lambda e, pv=pv, i=i, idxC=idxC: e.tensor_tensor(
                        pv, iot.ap[0:32, 128 * i:128 * i + 128].unsqueeze(1).to_broadcast([32, NE, 128]),
                        idxC.unsqueeze(2).to_broadcast([32, NE, 128]), ALU.is_equal), [iot, idxT], [Pc[x]])
                    for half in range(2):
                        for e_ in range(NE):
                            self.mm(self.ps[:, b0 + half, :], pv[:, e_, :], YCv[:, e_, 512 * half:512 * half + 512],
                                    e_ == 0, e_ == NE - 1, [Pc[x], YC], [self.pb[b0 + half]])
                self.dma(xt[x].ap, self.xs[s, 128 * i:128 * i + 128, :], r=[self.xsb[s]], w=[xt[x]])
                opv = self.ps[:, b0:b0 + 2, :].rearrange("p a n -> p (a n)")
                if "fmoe" in self.dbgn and s == 0:
                    self.dve(lambda e, x=x, opv=opv: e.tensor_copy(t1[x].ap, opv), [self.pb[b0], self.pb[b0 + 1]], [t1[x]])
                    self.dma(self.dbg_f[128 * i:128 * i + 128, :], t1[x].ap, r=[t1[x]])
                self.dve(lambda e, x=x, opv=opv, tg=tg: e.tensor_tensor(t1[x].ap, opv, g2[tg].ap, ALU.mult),
                         [self.pb[b0], self.pb[b0 + 1], g2[tg]], [t1[x]])
                self.dve(lambda e, x=x: e.scalar_tensor_tensor(t1[x].ap, xt[x].ap, ALPHA, t1[x].ap, ALU.mult, ALU.add), [xt[x], t1[x]], [t1[x]])
                self.layer_norm(t1[x], x2[x], lng, lnb, stt, mv, std, rstd)
                if last:
                    self.dma(self.out[s, 128 * (i - 2):128 * (i - 2) + 128, :], x2[x].ap, r=[x2[x]])
                else:
                    self.dma(self.xs[s, 128 * i:128 * i + 128, :], x2[x].ap, r=[x2[x]], w=[self.xsb[s]])
        self.release(m)
        self.release(mL)

    def layer_norm(self, src, dst, lng, lnb, stt, mv, std, rstd):
        st3 = stt.ap.rearrange("p (c f) -> p c f", f=6)
        for c in range(2):
            self.dve(lambda e, c=c: e.bn_stats(st3[:, c, :], src.ap[:, 512 * c:512 * c + 512]), [src], [stt])
        self.dve(lambda e: e.bn_aggr(mv.ap, st3), [stt], [mv])
        self.act(lambda e: e.activation(out=std.ap, in_=mv.ap[:, 1:2], func=ACT.Sqrt, bias=self.epsc.ap, scale=1.0), [mv, self.epsc], [std])
        self.dve(lambda e: e.reciprocal(rstd.ap, std.ap), [std], [rstd])
        self.dve(lambda e: e.tensor_scalar(src.ap, src.ap, mv.ap[:, 0:1], rstd.ap, ALU.subtract, ALU.mult), [src, mv, rstd], [src])
        self.pool(lambda e: e.tensor_tensor(dst.ap, src.ap, lng.ap, ALU.mult), [src, lng], [dst])
        self.pool(lambda e: e.tensor_tensor(dst.ap, dst.ap, lnb.ap, ALU.add), [dst, lnb], [dst])

    def head_norm_out(self, hsum, gn, gate, yT, yTv, ych):
        hsv = hsum.ap.rearrange("p (i h d) -> p i h d", h=4, d=64)
        gtv = gate.ap.rearrange("p (i c) -> p i c", c=256)
        m = self.mark()
        sq = [self.alloc(256) for _ in range(2)]
        st = [[self.alloc(4) for _ in range(6)] for _ in range(2)]
        tt = [self.alloc(256) for _ in range(2)]
        ym = [self.alloc(256, BF16) for _ in range(2)]
        for i in range(NT):
            x = i % 2
            s1, s2, mean, var, std, rstd = st[x]
            hv = hsv[:, i, :, :]
            sq3 = sq[x].ap.rearrange("p (h d) -> p h d", d=64)
            t3 = tt[x].ap.rearrange("p (h d) -> p h d", d=64)
            self.dve(lambda e, hv=hv, s1=s1: e.reduce_sum(s1.ap, hv, axis=AX.X), [hsum], [s1])
            self.pool(lambda e, hv=hv, sq3=sq3: e.tensor_tensor(sq3, hv, hv, ALU.mult), [hsum], [sq[x]])
            self.dve(lambda e, sq3=sq3, s2=s2: e.reduce_sum(s2.ap, sq3, axis=AX.X), [sq[x]], [s2])
            self.dve(lambda e, s1=s1, mean=mean: e.tensor_scalar(mean.ap, s1.ap, 1.0 / 64, None, ALU.mult), [s1], [mean])
            self.dve(lambda e, mean=mean, var=var: e.tensor_tensor(var.ap, mean.ap, mean.ap, ALU.mult), [mean], [var])
            self.dve(lambda e, s2=s2, var=var: e.scalar_tensor_tensor(var.ap, s2.ap, 1.0 / 64, var.ap, ALU.mult, ALU.subtract), [s2, var], [var])
            self.act(lambda e, var=var, std=std: e.activation(out=std.ap, in_=var.ap, func=ACT.Sqrt, bias=self.epsc.ap, scale=1.0),
                     [var, self.epsc], [std])
            self.dve(lambda e, std=std, rstd=rstd: e.reciprocal(rstd.ap, std.ap), [std], [rstd])
            mb_ = mean.ap.unsqueeze(2).to_broadcast([128, 4, 64])
            rb_ = rstd.ap.unsqueeze(2).to_broadcast([128, 4, 64])
            self.dve(lambda e, hv=hv, t3=t3, mb_=mb_: e.tensor_tensor(t3, hv, mb_, ALU.subtract), [hsum, mean], [tt[x]])
            self.dve(lambda e, t3=t3, rb_=rb_: e.tensor_tensor(t3, t3, rb_, ALU.mult), [tt[x], rstd], [tt[x]])
            self.pool(lambda e, x=x: e.tensor_tensor(tt[x].ap, tt[x].ap, gn.ap, ALU.mult), [tt[x], gn], [tt[x]])
            self.pool(lambda e, x=x, i=i: e.tensor_tensor(ym[x].ap, tt[x].ap, gtv[:, i, :], ALU.mult), [tt[x], gate], [ym[x]])
            bank = 6 + x
            psb = self.ps[:, bank, 0:128].bitcast(BF16)
            for c in range(2):
                self.pe(lambda e, c=c, x=x, psb=psb: e.transpose(psb[:, 128 * c:128 * c + 128], ym[x].ap[:, 128 * c:128 * c + 128], self.ident_b.ap),
                        [ym[x], self.ident_b], [self.pb[bank]])
            self.act(lambda e, i=i, psb=psb: e.copy(yTv[:, ych:ych + 2, 128 * i:128 * i + 128], psb.rearrange("p (c t) -> p c t", t=128)),
                     [self.pb[bank]], [yT])
        self.release(m)

    def proj_tm(self, l, uT, uTv, c0, ncols, evac_fn, wt, brep):
        I = self.din
        self.dma(wt.ap[:, 0:8 * ncols].rearrange("p (k c) -> p k c", c=ncols),
                 I["w_in"][l, :, c0:c0 + ncols].rearrange("(k p) c -> p k c", p=128), w=[wt], q="pool")
        self.load_rep(brep, I["b_in"][l, c0:c0 + ncols])
        for i in range(NT):
            bank = i % 2
            for k in range(8):
                self.mm(self.ps[:, bank, 0:ncols], uTv[:, k, 128 * i:128 * i + 128], wt.ap[:, k * ncols:(k + 1) * ncols],
                        k == 0, k == 7, [uT, wt], [self.pb[bank]])
            evac_fn(i, self.ps[:, bank, 0:ncols], self.pb[bank], brep)


N_CORES = 8
NS_CORE = 4
DEPTH = 4
_W_NAMES = ("w_mod", "b_mod", "w_in", "b_in", "conv_w", "gla_w2", "gla_b2", "mlstm_norm_g", "gla_norm_g",
            "w_out", "ln1_g", "ln1_b", "w_router", "w_gate", "w_up", "w_down", "ln2_g", "ln2_b")


def build_program(NS=NS_CORE, NL=DEPTH):
    B = Builder(NS, NL)
    B.consts()
    for l in range(NL):
        B.layer_mod(l)
        for s in range(NS):
            B.mixer(l, s)
        B.moe(l, l == NL - 1)
    B.S.final_wait("sp")
    B.S.emit()
    return B


def kernel(**inputs):
    f32 = lambda a: np.ascontiguousarray(np.asarray(a), dtype=np.float32)
    x = f32(inputs["x"])
    c = f32(inputs["c"])
    ctx = f32(inputs["ctx"])
    c_ctx = f32(inputs["c_ctx"])
    shared = {k: f32(inputs[k]) for k in _W_NAMES}
    shared["nbias"] = natten_tables(f32(inputs["rpb"]))
    shared.update(host_consts())
    B = build_program()
    in_maps = []
    for ci in range(N_CORES):
        sl = slice(NS_CORE * ci, NS_CORE * ci + NS_CORE)
        m = dict(shared)
        m["x"] = x[sl]
        m["ctx"] = ctx[sl]
        m["cc"] = np.concatenate([c[sl], c_ctx[None, :]], 0)
        in_maps.append(m)
    res = run_bass_kernel_spmd(B.nc, in_maps, core_ids=list(range(N_CORES)))
    return np.concatenate([np.asarray(r["out"]) for r in res.results], axis=0).astype(np.float32)
```
